# Optimizing a Trainium2 kernel written in Bass

```python
import jax, jax.numpy as jnp
from jax import lax
import numpy as np

D_MODEL = 1024
BATCH = 8
SEQ = 2048
DEPTH = 2
DEC_BATCH = 128
DEC_SEQ = 8
PAST_LEN = 16384
PAGE_SIZE = 128

D_SSM = D_MODEL
SSM_HEAD_DIM = 64
SSM_HEADS = D_SSM // SSM_HEAD_DIM
SSM_GROUPS = 2
SSM_STATE = 128
SSM_CONV = 4
SSM_CHUNK = 128
D_XBC = D_SSM + 2 * SSM_GROUPS * SSM_STATE
D_CONV = D_MODEL // 2
CONV_WIDTH = 31
D_POOL = D_MODEL // 2
POOL_WINDOWS = (2, 4, 8, 16)
POOL_GROUPS = 4
POOL_GROUP_DIM = D_POOL // POOL_GROUPS
POOL_BUF = 15
N_BRANCH = 3
IN_SIZES = (D_SSM, D_XBC, SSM_HEADS, 2 * D_CONV, D_CONV, D_POOL, D_POOL, N_BRANCH * D_MODEL)
D_IN = D_SSM + D_XBC + SSM_HEADS + 3 * D_CONV + 2 * D_POOL + N_BRANCH * D_MODEL
EPS = 1e-6

kernel_name = "hybrid_ssd_conformer_pool_step"


def rmsnorm(x, w):
    xf = x.astype(jnp.float32)
    y = xf * lax.rsqrt(jnp.mean(xf * xf, axis=-1, keepdims=True) + EPS)
    return (y * w.astype(jnp.float32)).astype(x.dtype)


def layernorm(x, w, b):
    xf = x.astype(jnp.float32)
    mu = jnp.mean(xf, axis=-1, keepdims=True)
    var = jnp.mean(jnp.square(xf - mu), axis=-1, keepdims=True)
    y = (xf - mu) * lax.rsqrt(var + EPS)
    return (y * w.astype(jnp.float32) + b.astype(jnp.float32)).astype(x.dtype)


def causal_dwconv(u, buf, w, b):
    k = w.shape[0]
    ext = jnp.concatenate([buf.astype(u.dtype), u], axis=1)
    out = lax.conv_general_dilated(
        ext, w[:, None, :].astype(u.dtype), window_strides=(1,), padding='VALID',
        dimension_numbers=('NWC', 'WIO', 'NWC'), feature_group_count=u.shape[-1])
    return out + b.astype(u.dtype), ext[:, -(k - 1):]


def ssd_scan(x, dt, a_log, b_mat, c_mat, h0):
    f32 = jnp.float32
    bsz, L, H, P = x.shape
    G, N = b_mat.shape[2], b_mat.shape[3]
    hg = H // G
    q = min(SSM_CHUNK, L)
    pad = (-L) % q
    xf, dtf, bf, cf = x.astype(f32), dt.astype(f32), b_mat.astype(f32), c_mat.astype(f32)
    if pad:
        padl = lambda t: jnp.pad(t, [(0, 0), (0, pad)] + [(0, 0)] * (t.ndim - 2))
        xf, dtf, bf, cf = padl(xf), padl(dtf), padl(bf), padl(cf)
    lp = L + pad
    nc = lp // q
    A = -jnp.exp(a_log.astype(f32))
    xdt = (xf * dtf[..., None]).reshape(bsz, nc, q, G, hg, P)
    a_cum = jnp.cumsum((dtf * A).reshape(bsz, nc, q, G, hg), axis=2)
    Bc = bf.reshape(bsz, nc, q, G, N)
    Cc = cf.reshape(bsz, nc, q, G, N)
    seg = a_cum[:, :, :, None] - a_cum[:, :, None, :]
    mask = jnp.tril(jnp.ones((q, q), dtype=bool))[None, None, :, :, None, None]
    decay = jnp.exp(jnp.where(mask, seg, -jnp.inf))
    cb = jnp.einsum('bclgn,bcsgn->bclsg', Cc, Bc)
    y_diag = jnp.einsum('bclsg,bclsgh,bcsghp->bclghp', cb, decay, xdt)
    decay_to_end = jnp.exp(a_cum[:, :, -1:] - a_cum)
    states = jnp.einsum('bcsgn,bcsgh,bcsghp->bcghpn', Bc, decay_to_end, xdt)
    chunk_decay = jnp.exp(a_cum[:, :, -1])

    def step(h, inp):
        s, d = inp
        return d[..., None, None] * h + s, h

    h_init = h0.astype(f32).reshape(bsz, G, hg, P, N)
    h_fin, h_enter = lax.scan(step, h_init, (jnp.moveaxis(states, 1, 0), jnp.moveaxis(chunk_decay, 1, 0)))
    h_enter = jnp.moveaxis(h_enter, 0, 1)
    y_off = jnp.einsum('bclgn,bclgh,bcghpn->bclghp', Cc, jnp.exp(a_cum), h_enter)
    y = (y_diag + y_off).reshape(bsz, lp, H, P)[:, :L]
    return y.astype(x.dtype), h_fin.reshape(bsz, H, P, N).astype(h0.dtype)


def mamba_branch(z, xbc, dt_raw, conv_buf, h0, conv_w, conv_b, dt_bias, a_log, d_skip, norm_w):
    bsz, L, _ = xbc.shape
    xbc_c, new_buf = causal_dwconv(xbc, conv_buf, conv_w, conv_b)
    xbc_c = jax.nn.silu(xbc_c)
    xs, bm, cm = jnp.split(xbc_c, [D_SSM, D_SSM + SSM_GROUPS * SSM_STATE], axis=-1)
    xs = xs.reshape(bsz, L, SSM_HEADS, SSM_HEAD_DIM)
    bm = bm.reshape(bsz, L, SSM_GROUPS, SSM_STATE)
    cm = cm.reshape(bsz, L, SSM_GROUPS, SSM_STATE)
    dt = jax.nn.softplus(dt_raw.astype(jnp.float32) + dt_bias.astype(jnp.float32))
    y, h_new = ssd_scan(xs, dt, a_log, bm, cm, h0)
    y = (y + d_skip[:, None].astype(y.dtype) * xs).reshape(bsz, L, D_SSM)
    yg = (y * jax.nn.silu(z)).reshape(bsz, L, SSM_GROUPS, D_SSM // SSM_GROUPS)
    yg = rmsnorm(yg, jnp.ones((D_SSM // SSM_GROUPS,), jnp.float32)).reshape(bsz, L, D_SSM)
    return yg * norm_w.astype(yg.dtype), new_buf, h_new


def conformer_branch(glu_in, gate, buf, conv_w, conv_b, ln_w, ln_b):
    val, g = jnp.split(glu_in, 2, axis=-1)
    v = val * jax.nn.sigmoid(g)
    c, new_buf = causal_dwconv(v, buf, conv_w, conv_b)
    h = jax.nn.silu(layernorm(c, ln_w, ln_b))
    return h * jax.nn.silu(gate), new_buf


def pool_branch(u, gate, buf, start_pos, mix_w, scale):
    bsz, L, _ = u.shape
    ext = jnp.concatenate([buf.astype(u.dtype), u], axis=1)
    cs = jnp.pad(jnp.cumsum(ext.astype(jnp.float32), axis=1), ((0, 0), (1, 0), (0, 0)))
    pos = start_pos + jnp.arange(L, dtype=jnp.int32)
    means = []
    for g, w in enumerate(POOL_WINDOWS):
        sl = slice(g * POOL_GROUP_DIM, (g + 1) * POOL_GROUP_DIM)
        hi = cs[:, POOL_BUF + 1:POOL_BUF + 1 + L, sl]
        lo = cs[:, POOL_BUF + 1 - w:POOL_BUF + 1 - w + L, sl]
        cnt = jnp.minimum(pos + 1, w).astype(jnp.float32)
        means.append((hi - lo) / cnt[None, :, None])
    pooled = jnp.concatenate(means, axis=-1) - u.astype(jnp.float32)
    pooled = pooled.reshape(bsz, L, POOL_GROUPS, POOL_GROUP_DIM)
    mixed = jnp.einsum('blgc,gcd->blgd', pooled, mix_w.astype(jnp.float32)).reshape(bsz, L, D_POOL)
    h = (mixed * scale.astype(jnp.float32)).astype(u.dtype)
    return h * jax.nn.silu(gate), ext[:, -POOL_BUF:]


def trunk(x, states, start_pos, params):
    (norm_w, w_in, ssm_conv_w, ssm_conv_b, ssm_dt_bias, ssm_a_log, ssm_d, ssm_norm_w,
     cf_conv_w, cf_conv_b, cf_ln_w, cf_ln_b, pool_mix_w, pool_scale,
     w_proj_a, w_proj_b, w_proj_c, w_out, final_norm_w) = params
    st_ssm, st_sconv, st_cf, st_pool = states
    split_idx = np.cumsum(IN_SIZES)[:-1].tolist()
    out_ssm, out_sconv, out_cf, out_pool = [], [], [], []
    for l in range(DEPTH):
        h = rmsnorm(x, norm_w[l])
        proj = jnp.einsum('bld,de->ble', h, w_in[l])
        z, xbc, dt_raw, glu_in, gate_b, u_pool, gate_c, merge = jnp.split(proj, split_idx, axis=-1)
        ya, nb_a, h_a = mamba_branch(z, xbc, dt_raw, st_sconv[l], st_ssm[l], ssm_conv_w[l], ssm_conv_b[l],
                                     ssm_dt_bias[l], ssm_a_log[l], ssm_d[l], ssm_norm_w[l])
        yb, nb_b = conformer_branch(glu_in, gate_b, st_cf[l], cf_conv_w[l], cf_conv_b[l], cf_ln_w[l], cf_ln_b[l])
        yc, nb_c = pool_branch(u_pool, gate_c, st_pool[l], start_pos, pool_mix_w[l], pool_scale[l])
        oa = jnp.einsum('ble,ed->bld', ya, w_proj_a[l])
        ob = jnp.einsum('ble,ed->bld', yb, w_proj_b[l])
        oc = jnp.einsum('ble,ed->bld', yc, w_proj_c[l])
        ga, gb, gc = jnp.split(jax.nn.sigmoid(merge), N_BRANCH, axis=-1)
        merged = ga * oa + gb * ob + gc * oc
        x = x + jnp.einsum('bld,de->ble', merged, w_out[l])
        out_ssm.append(h_a)
        out_sconv.append(nb_a)
        out_cf.append(nb_b)
        out_pool.append(nb_c)
    y = rmsnorm(x, final_norm_w)
    return y, jnp.stack(out_ssm), jnp.stack(out_sconv), jnp.stack(out_cf), jnp.stack(out_pool)


def setup_inputs(seed: int = 0) -> dict:
    key = jax.random.key(seed)
    ks = jax.random.split(key, 32)
    nrm = lambda k, s, sc: sc * jax.random.normal(k, s, jnp.float32)
    dt0 = jnp.exp(jax.random.uniform(ks[10], (DEPTH, SSM_HEADS)) * (jnp.log(0.1) - jnp.log(0.001)) + jnp.log(0.001))
    return {
        "x_prompt": nrm(ks[0], (BATCH, SEQ, D_MODEL), 1.0),
        "x_sample": nrm(ks[1], (DEC_BATCH, DEC_SEQ, D_MODEL), 1.0),
        "state_ssm": nrm(ks[2], (DEPTH, DEC_BATCH, SSM_HEADS, SSM_HEAD_DIM, SSM_STATE), 0.1),
        "state_ssm_conv": nrm(ks[3], (DEPTH, DEC_BATCH, SSM_CONV - 1, D_XBC), 1.0),
        "state_cf_conv": nrm(ks[4], (DEPTH, DEC_BATCH, CONV_WIDTH - 1, D_CONV), 1.0),
        "state_pool": nrm(ks[5], (DEPTH, DEC_BATCH, POOL_BUF, D_POOL), 1.0),
        "norm_w": 1.0 + nrm(ks[6], (DEPTH, D_MODEL), 0.02),
        "w_in": nrm(ks[7], (DEPTH, D_MODEL, D_IN), D_MODEL ** -0.5),
        "ssm_conv_w": nrm(ks[8], (DEPTH, SSM_CONV, D_XBC), SSM_CONV ** -0.5),
        "ssm_conv_b": nrm(ks[9], (DEPTH, D_XBC), 0.01),
        "ssm_dt_bias": dt0 + jnp.log(-jnp.expm1(-dt0)),
        "ssm_a_log": jnp.log(jax.random.uniform(ks[11], (DEPTH, SSM_HEADS), jnp.float32, 1.0, 16.0)),
        "ssm_d": 1.0 + nrm(ks[12], (DEPTH, SSM_HEADS), 0.1),
        "ssm_norm_w": 1.0 + nrm(ks[13], (DEPTH, D_SSM), 0.02),
        "cf_conv_w": nrm(ks[14], (DEPTH, CONV_WIDTH, D_CONV), CONV_WIDTH ** -0.5),
        "cf_conv_b": nrm(ks[15], (DEPTH, D_CONV), 0.01),
        "cf_ln_w": 1.0 + nrm(ks[16], (DEPTH, D_CONV), 0.02),
        "cf_ln_b": nrm(ks[17], (DEPTH, D_CONV), 0.01),
        "pool_mix_w": nrm(ks[18], (DEPTH, POOL_GROUPS, POOL_GROUP_DIM, POOL_GROUP_DIM), POOL_GROUP_DIM ** -0.5),
        "pool_scale": 1.0 + nrm(ks[19], (DEPTH, D_POOL), 0.1),
        "w_proj_a": nrm(ks[20], (DEPTH, D_SSM, D_MODEL), D_SSM ** -0.5),
        "w_proj_b": nrm(ks[21], (DEPTH, D_CONV, D_MODEL), D_CONV ** -0.5),
        "w_proj_c": nrm(ks[22], (DEPTH, D_POOL, D_MODEL), D_POOL ** -0.5),
        "w_out": nrm(ks[23], (DEPTH, D_MODEL, D_MODEL), D_MODEL ** -0.5),
        "final_norm_w": 1.0 + nrm(ks[24], (D_MODEL,), 0.02),
    }


def reference(x_prompt, x_sample, state_ssm, state_ssm_conv, state_cf_conv, state_pool,
              norm_w, w_in, ssm_conv_w, ssm_conv_b, ssm_dt_bias, ssm_a_log, ssm_d, ssm_norm_w,
              cf_conv_w, cf_conv_b, cf_ln_w, cf_ln_b, pool_mix_w, pool_scale,
              w_proj_a, w_proj_b, w_proj_c, w_out, final_norm_w):
    params = (norm_w, w_in, ssm_conv_w, ssm_conv_b, ssm_dt_bias, ssm_a_log, ssm_d, ssm_norm_w,
              cf_conv_w, cf_conv_b, cf_ln_w, cf_ln_b, pool_mix_w, pool_scale,
              w_proj_a, w_proj_b, w_proj_c, w_out, final_norm_w)
    bp = x_prompt.shape[0]
    dtp = x_prompt.dtype
    zero_states = (
        jnp.zeros((DEPTH, bp, SSM_HEADS, SSM_HEAD_DIM, SSM_STATE), dtp),
        jnp.zeros((DEPTH, bp, SSM_CONV - 1, D_XBC), dtp),
        jnp.zeros((DEPTH, bp, CONV_WIDTH - 1, D_CONV), dtp),
        jnp.zeros((DEPTH, bp, POOL_BUF, D_POOL), dtp),
    )
    y_prompt, ssm_p, sconv_p, cf_p, pool_p = trunk(x_prompt, zero_states, 0, params)
    y_sample, ssm_s, sconv_s, cf_s, pool_s = trunk(
        x_sample, (state_ssm, state_ssm_conv, state_cf_conv, state_pool), PAST_LEN, params)
    return (y_prompt, y_sample, ssm_p, ssm_s, sconv_p, sconv_s, cf_p, cf_s, pool_p, pool_s)
```

```python
import contextlib
import numpy as np
import concourse.bass as bass
import concourse.mybir as mybir
from concourse.bass_utils import run_bass_kernel_spmd

F32 = mybir.dt.float32
BF16 = mybir.dt.bfloat16
AF = mybir.ActivationFunctionType
ALU = mybir.AluOpType

NCORES = 8
DEPTH = 2
DM = 1024
SEQ = 2048
NSEQ_S = 16
LS = 8
D_IN = 8208
NEG = -2000.0


class Res:
    __slots__ = ("name", "last_w", "readers")

    def __init__(self, name):
        self.name = name
        self.last_w = None
        self.readers = []


class Op:
    __slots__ = ("eng", "fn", "deps", "dma", "stream", "count", "needed")

    def __init__(self, eng, fn, dma=False, stream=None):
        self.eng = eng
        self.fn = fn
        self.deps = []
        self.dma = dma
        self.stream = stream
        self.count = 0
        self.needed = False


class Prog:
    ENGS = ("pe", "act", "dve", "pool", "sp")

    def __init__(self, nc):
        self.nc = nc
        self.ops = {e: [] for e in self.ENGS}
        self.all_ops = []
        self.res = {}

    def R(self, name):
        r = self.res.get(name)
        if r is None:
            r = self.res[name] = Res(name)
        return r

    def _track(self, op, reads, writes):
        deps = set()
        writes = list(writes) + [r for r in reads if r.startswith("ps") and r not in writes]
        reads = [r for r in reads if not r.startswith("ps")]
        reads = [self.R(r) for r in reads]
        writes = [self.R(w) for w in writes]
        for r in reads:
            if r.last_w is not None:
                deps.add(r.last_w)
        for w in writes:
            if w.last_w is not None:
                deps.add(w.last_w)
            deps.update(w.readers)
        for r in reads:
            r.readers.append(op)
        for w in writes:
            w.last_w = op
            w.readers = []
        deps.discard(op)
        op.deps = list(deps)

    def op(self, eng, fn, reads=(), writes=()):
        o = Op(eng, fn)
        self.all_ops.append(o)
        self.ops[eng].append(o)
        self._track(o, reads, writes)
        return o

    def dma(self, eng, stream, fn, reads=(), writes=()):
        o = Op(eng, fn, dma=True, stream=stream)
        self.all_ops.append(o)
        self.ops[eng].append(o)
        self._track(o, reads, writes)
        return o

    def finalize(self, es, final_wait_eng="sp"):
        nc = self.nc
        for o in self.all_ops:
            for d in o.deps:
                if d.eng == "pe" and o.eng == "pe" and not d.dma and not o.dma:
                    continue
                d.needed = True
        cnt = {e: 0 for e in self.ENGS}
        scnt = {}
        for o in self.all_ops:
            if o.dma:
                scnt[o.stream] = scnt.get(o.stream, 0) + 16
                o.count = scnt[o.stream]
            elif o.needed:
                cnt[o.eng] += 1
                o.count = cnt[o.eng]
        esem = {e: es.enter_context(nc.semaphore("s_" + e)) for e in ("pe", "act", "dve", "pool")}
        ssem = {s: es.enter_context(nc.semaphore("d_%d" % i)) for i, s in enumerate(scnt)}
        block = es.enter_context(nc.Block())
        engobj = {"pe": "tensor", "act": "scalar", "dve": "vector", "pool": "gpsimd", "sp": "sync"}

        def make(ename):
            ops = self.ops[ename]

            def body(eng):
                waited = {}
                for o in ops:
                    for d in o.deps:
                        if d.dma:
                            key, sem, val = ("s", d.stream), ssem[d.stream], d.count
                        else:
                            if d.eng == "pe" and ename == "pe" and not o.dma:
                                continue
                            key, sem, val = ("e", d.eng), esem[d.eng], d.count
                        if waited.get(key, 0) >= val:
                            continue
                        waited[key] = val
                        eng.wait_ge(sem, val)
                    ins = o.fn(eng)
                    if o.dma:
                        ins.then_inc(ssem[o.stream], 16)
                    elif o.needed:
                        ins.then_inc(esem[o.eng], 1)
                if ename == final_wait_eng:
                    for s, v in scnt.items():
                        eng.wait_ge(ssem[s], v)

            return body

        for ename in self.ENGS:
            getattr(block, engobj[ename])(make(ename))


def f_mm(out, lhsT, rhs, start=True, stop=True):
    return lambda e: e.matmul(out, lhsT, rhs, start=start, stop=stop)


def f_tr(out, in_, ident):
    return lambda e: e.transpose(out, in_, ident)


def f_act(out, in_, func, bias=None, scale=None, accum=None):
    kw = {}
    if bias is not None:
        kw["bias"] = bias
    if scale is not None:
        kw["scale"] = scale
    if accum is not None:
        kw["accum_out"] = accum
    return lambda e: e.activation(out=out, in_=in_, func=func, **kw)


def f_tt(out, in0, in1, op):
    return lambda e: e.tensor_tensor(out=out, in0=in0, in1=in1, op=op)


def f_ts(out, in0, s1, op0, s2=None, op1=None):
    if op1 is None:
        return lambda e: e.tensor_scalar(out=out, in0=in0, scalar1=s1, scalar2=None, op0=op0)
    return lambda e: e.tensor_scalar(out=out, in0=in0, scalar1=s1, scalar2=s2, op0=op0, op1=op1)


def f_stt(out, in0, scalar, in1, op0, op1):
    return lambda e: e.scalar_tensor_tensor(out=out, in0=in0, scalar=scalar, in1=in1, op0=op0, op1=op1)


def f_cp(out, in_):
    return lambda e: e.tensor_copy(out=out, in_=in_)


def f_ms(out, val):
    return lambda e: e.memset(out, val)


def f_dma(out, in_):
    return lambda e: e.dma_start(out=out, in_=in_)


PC_SCW, PC_SCB, PC_D, PC_SNW, PC_CFW, PC_CFB, PC_LNW, PC_LNB, PC_PSC, PC_NW, PC_DTB, PC_ALOG, PC_N = (
    0, 48, 60, 68, 76, 200, 204, 208, 212, 216, 224, 225, 226)
CF_ID, CF_TRIP, CF_TRIS, CF_SAMES, CF_ONES, CF_SEL, CF_RC, CF_EPS, CF_ONE, CF_L2, CF_PAIR, CF_N = (
    0, 128, 256, 384, 512, 640, 656, 720, 721, 722, 850, 858)
CB_ID, CB_ONES, CB_MNP, CB_MNS, CB_OH, CB_N = 0, 128, 256, 768, 1280, 1280 + 2048
NGRP = 8 + 8 + 2
GW = 5120

_Z0, _X0, _DT0, _GV0, _GG0, _GB0, _UP0, _GC0, _MA0, _MB0, _MC0 = (
    0, 1024, 2560, 2576, 3088, 3600, 4112, 4624, 5136, 6160, 7184)
BCH = ([("z", j, _Z0 + 128 * j) for j in range(8)] + [("x", j, _X0 + 128 * j) for j in range(12)]
       + sum([[("gv", i, _GV0 + 128 * i), ("gg", i, _GG0 + 128 * i)] for i in range(4)], [])
       + [("gb", i, _GB0 + 128 * i) for i in range(4)] + [("up", i, _UP0 + 128 * i) for i in range(4)]
       + [("gc", i, _GC0 + 128 * i) for i in range(4)])


def _kxm(w, c0, n):
    k = w.shape[0] // 128
    return w[:, c0:c0 + n].reshape(k, 128, n).transpose(1, 0, 2)


def host_layout(inp, core):
    f = np.float32
    m = {}
    m["xp"] = np.ascontiguousarray(inp["x_prompt"][core])
    sl = slice(core * NSEQ_S, (core + 1) * NSEQ_S)
    m["xs"] = np.ascontiguousarray(inp["x_sample"][sl].reshape(NSEQ_S * LS, DM))
    m["st_ssm"] = np.ascontiguousarray(inp["state_ssm"][:, sl].reshape(DEPTH, NSEQ_S, 1024, 128))
    m["st_sc"] = np.ascontiguousarray(inp["state_ssm_conv"][:, sl].reshape(DEPTH, NSEQ_S * 3, 1536))
    m["st_cf"] = np.ascontiguousarray(inp["state_cf_conv"][:, sl].reshape(DEPTH, NSEQ_S * 30, 512))
    m["st_pl"] = np.ascontiguousarray(inp["state_pool"][:, sl].reshape(DEPTH, NSEQ_S * 15, 512))
    return m


def host_shared(inp):
    f = np.float32
    wg = np.zeros((DEPTH, NGRP, 128, GW), f)
    wdt = np.zeros((DEPTH, 128, 128), f)
    wmix = np.zeros((DEPTH, 128, 512), f)
    pcol = np.zeros((DEPTH, 128, PC_N), f)
    for l in range(DEPTH):
        w_in = inp["w_in"][l]
        for g in range(8):
            for k in range(5):
                _, _, c0 = BCH[g * 5 + k]
                blk = _kxm(w_in, c0, 128)
                wg[l, g].reshape(128, 5, 8, 128)[:, k] = blk
        pa, pb, pc = inp["w_proj_a"][l], inp["w_proj_b"][l], inp["w_proj_c"][l]
        for j in range(8):
            t = wg[l, 8 + j]
            t[:, 0:1024] = _kxm(pa, 128 * j, 128).reshape(128, 1024)
            t[:, 1024:1536] = _kxm(pb, 128 * j, 128).reshape(128, 512)
            t[:, 1536:2048] = _kxm(pc, 128 * j, 128).reshape(128, 512)
            t[:, 2048:3072] = _kxm(w_in, _MA0 + 128 * j, 128).reshape(128, 1024)
            t[:, 3072:4096] = _kxm(w_in, _MB0 + 128 * j, 128).reshape(128, 1024)
            t[:, 4096:5120] = _kxm(w_in, _MC0 + 128 * j, 128).reshape(128, 1024)
        for hf in range(2):
            wg[l, 16 + hf][:, 0:4096] = _kxm(inp["w_out"][l], 512 * hf, 512).reshape(128, 4096)
        wdt[l] = _kxm(w_in, _DT0, 16).reshape(128, 128)
        wmix[l] = inp["pool_mix_w"][l].transpose(1, 0, 2).reshape(128, 512)
        p = pcol[l]
        p[:, PC_SCW:PC_SCW + 48] = inp["ssm_conv_w"][l].T.reshape(12, 128, 4).transpose(1, 0, 2).reshape(128, 48)
        p[:, PC_SCB:PC_SCB + 12] = inp["ssm_conv_b"][l].reshape(12, 128).T
        p[:, PC_D:PC_D + 8] = np.repeat(inp["ssm_d"][l], 64).reshape(8, 128).T
        p[:, PC_SNW:PC_SNW + 8] = inp["ssm_norm_w"][l].reshape(8, 128).T
        p[:, PC_CFW:PC_CFW + 124] = inp["cf_conv_w"][l].T.reshape(4, 128, 31).transpose(1, 0, 2).reshape(128, 124)
        p[:, PC_CFB:PC_CFB + 4] = inp["cf_conv_b"][l].reshape(4, 128).T
        p[:, PC_LNW:PC_LNW + 4] = inp["cf_ln_w"][l].reshape(4, 128).T
        p[:, PC_LNB:PC_LNB + 4] = inp["cf_ln_b"][l].reshape(4, 128).T
        p[:, PC_PSC:PC_PSC + 4] = inp["pool_scale"][l].reshape(4, 128).T
        p[:, PC_NW:PC_NW + 8] = inp["norm_w"][l].reshape(8, 128).T
    prow = np.zeros((128, 64 + 1024), f)
    for l in range(DEPTH):
        prow[:, 32 * l:32 * l + 16] = inp["ssm_dt_bias"][l][None, :]
        prow[:, 32 * l + 16:32 * l + 32] = inp["ssm_a_log"][l][None, :]
    prow[:, 64:] = inp["final_norm_w"][None, :]
    t = np.arange(128)
    seq = t // LS
    cf = np.zeros((128, CF_N), f)
    cf[:, CF_ID:CF_ID + 128] = np.eye(128)
    cf[:, CF_TRIP:CF_TRIP + 128] = (t[:, None] <= t[None, :])
    cf[:, CF_TRIS:CF_TRIS + 128] = (t[:, None] <= t[None, :]) & (seq[:, None] == seq[None, :])
    cf[:, CF_SAMES:CF_SAMES + 128] = (seq[:, None] == seq[None, :])
    cf[:, CF_ONES:CF_ONES + 128] = 1.0
    cf[:, CF_SEL:CF_SEL + 16] = (seq[:, None] == np.arange(16)[None, :])
    for g, w in enumerate((2, 4, 8, 16)):
        cf[:, CF_RC + 16 * g:CF_RC + 16 * g + 16] = (1.0 / np.minimum(np.arange(16) + 1, w))[None, :]
    cf[:, CF_EPS] = 1e-6
    cf[:, CF_ONE] = 1.0
    kk = np.arange(16)
    cf[0:16, CF_L2:CF_L2 + 128] = (kk[:, None] % 2 == (np.arange(128)[None, :] // 64))
    cf[0:16, CF_PAIR:CF_PAIR + 8] = (kk[:, None] // 2 == np.arange(8)[None, :])
    cb = np.zeros((128, CB_N), f)
    cb[:, CB_ID:CB_ID + 128] = np.eye(128)
    cb[:, CB_ONES:CB_ONES + 128] = 1.0
    mp = np.where(t[:, None] <= t[None, :], 0.0, NEG)
    ms = np.where((t[:, None] <= t[None, :]) & (seq[:, None] == seq[None, :]), 0.0, NEG)
    cb[:, CB_MNP:CB_MNP + 512] = np.tile(mp, (1, 4))
    cb[:, CB_MNS:CB_MNS + 512] = np.tile(ms, (1, 4))
    oh = (np.arange(16)[:, None] == seq[None, :]).astype(f)
    cb[:, CB_OH:CB_OH + 2048] = oh.reshape(1, 2048)
    return {"wg": wg, "wdt": wdt, "wmix": wmix, "pcol": pcol, "prow": prow, "cstf": cf, "cstb": cb}


class Mode:
    def __init__(self, name):
        self.name = name
        if name == "P":
            self.TB, self.NT, self.nseq, self.L = 512, 4, 1, 512
        else:
            self.TB, self.NT, self.nseq, self.L = 128, 1, NSEQ_S, LS


def build_nc():
    nc = bass.Bass("TRN2", target_bir_lowering=False)
    D = {}

    def din(name, shape):
        D[name] = nc.dram_tensor(name, list(shape), F32, kind="ExternalInput").ap()

    def dout(name, shape):
        D[name] = nc.dram_tensor(name, list(shape), F32, kind="ExternalOutput").ap()

    din("xp", [SEQ, DM]); din("xs", [128, DM])
    din("st_ssm", [DEPTH, NSEQ_S, 1024, 128]); din("st_sc", [DEPTH, 48, 1536])
    din("st_cf", [DEPTH, 480, 512]); din("st_pl", [DEPTH, 240, 512])
    din("wg", [DEPTH, NGRP, 128, GW]); din("wdt", [DEPTH, 128, 128]); din("wmix", [DEPTH, 128, 512])
    din("pcol", [DEPTH, 128, PC_N]); din("prow", [128, 64 + 1024])
    din("cstf", [128, CF_N]); din("cstb", [128, CB_N])
    dout("yp", [SEQ, DM]); dout("ys", [128, DM])
    dout("ssm_p", [DEPTH, 1024, 128]); dout("ssm_s", [DEPTH, NSEQ_S, 1024, 128])
    dout("sc_p", [DEPTH, 3, 1536]); dout("sc_s", [DEPTH, 48, 1536])
    dout("cf_p", [DEPTH, 30, 512]); dout("cf_s", [DEPTH, 480, 512])
    dout("pl_p", [DEPTH, 15, 512]); dout("pl_s", [DEPTH, 240, 512])

    P = Prog(nc)
    es = contextlib.ExitStack()
    with es:
        def sb(name, shape, dt=F32):
            return es.enter_context(nc.sbuf_tensor("sb_" + name, list(shape), dt))

        psum = es.enter_context(nc.psum_tensor("psum", [128, 8, 512], F32))
        ps_ctr = [0]

        pinned = set()

        def ps_get(pin=False):
            assert len(pinned) < 8
            while True:
                b = ps_ctr[0] % 8
                ps_ctr[0] += 1
                if b not in pinned:
                    break
            if pin:
                pinned.add(b)
            return b

        def psf(b):
            return psum[:, b, :]

        def psb(b):
            return psum[:, b, :].bitcast(BF16)

        def psn(b):
            return "ps%d" % b

        x_tok = sb("x_tok", [128, 4, 1024])
        xn_fm = sb("xn_fm", [128, 8, 512], BF16)
        zy = sb("zy", [128, 8, 512], BF16)
        xe = sb("xe", [128, 2, 528])
        xbc = sb("xbc", [128, 12, 512], BF16)
        ve = sb("ve", [128, 4, 608])
        ue = sb("ue", [128, 4, 528])
        gbs = sb("gbs", [128, 4, 512], BF16)
        gcs = sb("gcs", [128, 4, 512], BF16)
        wring = sb("wring", [128, 3, GW], BF16)
        wdt_sb = sb("wdt_sb", [128, 2, 128], BF16)
        wmix_sb = sb("wmix_sb", [128, 2, 512], BF16)
        halo_sc = sb("halo_sc", [128, 2, 36])
        halo_cf = sb("halo_cf", [128, 2, 120])
        halo_pl = sb("halo_pl", [128, 2, 60])
        stP = sb("stP", [128, 2, 1024])
        cstf = sb("cstf", [128, CF_N])
        cstb = sb("cstb", [128, CB_N], BF16)
        pcol = sb("pcol", [128, 2, PC_N])
        prow = sb("prow", [128, 64 + 1024])
        A_bc = sb("A_bc", [128, 2, 16])
        sm = sb("sm", [128, 16, 16])
        ss = sb("ss", [128, 16])
        dt_tok = sb("dt_tok", [128, 4, 16])
        a_tok = sb("a_tok", [128, 4, 16])
        FA = sb("FA", [128, 1024]); FB = sb("FB", [128, 1024]); FC = sb("FC", [128, 1024])
        FD = sb("FD", [128, 1024])
        HA = sb("HA", [128, 1024], BF16); HB = sb("HB", [128, 1024], BF16); HC = sb("HC", [128, 1024], BF16)
        HD = sb("HD", [128, 1024], BF16)
        XC = sb("XC", [128, 2048])
        DMb = sb("DMb", [128, 4096], BF16)
        hT = sb("hT", [128, 2, 1024], BF16)
        UNI = sb("UNI", [128, 4672])
        stgA = UNI[:, 0:2048]
        h0b = UNI[:, 2048:3072].bitcast(BF16)
        CBx = UNI[:, 3072:4096].bitcast(BF16).rearrange("p (a b) -> p a b", a=2)
        halo_scS = UNI[:, 4096:4672].rearrange("p (j e) -> p j e", j=12)
        ve_bf = UNI[:, 0:1088].bitcast(BF16)[:, 0:4 * 542].rearrange("p (i e) -> p i e", i=4)
        diag = UNI[:, 1088:1600].bitcast(BF16).rearrange("p (s m) -> p s m", s=8)
        cbf = UNI[:, 1600:2624].bitcast(BF16).rearrange("p (i t) -> p i t", i=4)
        csq = UNI[:, 2624:3136].bitcast(BF16).rearrange("p (s t) -> p s t", s=2)
        HA1 = UNI[:, 3136:3648].bitcast(BF16)
        HB1 = UNI[:, 3648:4160].bitcast(BF16)
        LT = UNI[:, 4160:4672]
        UNI_GUARD = ["stgA", "h0b", "Cx", "Bx"] + ["hscS%d" % j for j in range(12)] + ["ve%d" % i for i in range(4)] + ["YT0", "YT1", "FDb"]
        ve_flat = ve[:].rearrange("p a b -> p (a b)")
        YT0 = ve_flat[:, 0:512].bitcast(BF16)
        YT1 = ve_flat[:, 512:1024].bitcast(BF16)
        FDb = ve_flat[:, 1024:2048]
        ecend = sb("ecend", [128, 8, 16])
        rstdg = sb("rstdg", [128, 256])

        identF = cstf[:, CF_ID:CF_ID + 128]
        identB = cstb[:, CB_ID:CB_ID + 128]
        onesB = cstb[:, CB_ONES:CB_ONES + 128]
        onesF = cstf[:, CF_ONES:CF_ONES + 128]
        epsc = cstf[:, CF_EPS:CF_EPS + 1]
        onec = cstf[:, CF_ONE:CF_ONE + 1]

        P.dma("sp", "cst", f_dma(cstf[:], D["cstf"]), writes=["cst"])
        P.dma("sp", "cst", f_dma(prow[:], D["prow"]), writes=["cst"])
        for l in range(DEPTH):
            P.dma("sp", "cst", f_dma(pcol[:, l, :], D["pcol"][l]), writes=["cst"])
        P.dma("pool", "cstb", f_dma(cstb[:], D["cstb"]), writes=["cstb"])
        for l in range(DEPTH):
            P.dma("pool", "cstb", f_dma(wdt_sb[:, l, :], D["wdt"][l]), writes=["cstb"])
            P.dma("pool", "cstb", f_dma(wmix_sb[:, l, :], D["wmix"][l]), writes=["cstb"])
        for l in range(DEPTH):
            P.op("act", f_act(A_bc[:, l, :], prow[:, 32 * l + 16:32 * l + 32], AF.Exp), reads=["cst"], writes=["A_bc"])
            P.op("dve", f_ts(A_bc[:, l, :], A_bc[:, l, :], -1.0, ALU.mult), reads=["A_bc"], writes=["A_bc"])
        P.op("dve", f_ms(halo_sc[:], 0.0), writes=["halo_sc0", "halo_sc1"])
        P.op("dve", f_ms(halo_cf[:], 0.0), writes=["halo_cf0", "halo_cf1"])
        P.op("dve", f_ms(halo_pl[:], 0.0), writes=["halo_pl0", "halo_pl1"])
        P.op("dve", f_ms(stP[:], 0.0), writes=["stP0", "stP1"])

        wstate = {"n": 0}

        wsc = nc.dram_tensor("wsc", [DEPTH, NGRP, 128, GW], BF16, kind="Internal").ap()
        wstate["first"] = True

        def wload(l, g, ncols=GW):
            k = wstate["n"] % 3
            wstate["n"] += 1
            scn = "wsc_%d_%d" % (l, g)
            if wstate["first"]:
                P.dma("pool", "wp%d" % k, f_dma(wring[:, k, 0:ncols], D["wg"][l, g, :, 0:ncols]), writes=["w%d" % k])
                P.dma("sp", "wo%d" % k, f_dma(wsc[l, g, :, 0:ncols], wring[:, k, 0:ncols]), reads=["w%d" % k], writes=[scn])
            else:
                P.dma("sp", "w%d" % k, f_dma(wring[:, k, 0:ncols], wsc[l, g, :, 0:ncols]), reads=[scn], writes=["w%d" % k])
            return k

        def layer(md, l, blk_idx, first_block, last_block):
            TB, NT, nseq, L = md.TB, md.NT, md.nseq, md.L
            S = md.name == "S"
            pc = lambda c0, n=1: pcol[:, l, c0:c0 + n]
            TRI = cstf[:, CF_TRIS:CF_TRIS + 128] if S else cstf[:, CF_TRIP:CF_TRIP + 128]
            SAME = cstf[:, CF_SAMES:CF_SAMES + 128] if S else onesF
            MNEG = cstb[:, CB_MNS:CB_MNS + 512] if S else cstb[:, CB_MNP:CB_MNP + 512]
            SEL = cstf[:, CF_SEL:CF_SEL + 16] if S else cstf[:, CF_ONES:CF_ONES + 1]

            def tokv(ap):
                return ap.rearrange("p (b l) -> p b l", b=nseq)

            def extT(ap, E):
                return ap[:, 0:16 * E].rearrange("p (e b) -> p e b", b=16)

            def tok_ib(ap):
                return ap.rearrange("p (b i) -> p i b", b=16)

            def ib(ap):
                return ap.rearrange("p (i b) -> p i b", b=16)

            if S:
                P.dma("sp", "stgA", f_dma(stgA[0:48, 0:1536], D["st_sc"][l]), writes=["stgA"])
                for j in range(12):
                    b = ps_get()
                    P.op("pe", f_tr(psf(b)[:, 0:48], stgA[0:48, j * 128:(j + 1) * 128], identF[0:48, 0:48]),
                         reads=["stgA", "cst"], writes=[psn(b)])
                    P.op("act", f_act(halo_scS[:, j, :], psf(b)[:, 0:48], AF.Copy), reads=[psn(b)], writes=["hscS%d" % j])
                for q in range(4):
                    P.dma("sp", "stgA", f_dma(stgA[0:120, 0:512], D["st_cf"][l, q * 120:(q + 1) * 120, :]), writes=["stgA"])
                    for i in range(4):
                        b = ps_get()
                        P.op("pe", f_tr(psf(b)[:, 0:120], stgA[0:120, i * 128:(i + 1) * 128], identF[0:120, 0:120]),
                             reads=["stgA", "cst"], writes=[psn(b)])
                        dst = extT(ve[:, i, :], 38)[:, 0:30, 4 * q:4 * q + 4]
                        P.op("act", f_act(dst, psf(b)[:, 0:120].rearrange("p (b e) -> p e b", b=4), AF.Copy),
                             reads=[psn(b)], writes=["ve%d" % i])
                for q in range(2):
                    P.dma("sp", "stgA", f_dma(stgA[0:120, 0:512], D["st_pl"][l, q * 120:(q + 1) * 120, :]), writes=["stgA"])
                    for i in range(4):
                        b = ps_get()
                        P.op("pe", f_tr(psf(b)[:, 0:120], stgA[0:120, i * 128:(i + 1) * 128], identF[0:120, 0:120]),
                             reads=["stgA", "cst"], writes=[psn(b)])
                        dst = extT(ue[:, i, :], 23)[:, 0:15, 8 * q:8 * q + 8]
                        P.op("act", f_act(dst, psf(b)[:, 0:120].rearrange("p (b e) -> p e b", b=8), AF.Copy),
                             reads=[psn(b)], writes=["ue%d" % i])

            for i in range(NT):
                sl = i % 2
                P.op("dve", f_ms(ss[:, i:i + 1], 0.0), writes=["ss%d" % i])
                P.op("act", f_act((HA, HB)[sl][:], x_tok[:, i, :], AF.Square, accum=ss[:, i:i + 1]),
                     reads=["xt%d" % i], writes=[("HA", "HB")[sl], "ss%d" % i])
                P.op("act", f_act(ss[:, 4 + i:5 + i], ss[:, i:i + 1], AF.Ln, bias=epsc, scale=1.0 / DM),
                     reads=["ss%d" % i, "cst"], writes=["ssb%d" % i])
                P.op("act", f_act(ss[:, 8 + i:9 + i], ss[:, 4 + i:5 + i], AF.Exp, scale=-0.5),
                     reads=["ssb%d" % i], writes=["rstd%d" % i])
                P.op("dve", f_ts((HA, HB)[sl][:], x_tok[:, i, :], ss[:, 8 + i:9 + i], ALU.mult),
                     reads=["xt%d" % i, "rstd%d" % i], writes=[("HA", "HB")[sl]])
                b = ps_get()
                for dc in range(8):
                    P.op("pe", f_tr(psb(b)[:, dc * 128:(dc + 1) * 128], (HA, HB)[sl][:, dc * 128:(dc + 1) * 128], identB),
                         reads=[("HA", "HB")[sl], "cstb"], writes=[psn(b)])
                P.op("dve", f_tt(xn_fm[:, :, i * 128:(i + 1) * 128], psb(b).rearrange("p (d t) -> p d t", d=8),
                                 pc(PC_NW, 8).unsqueeze(2).broadcast_to([128, 8, 128]), ALU.mult),
                     reads=[psn(b), "cst"], writes=["xn_fm%d" % i])
            xnr = ["xn_fm%d" % i for i in range(NT)]

            def inproj(k, off, M=128):
                b = ps_get()
                for dc in range(8):
                    P.op("pe", f_mm(psf(b)[0:M, 0:TB], wring[:, k, off + dc * 128: off + dc * 128 + M], xn_fm[:, dc, 0:TB],
                                    start=dc == 0, stop=dc == 7), reads=["w%d" % k] + xnr, writes=[psn(b)])
                return b

            pend_val = {}
            for g in range(8):
                k = wload(l, g)
                for kk in range(5):
                    kind, j, _ = BCH[g * 5 + kk]
                    b = inproj(k, kk * 1024)
                    src = psf(b)[:, 0:TB]
                    if kind == "z":
                        P.op("act", f_act(zy[:, j, 0:TB], src, AF.Silu), reads=[psn(b)], writes=["zy%d" % j])
                    elif kind == "x":
                        s2 = j % 2
                        E = 3 + L
                        xevT = extT(xe[:, s2, :], 11)
                        xfl = xe[:, s2, :]
                        hsrc = halo_scS[:, j, :].rearrange("p (b r) -> p r b", b=16)
                        hres = "hscS%d" % j
                        P.op("dve", f_cp(xevT[:, 0:3, :], hsrc), reads=[hres], writes=["xe%d" % s2])
                        P.op("act", f_act(xevT[:, 3:11, :], tok_ib(src), AF.Copy), reads=[psn(b)], writes=["xe%d" % s2])
                        accb = (FC, FD)[s2]
                        accn = ("FC", "FD")[s2]
                        av = accb[:, 0:128]
                        P.op("dve", f_ts(av, xfl[:, 0:128], pc(PC_SCW + 4 * j), ALU.mult),
                             reads=["xe%d" % s2, "cst"], writes=[accn])
                        for t in range(1, 4):
                            P.op("dve", f_stt(av, xfl[:, t * 16:t * 16 + 128], pc(PC_SCW + 4 * j + t), av, ALU.mult, ALU.add),
                                 reads=["xe%d" % s2, accn, "cst"], writes=[accn])
                        P.op("act", f_act(tok_ib(xbc[:, j, 0:128]), ib(av), AF.Silu, bias=pc(PC_SCB + j)),
                             reads=[accn, "cst"], writes=["xbc%d" % j])
                        P.op("dve", f_cp(hsrc, xevT[:, 8:11, :]), reads=["xe%d" % s2], writes=[hres])
                    elif kind == "gv":
                        pend_val[j] = b
                    elif kind == "gg":
                        bv = pend_val.pop(j)
                        s2 = j % 2
                        sgt = (FA if s2 == 0 else FB)
                        sgn = "FA" if s2 == 0 else "FB"
                        P.op("act", f_act(sgt[:, 0:TB], src, AF.Sigmoid), reads=[psn(b)], writes=[sgn])
                        E = 30 + L
                        P.op("dve", f_tt(extT(ve[:, j, :], 38)[:, 30:38, :], tok_ib(psf(bv)[:, 0:128]), tok_ib(sgt[:, 0:128]), ALU.mult),
                             reads=[psn(bv), sgn], writes=["ve%d" % j])
                    elif kind == "gb":
                        P.op("act", f_act(gbs[:, j, 0:TB], src, AF.Silu), reads=[psn(b)], writes=["gbs%d" % j])
                    elif kind == "up":
                        E = 15 + L
                        P.op("act", f_act(extT(ue[:, j, :], 23)[:, 15:23, :], tok_ib(src), AF.Copy), reads=[psn(b)], writes=["ue%d" % j])
                    elif kind == "gc":
                        P.op("act", f_act(gcs[:, j, 0:TB], src, AF.Silu), reads=[psn(b)], writes=["gcs%d" % j])
                if g == 3:
                    b = ps_get()
                    for i in range(NT):
                        for dc in range(8):
                            P.op("pe", f_mm(psf(b)[:, i * 16:(i + 1) * 16], xn_fm[:, dc, i * 128:(i + 1) * 128],
                                            wdt_sb[:, l, dc * 16:(dc + 1) * 16], start=dc == 0, stop=dc == 7),
                                 reads=xnr + ["cstb"], writes=[psn(b)])
                    dtv = dt_tok[:, 0:NT, :]
                    P.op("dve", f_tt(dtv, psf(b)[:, 0:NT * 16].rearrange("p (i h) -> p i h", i=NT),
                                     prow[:, 32 * l:32 * l + 16].unsqueeze(1).broadcast_to([128, NT, 16]), ALU.add),
                         reads=[psn(b), "cst"], writes=["dt_tok"])
                    P.op("act", f_act(dtv, dtv, AF.Exp), reads=["dt_tok"], writes=["dt_tok"])
                    P.op("act", f_act(dtv, dtv, AF.Ln, bias=onec, scale=1.0), reads=["dt_tok", "cst"], writes=["dt_tok"])
                    P.op("dve", f_tt(a_tok[:, 0:NT, :], dtv, A_bc[:, l, :].unsqueeze(1).broadcast_to([128, NT, 16]), ALU.mult),
                         reads=["dt_tok", "A_bc"], writes=["a_tok"])

            decay = DMb[:, 0:2048].rearrange("p (h t) -> p h t", h=16)
            Mt = DMb[:, 2048:4096].rearrange("p (h t) -> p h t", h=16)
            for c in range(NT):
                tk = slice(c * 128, (c + 1) * 128)
                xsr = ["xbc%d" % j for j in range(8)]
                b1 = ps_get()
                for j in range(8):
                    P.op("pe", f_tr(psb(b1)[:, j * 128:(j + 1) * 128], xbc[:, j, tk], identB),
                         reads=["xbc%d" % j, "cstb"], writes=[psn(b1)])
                b2 = ps_get()
                for g in range(2):
                    P.op("pe", f_tr(psb(b2)[:, g * 128:(g + 1) * 128], xbc[:, 8 + g, tk], identB),
                         reads=["xbc%d" % (8 + g), "cstb"], writes=[psn(b2)])
                xdt = HA
                P.op("dve", f_tt(xdt[:].rearrange("p (h q) -> p h q", h=16), psb(b1).rearrange("p (h q) -> p h q", h=16),
                                 dt_tok[:, c, :].unsqueeze(2).broadcast_to([128, 16, 64]), ALU.mult),
                     reads=[psn(b1), "dt_tok"], writes=["HA"])
                Btok = HC
                P.op("act", f_act(Btok[:, 0:256], psb(b2)[:, 0:256], AF.Copy), reads=[psn(b2)], writes=["HC"])
                b3 = ps_get()
                P.op("pe", f_mm(psf(b3)[:, 0:16], TRI, a_tok[:, c, :]), reads=["a_tok", "cst"], writes=[psn(b3)])
                P.op("pe", f_mm(psf(b3)[:, 16:32], SAME, a_tok[:, c, :]), reads=["a_tok", "cst"], writes=[psn(b3)])
                P.op("pe", f_mm(psf(b3)[0:16, 128:256], a_tok[:, c, :], TRI), reads=["a_tok", "cst"], writes=[psn(b3)])
                cum, ncum, ecum, dd, dte = (sm[:, k, :] for k in range(5))
                P.op("dve", f_cp(cum, psf(b3)[:, 0:16]), reads=[psn(b3)], writes=["cum"])
                P.op("dve", f_ts(ncum, psf(b3)[:, 0:16], -1.0, ALU.mult), reads=[psn(b3)], writes=["ncum"])
                P.op("act", f_act(ecum, cum, AF.Exp), reads=["cum"], writes=["ecum"])
                P.op("dve", f_tt(dd, psf(b3)[:, 16:32], cum, ALU.subtract), reads=[psn(b3), "cum"], writes=["dd"])
                P.op("act", f_act(dte, dd, AF.Exp), reads=["dd"], writes=["dte"])
                xdtd = HB
                P.op("dve", f_tt(xdtd[:].rearrange("p (h q) -> p h q", h=16), xdt[:].rearrange("p (h q) -> p h q", h=16),
                                 dte.unsqueeze(2).broadcast_to([128, 16, 64]), ALU.mult),
                     reads=["HA", "dte"], writes=["HB"])
                P.op("dve", f_tt(XC[0:16, :].rearrange("p (h t) -> p h t", h=16),
                                 psf(b3)[0:16, 128:256].unsqueeze(1).broadcast_to([16, 16, 128]),
                                 identF[0:16, 0:16].unsqueeze(2).broadcast_to([16, 16, 128]), ALU.mult),
                     reads=[psn(b3), "cst"], writes=["XC"])
                b4 = ps_get()
                for g in range(2):
                    P.op("pe", f_mm(psf(b4)[:, g * 128:(g + 1) * 128], xbc[:, 8 + g, tk], xbc[:, 10 + g, tk]),
                         reads=["xbc%d" % (8 + g), "xbc%d" % (10 + g)], writes=[psn(b4)])
                for q in range(4):
                    bq = ps_get()
                    P.op("pe", f_mm(psf(bq), onesF[0:16, :], XC[0:16, q * 512:(q + 1) * 512], start=True, stop=False),
                         reads=["XC", "cst"], writes=[psn(bq)])
                    P.op("pe", f_mm(psf(bq), identB, MNEG, start=False, stop=True), reads=["cstb"], writes=[psn(bq)])
                    for hh in range(4):
                        h = 4 * q + hh
                        P.op("act", f_act(decay[:, h, :], psf(bq)[:, hh * 128:(hh + 1) * 128], AF.Exp, bias=ncum[:, h:h + 1]),
                             reads=[psn(bq), "ncum"], writes=["DMlo"])
                for g in range(2):
                    P.op("dve", f_tt(Mt[:, 8 * g:8 * g + 8, :], decay[:, 8 * g:8 * g + 8, :],
                                     psf(b4)[:, g * 128:(g + 1) * 128].unsqueeze(1).broadcast_to([128, 8, 128]), ALU.mult),
                         reads=["DMlo", psn(b4)], writes=["DMhi"])
                by = [ps_get(pin=True), ps_get(pin=True)]
                for h in range(16):
                    P.op("pe", f_mm(psf(by[h // 8])[:, (h % 8) * 64:(h % 8 + 1) * 64], Mt[:, h, :], xdt[:, h * 64:(h + 1) * 64]),
                         reads=["DMhi", "HA"], writes=[psn(by[h // 8])])
                a_x = FC
                P.op("dve", f_cp(a_x[:].rearrange("p (h q) -> p h q", h=16), a_tok[:, c, :].unsqueeze(2).broadcast_to([128, 16, 64])),
                     reads=["a_tok"], writes=["FC"])
                b5 = ps_get()
                for j in range(8):
                    P.op("pe", f_mm(psf(b5)[:, j * nseq:(j + 1) * nseq], a_x[:, j * 128:(j + 1) * 128], SEL),
                         reads=["FC", "cst"], writes=[psn(b5)])
                P.op("act", f_act(ecend[:, :, 0:nseq], psf(b5)[:, 0:8 * nseq].rearrange("p (j b) -> p j b", j=8), AF.Exp),
                     reads=[psn(b5)], writes=["ecend"])
                bo = [ps_get(pin=True), ps_get(pin=True)]
                if not S:
                    st = stP[:, l, :]
                    stn = "stP%d" % l
                    stb = HD
                    P.op("act", f_act(stb[:], st, AF.Copy), reads=[stn], writes=["HD"])
                    b6 = ps_get()
                    for j in range(8):
                        P.op("pe", f_tr(psb(b6)[:, j * 128:(j + 1) * 128], stb[:, j * 128:(j + 1) * 128], identB),
                             reads=["HD", "cstb"], writes=[psn(b6)])
                    P.op("act", f_act(hT[:, 0, :], psb(b6), AF.Copy), reads=[psn(b6)], writes=["hT0"])
                    for g in range(2):
                        P.op("pe", f_mm(psf(bo[g]), xbc[:, 10 + g, tk], hT[:, 0, g * 512:(g + 1) * 512]),
                             reads=["xbc%d" % (10 + g), "hT0"], writes=[psn(bo[g])])
                    bs = [ps_get(), ps_get()]
                    for j in range(8):
                        P.op("pe", f_mm(psf(bs[j // 4])[:, (j % 4) * 128:(j % 4 + 1) * 128], xdtd[:, j * 128:(j + 1) * 128],
                                        Btok[:, (j // 4) * 128:(j // 4 + 1) * 128]),
                             reads=["HB", "HC"], writes=[psn(bs[j // 4])])
                    for j in range(8):
                        P.op("dve", f_stt(st[:, j * 128:(j + 1) * 128], st[:, j * 128:(j + 1) * 128], ecend[:, j, 0:1],
                                          psf(bs[j // 4])[:, (j % 4) * 128:(j % 4 + 1) * 128], ALU.mult, ALU.add),
                             reads=[stn, "ecend", psn(bs[j // 4])], writes=[stn])
                else:
                    OH = cstb[:, CB_OH:CB_OH + 2048].rearrange("p (b t) -> p b t", b=16)
                    SELb = cstf[:, CF_SEL:CF_SEL + 16]
                    P.op("dve", f_ms(ss[:, 14:15], 0.0), writes=["stgA", "XC", "h0b", "Cx", "Bx", "Q0", "Q1", "Q2", "Q3", "h0b0", "h0b1"] + ["cbx%d" % k_ for k_ in range(4)])
                    Qs = [(stgA[:, 0:1024], "Q0"), (stgA[:, 1024:2048], "Q1"), (XC[:, 0:1024], "Q2"), (XC[:, 1024:2048], "Q3")]
                    CBf = CBx.rearrange("p a b -> p (a b)")
                    def q_load(sq_):
                        Q_, Qn_ = Qs[sq_ % 4]
                        P.dma("sp", Qn_, f_dma(Q_.rearrange("p (j n) -> p j n", j=8),
                                               D["st_ssm"][l, sq_].rearrange("(j p) n -> p j n", p=128)), writes=[Qn_])

                    for sq in range(4):
                        q_load(sq)
                    for sq in range(NSEQ_S):
                        Q, Qn = Qs[sq % 4]
                        s2 = sq % 2
                        hb_, hbn = h0b[:, s2 * 1024:(s2 + 1) * 1024], "h0b%d" % s2
                        cb_, cbn = CBf[:, (sq % 4) * 512:(sq % 4 + 1) * 512], "cbx%d" % (sq % 4)
                        Q3 = Q.rearrange("p (j n) -> p j n", j=8)
                        P.op("act", f_act(hb_, Q, AF.Copy), reads=[Qn], writes=[hbn])
                        for g in range(2):
                            P.op("pool", f_tt(cb_[:, g * 128:(g + 1) * 128], xbc[:, 10 + g, 0:128], OH[:, sq, :], ALU.mult),
                                 reads=["xbc%d" % (10 + g), "cstb"], writes=[cbn])
                            P.op("pool", f_ts(cb_[:, 256 + g * 128:256 + (g + 1) * 128], Btok[:, g * 128:(g + 1) * 128], SELb[:, sq:sq + 1], ALU.mult),
                                 reads=["HC", "cst"], writes=[cbn])
                        b6 = ps_get()
                        for j in range(8):
                            P.op("pe", f_tr(psb(b6)[:, j * 128:(j + 1) * 128], hb_[:, j * 128:(j + 1) * 128], identB),
                                 reads=[hbn, "cstb"], writes=[psn(b6)])
                        P.op("act", f_act(hT[:, s2, :], psb(b6), AF.Copy), reads=[psn(b6)], writes=["hT%d" % s2])
                        for g in range(2):
                            P.op("pe", f_mm(psf(bo[g]), cb_[:, g * 128:(g + 1) * 128], hT[:, s2, g * 512:(g + 1) * 512],
                                            start=(sq == 0), stop=(sq == NSEQ_S - 1)),
                                 reads=[cbn, "hT%d" % s2], writes=[psn(bo[g])])
                        P.op("dve", f_tt(Q3, Q3, ecend[:, :, sq:sq + 1].broadcast_to([128, 8, 128]), ALU.mult),
                             reads=[Qn, "ecend", hbn], writes=[Qn])
                        bs_ = [ps_get(), ps_get()]
                        for j in range(8):
                            g = j // 4
                            P.op("pe", f_mm(psf(bs_[g])[:, (j % 4) * 128:(j % 4 + 1) * 128], xdtd[:, j * 128:(j + 1) * 128],
                                            cb_[:, 256 + g * 128:256 + (g + 1) * 128]),
                                 reads=["HB", cbn], writes=[psn(bs_[g])])
                        for g in range(2):
                            P.op("dve", f_tt(Q[:, g * 512:(g + 1) * 512], Q[:, g * 512:(g + 1) * 512], psf(bs_[g]), ALU.add),
                                 reads=[Qn, psn(bs_[g])], writes=[Qn])
                        P.dma("sp", Qn, f_dma(D["ssm_s"][l, sq].rearrange("(j p) n -> p j n", p=128), Q3), reads=[Qn])
                        if sq + 4 < NSEQ_S:
                            q_load(sq + 4)
                    P.op("dve", f_ms(ss[:, 14:15], 0.0), writes=["stgA", "XC", "h0b", "Cx", "Bx", "Q0", "Q1", "Q2", "Q3", "h0b0", "h0b1"] + ["cbx%d" % k_ for k_ in range(4)])
                t1 = FD
                ytok = HD
                for g in range(2):
                    P.op("dve", f_tt(t1[:, g * 512:(g + 1) * 512].rearrange("p (h q) -> p h q", h=8),
                                     psf(bo[g]).rearrange("p (h q) -> p h q", h=8),
                                     ecum[:, 8 * g:8 * g + 8].unsqueeze(2).broadcast_to([128, 8, 64]), ALU.mult),
                         reads=[psn(bo[g]), "ecum"], writes=["FD"])
                    P.op("dve", f_tt(ytok[:, g * 512:(g + 1) * 512], t1[:, g * 512:(g + 1) * 512], psf(by[g]), ALU.add),
                         reads=["FD", psn(by[g])], writes=["HD"])
                for b_ in by + bo:
                    pinned.discard(b_)
                b7 = ps_get()
                for j in range(8):
                    P.op("pe", f_tr(psb(b7)[:, j * 128:(j + 1) * 128], ytok[:, j * 128:(j + 1) * 128], identB),
                         reads=["HD", "cstb"], writes=[psn(b7)])
                dxs = FC
                P.op("dve", f_tt(dxs[:].rearrange("p (j t) -> p j t", j=8), xbc[:, 0:8, tk],
                                  pc(PC_D, 8).unsqueeze(2).broadcast_to([128, 8, 128]), ALU.mult),
                     reads=xsr + ["cst"], writes=["FC"])
                P.op("dve", f_tt(dxs[:], dxs[:], psb(b7), ALU.add), reads=["FC", psn(b7)], writes=["FC"])
                zyr = ["zy%d" % j for j in range(8)]
                yg = FD
                P.op("dve", f_tt(yg[:].rearrange("p (j t) -> p j t", j=8), dxs[:].rearrange("p (j t) -> p j t", j=8),
                                 zy[:, :, tk], ALU.mult), reads=["FC"] + zyr, writes=["FD"])
                ygsq = HA
                P.op("act", f_act(ygsq[:], yg[:], AF.Square), reads=["FD"], writes=["HA"])
                b8 = ps_get()
                for g in range(2):
                    for jj in range(4):
                        P.op("pe", f_mm(psf(b8)[:, g * 128:(g + 1) * 128], onesB, ygsq[:, (4 * g + jj) * 128:(4 * g + jj + 1) * 128],
                                        start=jj == 0, stop=jj == 3), reads=["HA", "cstb"], writes=[psn(b8)])
                P.op("act", f_act(rstdg[:], psf(b8)[:, 0:256], AF.Ln, bias=epsc, scale=1.0 / 512), reads=[psn(b8), "cst"], writes=["rstdg"])
                P.op("act", f_act(rstdg[:], rstdg[:], AF.Exp, scale=-0.5), reads=["rstdg"], writes=["rstdg"])
                for j in range(8):
                    P.op("dve", f_stt(zy[:, j, tk], yg[:, j * 128:(j + 1) * 128], pc(PC_SNW + j),
                                      rstdg[:, (j // 4) * 128:(j // 4 + 1) * 128], ALU.mult, ALU.mult),
                         reads=["FD", "rstdg", "cst"], writes=["zy%d" % j])

            cacc = XC
            for i in range(4):
                E = 30 + L
                cv = cacc[:, i * 512:i * 512 + 128]
                rn = "XC"
                vbfS = UNI[:, 0:1216].bitcast(BF16).rearrange("p (i e) -> p i e", i=4)
                dgS = UNI[:, 2048:2560].bitcast(BF16).rearrange("p (s m) -> p s m", s=8)
                P.op("act", f_act(vbfS[:, i, :], ve[:, i, :], AF.Copy), reads=["ve%d" % i], writes=["stgA"])
                bcv = ps_get()
                for t0 in range(0, 31, 4):
                    nt_ = min(4, 31 - t0)
                    hs = (t0 // 4) % 2
                    P.op("dve", f_tt(dgS[:, 4 * hs:4 * hs + nt_, :], identB.unsqueeze(1).broadcast_to([128, nt_, 128]),
                                     pc(PC_CFW + 31 * i + t0, nt_).unsqueeze(2).broadcast_to([128, nt_, 128]), ALU.mult),
                         reads=["cstb", "cst"], writes=["h0b%d" % hs])
                    for t in range(t0, t0 + nt_):
                        P.op("pe", f_mm(psf(bcv)[:, 0:128], dgS[:, 4 * hs + t - t0, :], vbfS[:, i, t * 16:t * 16 + 128], start=t == 0, stop=t == 30),
                             reads=["h0b%d" % hs, "stgA"], writes=[psn(bcv)])
                P.op("act", f_act(cv, psf(bcv)[:, 0:128], AF.Copy), reads=[psn(bcv)], writes=[rn])
            bm, bq_ = ps_get(), ps_get()
            for i in range(4):
                cfl = cacc[:, i * 512:i * 512 + TB]
                s2 = i % 2
                hb = HA if s2 == 0 else HB
                hq = HC if s2 == 0 else HD
                P.op("act", f_act(hb[:, 0:TB], cfl, AF.Identity, bias=pc(PC_CFB + i)), reads=["XC", "cst"], writes=["HA" if s2 == 0 else "HB"])
                P.op("act", f_act(hq[:, 0:TB], cfl, AF.Square, bias=pc(PC_CFB + i)), reads=["XC", "cst"], writes=["HC" if s2 == 0 else "HD"])
                P.op("pe", f_mm(psf(bm)[:, 0:TB], onesB, hb[:, 0:TB], start=i == 0, stop=i == 3),
                     reads=["HA" if s2 == 0 else "HB", "cstb"], writes=[psn(bm)])
                P.op("pe", f_mm(psf(bq_)[:, 0:TB], onesB, hq[:, 0:TB], start=i == 0, stop=i == 3),
                     reads=["HC" if s2 == 0 else "HD", "cstb"], writes=[psn(bq_)])
            mean = FA
            m2 = FB
            P.op("dve", f_ts(mean[:, 0:TB], psf(bm)[:, 0:TB], 1.0 / 512, ALU.mult), reads=[psn(bm)], writes=["FA"])
            P.op("dve", f_tt(m2[:, 0:TB], mean[:, 0:TB], mean[:, 0:TB], ALU.mult), reads=["FA"], writes=["FB"])
            P.op("dve", f_stt(m2[:, 0:TB], psf(bq_)[:, 0:TB], 1.0 / 512, m2[:, 0:TB], ALU.mult, ALU.subtract),
                 reads=[psn(bq_), "FB"], writes=["FB"])
            P.op("act", f_act(m2[:, 0:TB], m2[:, 0:TB], AF.Ln, bias=epsc, scale=1.0), reads=["FB", "cst"], writes=["FB"])
            P.op("act", f_act(m2[:, 0:TB], m2[:, 0:TB], AF.Exp, scale=-0.5), reads=["FB"], writes=["FB"])
            for i in range(4):
                cfl = cacc[:, i * 512:i * 512 + TB]
                P.op("dve", f_stt(cfl, cfl, pc(PC_CFB + i), mean[:, 0:TB], ALU.add, ALU.subtract), reads=["XC", "FA", "cst"], writes=["XC"])
                P.op("dve", f_tt(cfl, cfl, m2[:, 0:TB], ALU.mult), reads=["XC", "FB"], writes=["XC"])
                P.op("act", f_act(FC[:, 0:TB], cfl, AF.Silu, bias=pc(PC_LNB + i), scale=pc(PC_LNW + i)), reads=["XC", "cst"], writes=["FC"])
                P.op("dve", f_tt(tokv(gbs[:, i, 0:TB]), FC[:, 0:TB].rearrange("p (i b) -> p b i", b=16), tokv(gbs[:, i, 0:TB]), ALU.mult),
                     reads=["FC", "gbs%d" % i], writes=["gbs%d" % i])

            for g in range(4):
                w = 2 ** (g + 1)
                E = 15 + L
                ufl = ue[:, g, :]
                cur, curn = ufl, "ue%d" % g
                off = 0
                bufs = [(FA, "FA"), (FB, "FB")]
                for si, d in enumerate([1, 2, 4, 8][:g + 1]):
                    nb, nbn = bufs[si % 2]
                    P.op("dve", f_tt(nb[:, (off + d) * 16:E * 16], cur[:, (off + d) * 16:E * 16], cur[:, off * 16:(E - d) * 16], ALU.add),
                         reads=[curn], writes=[nbn])
                    cur, curn = nb, nbn
                    off += d
                pl = HA if g % 2 == 0 else HB
                pln = "HA" if g % 2 == 0 else "HB"
                P.op("dve", f_stt(pl[:, 0:128], cur[:, 15 * 16:23 * 16], 1.0 / w, ufl[:, 15 * 16:23 * 16], ALU.mult, ALU.subtract),
                     reads=[curn, "ue%d" % g], writes=[pln])
                if (not S) and first_block:
                    tmp = sm[:, 8, :]
                    P.op("dve", f_tt(tmp, cur[:, 0, 15:31], cstf[:, CF_RC + 16 * g:CF_RC + 16 * g + 16], ALU.mult),
                         reads=[curn, "cst"], writes=["tmp16"])
                    P.op("dve", f_tt(pl[:, 0:16], tmp, uev[:, 0, 15:31], ALU.subtract), reads=["tmp16", "ue%d" % g, pln], writes=[pln])
                if not S:
                    P.op("dve", f_cp(halo_pl[:, l, g * 15:(g + 1) * 15].rearrange("p (b e) -> p b e", b=1), uev[:, :, L:L + 15]),
                         reads=["ue%d" % g], writes=["halo_pl%d" % l])
                b = ps_get()
                P.op("pe", f_mm(psf(b)[:, 0:TB], wmix_sb[:, l, g * 128:(g + 1) * 128], pl[:, 0:TB]), reads=[pln, "cstb"], writes=[psn(b)])
                P.op("dve", f_stt(tokv(gcs[:, g, 0:TB]), psf(b)[:, 0:TB].rearrange("p (i b) -> p b i", b=16), pc(PC_PSC + g),
                                  tokv(gcs[:, g, 0:TB]), ALU.mult, ALU.mult),
                     reads=[psn(b), "gcs%d" % g, "cst"], writes=["gcs%d" % g])

            def rows_out(src_fn, nchunk, nrows, dst, stg, stgn):
                for i in range(nchunk):
                    b = ps_get()
                    sap, srn = src_fn(i)
                    if len(sap.shape) > 2:
                        fb, fbn = ((FA, "FA"), (FB, "FB"))[i % 2]
                        P.op("dve", f_cp(fb[:, 0:nrows].rearrange("p (b e) -> p b e", b=sap.shape[1]), sap), reads=[srn], writes=[fbn])
                        sap, srn = fb[:, 0:nrows], fbn
                    P.op("pe", f_tr(psf(b)[0:nrows, 0:128], sap, identF), reads=[srn, "cst"], writes=[psn(b)])
                    P.op("act", f_act(stg[0:nrows, i * 128:(i + 1) * 128], psf(b)[0:nrows, 0:128], AF.Copy), reads=[psn(b)], writes=[stgn])
                P.dma("sp", stgn, f_dma(dst, stg[0:nrows, 0:nchunk * 128]), reads=[stgn])

            zyr = ["zy%d" % j for j in range(8)]
            gbr = ["gbs%d" % j for j in range(4)]
            gcr = ["gcs%d" % j for j in range(4)]
            merged = DMb[:].rearrange("p (j t) -> p j t", j=8)
            for j in range(8):
                k = wload(l, 8 + j)
                wk = "w%d" % k
                mres = "DMlo" if j < 4 else "DMhi"
                prods = []
                for br, (koff, goff, nk, src, srcr) in enumerate(((0, 2048, 8, zy, zyr), (1024, 3072, 4, gbs, gbr), (1536, 4096, 4, gcs, gcr))):
                    bo_ = ps_get()
                    for kc in range(nk):
                        P.op("pe", f_mm(psf(bo_)[:, 0:TB], wring[:, k, koff + kc * 128:koff + (kc + 1) * 128], src[:, kc, 0:TB],
                                        start=kc == 0, stop=kc == nk - 1), reads=[wk] + srcr, writes=[psn(bo_)])
                    bg_ = inproj(k, goff)
                    sgt, sgn = ((FA, "FA"), (FB, "FB"), (FC, "FC"))[br]
                    P.op("act", f_act(sgt[:, 0:TB], psf(bg_)[:, 0:TB], AF.Sigmoid), reads=[psn(bg_)], writes=[sgn])
                    P.op("dve", f_tt(sgt[:, 0:TB], sgt[:, 0:TB], psf(bo_)[:, 0:TB], ALU.mult), reads=[sgn, psn(bo_)], writes=[sgn])
                P.op("dve", f_tt(FA[:, 0:TB], FA[:, 0:TB], FB[:, 0:TB], ALU.add), reads=["FA", "FB"], writes=["FA"])
                P.op("dve", f_tt(merged[:, j, 0:TB], FA[:, 0:TB], FC[:, 0:TB], ALU.add), reads=["FA", "FC"], writes=[mres])
            for hf in range(2):
                k = wload(l, 16 + hf, 4096)
                for i in range(NT):
                    b = ps_get()
                    for dc in range(8):
                        P.op("pe", f_mm(psf(b), merged[:, dc, i * 128:(i + 1) * 128], wring[:, k, dc * 512:(dc + 1) * 512],
                                        start=dc == 0, stop=dc == 7), reads=["w%d" % k, "DMlo", "DMhi"], writes=[psn(b)])
                    xv = x_tok[:, i, hf * 512:(hf + 1) * 512]
                    P.op("dve", f_tt(xv, xv, psf(b), ALU.add), reads=["xt%d" % i, psn(b)], writes=["xt%d" % i])

            if S:
                rows_out(lambda j: (halo_scS[:, j, :], "hscS%d" % j), 12, 48, D["sc_s"][l], XC, "XC")
                for q in range(4):
                    rows_out(lambda i, q=q: (extT(ve[:, i, :], 38)[:, 8:38, 4 * q:4 * q + 4].rearrange("p e b -> p b e"), "ve%d" % i),
                             4, 120, D["cf_s"][l, q * 120:(q + 1) * 120, :], XC, "XC")
                for q in range(2):
                    rows_out(lambda i, q=q: (extT(ue[:, i, :], 23)[:, 8:23, 8 * q:8 * q + 8].rearrange("p e b -> p b e"), "ue%d" % i),
                             4, 120, D["pl_s"][l, q * 120:(q + 1) * 120, :], XC, "XC")
            elif last_block:
                rows_out(lambda j: (halo_sc[:, l, j * 3:(j + 1) * 3], "halo_sc%d" % l), 12, 3, D["sc_p"][l], XC, "XC")
                rows_out(lambda i: (halo_cf[:, l, i * 30:(i + 1) * 30], "halo_cf%d" % l), 4, 30, D["cf_p"][l], XC, "XC")
                rows_out(lambda i: (halo_pl[:, l, i * 15:(i + 1) * 15], "halo_pl%d" % l), 4, 15, D["pl_p"][l], XC, "XC")
                P.dma("sp", "stP%d" % l, f_dma(D["ssm_p"][l].rearrange("(j p) n -> p j n", p=128),
                                               stP[:, l, :].rearrange("p (j n) -> p j n", j=8)), reads=["stP%d" % l])


        def interleave(gens):
            gens = list(gens)
            while gens:
                for g_ in list(gens):
                    try:
                        next(g_)
                    except StopIteration:
                        gens.remove(g_)
                yield

        def layer_P(l, k_blk, first_block, last_block):
            TB, NT, L = 512, 4, 512
            guard = []
            pc = lambda c0, n=1: pcol[:, l, c0:c0 + n]
            TRI = cstf[:, CF_TRIP:CF_TRIP + 128]
            MNEG = cstb[:, CB_MNP:CB_MNP + 512]
            L2c = cstf[0:16, CF_L2:CF_L2 + 128]
            PAIR = cstf[0:16, CF_PAIR:CF_PAIR + 8]

            xts = [(HA, "HA"), (HB, "HB"), (HC, "HCx"), (HD, "HD")]
            for i in range(NT):
                xt_, xtn = xts[i]
                P.op("dve", f_ms(ss[:, i:i + 1], 0.0), writes=["ss%d" % i])
                P.op("act", f_act(xt_[:], x_tok[:, i, :], AF.Square, accum=ss[:, i:i + 1]),
                     reads=["xt%d" % i], writes=[xtn, "ss%d" % i] + (["HC0", "HC1"] if i == 2 else []))
            for i in range(NT):
                P.op("act", f_act(ss[:, 4 + i:5 + i], ss[:, i:i + 1], AF.Ln, bias=epsc, scale=1.0 / DM),
                     reads=["ss%d" % i, "cst"], writes=["ssb%d" % i])
            for i in range(NT):
                P.op("act", f_act(ss[:, 8 + i:9 + i], ss[:, 4 + i:5 + i], AF.Exp, scale=-0.5),
                     reads=["ssb%d" % i], writes=["rstd%d" % i])
            for i in range(NT):
                xt_, xtn = xts[i]
                P.op("dve", f_ts(xt_[:], x_tok[:, i, :], ss[:, 8 + i:9 + i], ALU.mult),
                     reads=["xt%d" % i, "rstd%d" % i], writes=[xtn])
            pbs = []
            for i in range(NT):
                xt_, xtn = xts[i]
                b = ps_get()
                pbs.append(b)
                for dc in range(8):
                    P.op("pe", f_tr(psb(b)[:, dc * 128:(dc + 1) * 128], xt_[:, dc * 128:(dc + 1) * 128], identB),
                         reads=[xtn, "cstb"], writes=[psn(b)])
            for i in range(NT):
                b = pbs[i]
                P.op("dve", f_tt(xn_fm[:, :, i * 128:(i + 1) * 128], psb(b).rearrange("p (d t) -> p d t", d=8),
                                 pc(PC_NW, 8).unsqueeze(2).broadcast_to([128, 8, 128]), ALU.mult),
                     reads=[psn(b), "cst"], writes=["xn_fm%d" % i])
            xnr = ["xn_fm%d" % i for i in range(NT)]

            def inproj(k, off, pin=False):
                b = ps_get(pin=pin)
                for dc in range(8):
                    P.op("pe", f_mm(psf(b), wring[:, k, off + dc * 128: off + dc * 128 + 128], xn_fm[:, dc, :],
                                    start=dc == 0, stop=dc == 7), reads=["w%d" % k] + xnr, writes=[psn(b)])
                return b

            pend_val = {}
            pend_x = []
            xctr = [0]

            def bchunk(k, g, kk):
                kind, j, _ = BCH[g * 5 + kk]
                b = inproj(k, kk * 1024, pin=(kind == "gv"))
                src = psf(b)
                while pend_x:
                    pend_x.pop(0)()
                if kind == "z":
                    P.op("act", f_act(zy[:, j, :], src, AF.Silu), reads=[psn(b)], writes=["zy%d" % j])
                elif kind == "x":
                    s2 = j % 2
                    xeb = xe[:, s2, :].bitcast(BF16)[:, 0:515]
                    hsrc = halo_sc[:, l, j * 3:(j + 1) * 3]
                    hres = "halo_sc%d" % l
                    P.op("dve", f_cp(xeb[:, 0:3], hsrc), reads=[hres], writes=["xe%d" % s2])
                    P.op("act", f_act(xeb[:, 3:515], src, AF.Copy), reads=[psn(b)], writes=["xe%d" % s2])
                    P.op("act", f_act(hsrc, src[:, 509:512], AF.Copy), reads=[psn(b)], writes=[hres])
                    hs = xctr[0] % 2
                    xctr[0] += 1
                    P.op("dve", f_tt(diag[:, 4 * hs:4 * hs + 4, :], identB.unsqueeze(1).broadcast_to([128, 4, 128]),
                                     pc(PC_SCW + 4 * j, 4).unsqueeze(2).broadcast_to([128, 4, 128]), ALU.mult),
                         reads=["cstb", "cst"], writes=["diagh%d" % hs])

                    def part2(j=j, hs=hs, s2=s2, xeb=xeb):
                        b2_ = ps_get()
                        for t in range(4):
                            P.op("pe", f_mm(psf(b2_), diag[:, 4 * hs + t, :], xeb[:, t:t + 512], start=t == 0, stop=t == 3),
                                 reads=["diagh%d" % hs, "xe%d" % s2], writes=[psn(b2_)])
                        P.op("act", f_act(xbc[:, j, :], psf(b2_), AF.Silu, bias=pc(PC_SCB + j)), reads=[psn(b2_), "cst"], writes=["xbc%d" % j])

                    pend_x.append(part2)
                elif kind == "gv":
                    pend_val[j] = b
                elif kind == "gg":
                    bv = pend_val.pop(j)
                    s2 = j % 2
                    sgt, sgn = (FA, FB)[s2], ("FA", "FB")[s2]
                    P.op("act", f_act(sgt[:, 0:512], src, AF.Sigmoid), reads=[psn(b)], writes=[sgn])
                    hcf = halo_cf[:, l, j * 30:(j + 1) * 30]
                    P.op("dve", f_cp(ve_bf[:, j, 0:30], hcf), reads=["halo_cf%d" % l], writes=["vebf%d" % j])
                    P.op("dve", f_tt(ve_bf[:, j, 30:542], psf(bv), sgt[:, 0:512], ALU.mult), reads=[psn(bv), sgn], writes=["vebf%d" % j])
                    P.op("dve", f_tt(hcf, psf(bv)[:, 482:512], sgt[:, 482:512], ALU.mult), reads=[psn(bv), sgn], writes=["halo_cf%d" % l])
                    pinned.discard(bv)
                elif kind == "gb":
                    P.op("act", f_act(gbs[:, j, :], src, AF.Silu), reads=[psn(b)], writes=["gbs%d" % j])
                elif kind == "up":
                    uev = ue[:, j, 0:527]
                    P.op("dve", f_cp(uev[:, 0:15], halo_pl[:, l, j * 15:(j + 1) * 15]), reads=["halo_pl%d" % l], writes=["ue%d" % j])
                    P.op("act", f_act(uev[:, 15:527], src, AF.Copy), reads=[psn(b)], writes=["ue%d" % j])
                    P.op("dve", f_cp(halo_pl[:, l, j * 15:(j + 1) * 15], uev[:, 512:527]), reads=["ue%d" % j], writes=["halo_pl%d" % l])
                elif kind == "gc":
                    P.op("act", f_act(gcs[:, j, :], src, AF.Silu), reads=[psn(b)], writes=["gcs%d" % j])

            deferred_up = []
            for g in range(8):
                k = wload(l, g)
                for kk in range(5):
                    if BCH[g * 5 + kk][0] == "up":
                        deferred_up.append((k, g, kk))
                    else:
                        bchunk(k, g, kk)
            while pend_x:
                pend_x.pop(0)()
            b = ps_get()
            for i in range(NT):
                for dc in range(8):
                    P.op("pe", f_mm(psf(b)[:, i * 16:(i + 1) * 16], xn_fm[:, dc, i * 128:(i + 1) * 128],
                                    wdt_sb[:, l, dc * 16:(dc + 1) * 16], start=dc == 0, stop=dc == 7),
                         reads=xnr + ["cstb"], writes=[psn(b)])
            P.op("dve", f_tt(dt_tok[:], psf(b)[:, 0:64].rearrange("p (i h) -> p i h", i=4),
                             prow[:, 32 * l:32 * l + 16].unsqueeze(1).broadcast_to([128, 4, 16]), ALU.add),
                 reads=[psn(b), "cst"], writes=["dt_tok"])
            P.op("act", f_act(dt_tok[:], dt_tok[:], AF.Exp), reads=["dt_tok"], writes=["dt_tok"])
            P.op("act", f_act(dt_tok[:], dt_tok[:], AF.Ln, bias=onec, scale=1.0), reads=["dt_tok", "cst"], writes=["dt_tok"])
            P.op("dve", f_tt(a_tok[:], dt_tok[:], A_bc[:, l, :].unsqueeze(1).broadcast_to([128, 4, 16]), ALU.mult),
                 reads=["dt_tok", "A_bc"], writes=["a_tok"])

            def gen_conv():
                dctr = 0
                for i in range(4):
                    bc_ = ps_get(pin=True)
                    for t0 in range(0, 31, 4):
                        nt_ = min(4, 31 - t0)
                        hs = dctr % 2
                        dctr += 1
                        P.op("pool", f_tt(diag[:, 4 * hs:4 * hs + nt_, :], identB.unsqueeze(1).broadcast_to([128, nt_, 128]),
                                          pc(PC_CFW + 31 * i + t0, nt_).unsqueeze(2).broadcast_to([128, nt_, 128]), ALU.mult),
                             reads=["cstb", "cst"], writes=["diagh%d" % hs] + guard)
                        for t in range(t0, t0 + nt_):
                            P.op("pe", f_mm(psf(bc_), diag[:, 4 * hs + t - t0, :], ve_bf[:, i, t:t + 512], start=t == 0, stop=t == 30),
                                 reads=["diagh%d" % hs, "vebf%d" % i], writes=[psn(bc_)])
                        yield
                    P.op("act", f_act(cbf[:, i, :], psf(bc_), AF.Identity, bias=pc(PC_CFB + i)), reads=[psn(bc_), "cst"], writes=["cbf%d" % i] + guard)
                    pinned.discard(bc_)
                    yield

            def gen_fill():
                for (k_, g_, kk_) in deferred_up:
                    bchunk(k_, g_, kk_)
                    yield
                yield from gen_conv()
                bm, bq_ = ps_get(), ps_get()
                for i in range(4):
                    s2 = i % 2
                    P.op("act", f_act(csq[:, s2, :], cbf[:, i, :], AF.Square), reads=["cbf%d" % i], writes=["csq%d" % s2] + guard)
                    P.op("pe", f_mm(psf(bm), onesB, cbf[:, i, :], start=i == 0, stop=i == 3), reads=["cbf%d" % i, "cstb"], writes=[psn(bm)])
                    P.op("pe", f_mm(psf(bq_), onesB, csq[:, s2, :], start=i == 0, stop=i == 3), reads=["csq%d" % s2, "cstb"], writes=[psn(bq_)])
                mean, m2 = FA, FB
                P.op("dve", f_ts(mean[:, 0:512], psf(bm), 1.0 / 512, ALU.mult), reads=[psn(bm)], writes=["FA"])
                P.op("dve", f_tt(m2[:, 0:512], mean[:, 0:512], mean[:, 0:512], ALU.mult), reads=["FA"], writes=["FB"])
                P.op("dve", f_stt(m2[:, 0:512], psf(bq_), 1.0 / 512, m2[:, 0:512], ALU.mult, ALU.subtract), reads=[psn(bq_), "FB"], writes=["FB"])
                P.op("act", f_act(m2[:, 0:512], m2[:, 0:512], AF.Ln, bias=epsc, scale=1.0), reads=["FB", "cst"], writes=["FB"])
                P.op("act", f_act(m2[:, 0:512], m2[:, 0:512], AF.Exp, scale=-0.5), reads=["FB"], writes=["FB"])
                yield
                for i in range(4):
                    s2 = i % 2
                    P.op("pool", f_tt(LT, cbf[:, i, :], mean[:, 0:512], ALU.subtract), reads=["cbf%d" % i, "FA"], writes=["LT"] + guard)
                    P.op("pool", f_tt(cbf[:, i, :], LT, m2[:, 0:512], ALU.mult), reads=["LT", "FB"], writes=["cbf%d" % i])
                    yield
                for g in range(4):
                    w = 2 ** (g + 1)
                    uev = ue[:, g, 0:527]
                    cur, curn, off = uev, "ue%d" % g, 0
                    bufs = [(FA, "FA"), (FB, "FB")]
                    for si, d in enumerate([1, 2, 4, 8][:g + 1]):
                        nb, nbn = bufs[si % 2]
                        P.op("pool", f_tt(nb[:, off + d:527], cur[:, off + d:527], cur[:, off:527 - d], ALU.add), reads=[curn], writes=[nbn])
                        cur, curn = nb[:, 0:527], nbn
                        off += d
                    s2 = g % 2
                    pl, pln = csq[:, s2, :], "csq%d" % s2
                    P.op("dve", f_stt(pl, cur[:, 15:527], 1.0 / w, uev[:, 15:527], ALU.mult, ALU.subtract), reads=[curn, "ue%d" % g], writes=[pln])
                    if first_block:
                        tmp = sm[:, 10, :]
                        P.op("dve", f_tt(tmp, cur[:, 15:31], cstf[:, CF_RC + 16 * g:CF_RC + 16 * g + 16], ALU.mult), reads=[curn, "cst"], writes=["tmp16"])
                        P.op("dve", f_tt(pl[:, 0:16], tmp, uev[:, 15:31], ALU.subtract), reads=["tmp16", "ue%d" % g, pln], writes=[pln])
                    b = ps_get()
                    P.op("pe", f_mm(psf(b), wmix_sb[:, l, g * 128:(g + 1) * 128], pl), reads=[pln, "cstb"], writes=[psn(b)])
                    P.op("dve", f_stt(gcs[:, g, :], psf(b), pc(PC_PSC + g), gcs[:, g, :], ALU.mult, ALU.mult),
                         reads=[psn(b), "gcs%d" % g, "cst"], writes=["gcs%d" % g])
                    yield

            decay = DMb[:, 0:2048].rearrange("p (h t) -> p h t", h=16)
            Mt = DMb[:, 2048:4096].rearrange("p (h t) -> p h t", h=16)
            byb = {}

            def bufs_p(c):
                p = c % 2
                return ((HA, HA1)[p], ("HA", "HA1")[p], (HB, HB1)[p], ("HB", "HB1")[p],
                        HC[:, p * 256:(p + 1) * 256], "HC%d" % p, p)

            fctx = {}
            ev = set()

            def front1(c):
                tk = slice(c * 128, (c + 1) * 128)
                xdt, xdtn, xdtd, xdtdn, Btok, Btn, p = bufs_p(c)
                gd = guard if p == 1 else []
                b1 = ps_get()
                for j in range(8):
                    P.op("pe", f_tr(psb(b1)[:, j * 128:(j + 1) * 128], xbc[:, j, tk], identB), reads=["xbc%d" % j, "cstb"], writes=[psn(b1)])
                b2 = ps_get()
                for g in range(2):
                    P.op("pe", f_tr(psb(b2)[:, g * 128:(g + 1) * 128], xbc[:, 8 + g, tk], identB), reads=["xbc%d" % (8 + g), "cstb"], writes=[psn(b2)])
                b3 = ps_get(pin=True)
                P.op("pe", f_mm(psf(b3)[:, 0:16], TRI, a_tok[:, c, :]), reads=["a_tok", "cst"], writes=[psn(b3)])
                P.op("pe", f_mm(psf(b3)[:, 16:32], onesF, a_tok[:, c, :]), reads=["a_tok", "cst"], writes=[psn(b3)])
                P.op("pe", f_mm(psf(b3)[0:16, 128:256], a_tok[:, c, :], TRI), reads=["a_tok", "cst"], writes=[psn(b3)])
                P.op("dve", f_tt(xdt[:].rearrange("p (h q) -> p h q", h=16), psb(b1).rearrange("p (h q) -> p h q", h=16),
                                 dt_tok[:, c, :].unsqueeze(2).broadcast_to([128, 16, 64]), ALU.mult),
                     reads=[psn(b1), "dt_tok"], writes=[xdtn] + gd)
                P.op("act", f_act(Btok, psb(b2)[:, 0:256], AF.Copy), reads=[psn(b2)], writes=[Btn, "HCx"])
                yield
                cum, ncum, ecum, dd, dte = (sm[:, 5 * p + q_, :] for q_ in range(5))
                sfx = str(p)
                P.op("dve", f_cp(cum, psf(b3)[:, 0:16]), reads=[psn(b3)], writes=["cum" + sfx])
                P.op("dve", f_ts(ncum, psf(b3)[:, 0:16], -1.0, ALU.mult), reads=[psn(b3)], writes=["ncum" + sfx])
                P.op("act", f_act(ecum, cum, AF.Exp), reads=["cum" + sfx], writes=["ecum" + sfx])
                P.op("dve", f_tt(dd, psf(b3)[:, 16:32], cum, ALU.subtract), reads=[psn(b3), "cum" + sfx], writes=["dd" + sfx])
                P.op("act", f_act(dte, dd, AF.Exp), reads=["dd" + sfx], writes=["dte" + sfx])
                yield
                while c > 0 and ("seg%d" % (c - 1)) not in ev:
                    yield
                P.op("pool", f_tt(xdtd[:].rearrange("p (h q) -> p h q", h=16), xdt[:].rearrange("p (h q) -> p h q", h=16),
                                  dte.unsqueeze(2).broadcast_to([128, 16, 64]), ALU.mult),
                     reads=[xdtn, "dte" + sfx], writes=[xdtdn] + gd)
                P.op("dve", f_tt(XC[0:16, :].rearrange("p (h t) -> p h t", h=16),
                                 psf(b3)[0:16, 128:256].unsqueeze(1).broadcast_to([16, 16, 128]),
                                 identF[0:16, 0:16].unsqueeze(2).broadcast_to([16, 16, 128]), ALU.mult),
                     reads=[psn(b3), "cst"], writes=["XC"])
                cfc = sm[0:16, 11 + p, 0:1]
                r2 = sm[0:16, 13 + p, 0:8]
                P.op("dve", f_cp(cfc, psf(b3)[0:16, 255:256]), reads=[psn(b3)], writes=["cfc" + sfx])
                P.op("dve", f_ts(r2, PAIR, cfc, ALU.mult), reads=["cfc" + sfx, "cst"], writes=["r2" + sfx])
                pinned.discard(b3)
                b4 = ps_get(pin=True)
                for g in range(2):
                    P.op("pe", f_mm(psf(b4)[:, g * 128:(g + 1) * 128], xbc[:, 8 + g, tk], xbc[:, 10 + g, tk]),
                         reads=["xbc%d" % (8 + g), "xbc%d" % (10 + g)], writes=[psn(b4)])
                b5 = ps_get()
                P.op("pe", f_mm(psf(b5)[:, 0:8], L2c, r2), reads=["r2" + sfx, "cst"], writes=[psn(b5)])
                P.op("act", f_act(ecend[:, :, p], psf(b5)[:, 0:8], AF.Exp), reads=[psn(b5)], writes=["ecend" + sfx])
                fctx[c] = b4
                yield

            def front2(c):
                xdt, xdtn, xdtd, xdtdn, Btok, Btn, p = bufs_p(c)
                sfx = str(p)
                ncum = sm[:, 5 * p + 1, :]
                b4 = fctx.pop(c)
                for q in range(4):
                    bq = ps_get()
                    P.op("pe", f_mm(psf(bq), onesF[0:16, :], XC[0:16, q * 512:(q + 1) * 512], start=True, stop=False), reads=["XC", "cst"], writes=[psn(bq)])
                    P.op("pe", f_mm(psf(bq), identB, MNEG, start=False, stop=True), reads=["cstb"], writes=[psn(bq)])
                    for hh in range(4):
                        h = 4 * q + hh
                        P.op("act", f_act(decay[:, h, :], psf(bq)[:, hh * 128:(hh + 1) * 128], AF.Exp, bias=ncum[:, h:h + 1]),
                             reads=[psn(bq), "ncum" + sfx], writes=["DMlo"])
                    yield
                ev.add("seg%d" % c)
                for g in range(2):
                    P.op("dve", f_tt(Mt[:, 8 * g:8 * g + 8, :], decay[:, 8 * g:8 * g + 8, :],
                                     psf(b4)[:, g * 128:(g + 1) * 128].unsqueeze(1).broadcast_to([128, 8, 128]), ALU.mult),
                         reads=["DMlo", psn(b4)], writes=["DMhi"])
                pinned.discard(b4)
                yield
                by = [ps_get(), ps_get()]
                for h in range(16):
                    P.op("pe", f_mm(psf(by[h // 8])[:, (h % 8) * 64:(h % 8 + 1) * 64], Mt[:, h, :], xdt[:, h * 64:(h + 1) * 64]),
                         reads=["DMhi", xdtn], writes=[psn(by[h // 8])])
                YT, YTn = (YT0, YT1)[p], "YT%d" % p
                for g in range(2):
                    P.op("act", f_act(YT[:, g * 512:(g + 1) * 512], psf(by[g]), AF.Copy), reads=[psn(by[g])], writes=[YTn])
                yield

            def back_a(c):
                tk = slice(c * 128, (c + 1) * 128)
                xdt, xdtn, xdtd, xdtdn, Btok, Btn, p = bufs_p(c)
                sfx = str(p)
                ecum = sm[:, 5 * p + 2, :]
                YT, YTn = (YT0, YT1)[p], "YT%d" % p
                st, stn = stP[:, l, :], "stP%d" % l
                P.op("act", f_act(HD[:], st, AF.Copy), reads=[stn], writes=["HD"])
                b6 = ps_get()
                for j in range(8):
                    P.op("pe", f_tr(psb(b6)[:, j * 128:(j + 1) * 128], HD[:, j * 128:(j + 1) * 128], identB), reads=["HD", "cstb"], writes=[psn(b6)])
                P.op("act", f_act(hT[:, 0, :], psb(b6), AF.Copy), reads=[psn(b6)], writes=["hT0"])
                yield
                bo = [ps_get(), ps_get()]
                for g in range(2):
                    P.op("pe", f_mm(psf(bo[g]), xbc[:, 10 + g, tk], hT[:, 0, g * 512:(g + 1) * 512]),
                         reads=["xbc%d" % (10 + g), "hT0"], writes=[psn(bo[g])])
                for g in range(2):
                    P.op("dve", f_tt(FD[:, g * 512:(g + 1) * 512].rearrange("p (h q) -> p h q", h=8),
                                     psf(bo[g]).rearrange("p (h q) -> p h q", h=8),
                                     ecum[:, 8 * g:8 * g + 8].unsqueeze(2).broadcast_to([128, 8, 64]), ALU.mult),
                         reads=[psn(bo[g]), "ecum" + sfx], writes=["FD"])
                    P.op("dve", f_tt(YT[:, g * 512:(g + 1) * 512], FD[:, g * 512:(g + 1) * 512], YT[:, g * 512:(g + 1) * 512], ALU.add),
                         reads=["FD", YTn], writes=[YTn])
                yield
                bs = [ps_get(), ps_get()]
                for j in range(8):
                    P.op("pe", f_mm(psf(bs[j // 4])[:, (j % 4) * 128:(j % 4 + 1) * 128], xdtd[:, j * 128:(j + 1) * 128],
                                    Btok[:, (j // 4) * 128:(j // 4 + 1) * 128]), reads=[xdtdn, Btn], writes=[psn(bs[j // 4])])
                P.op("pool", f_tt(st.rearrange("p (j n) -> p j n", j=8), st.rearrange("p (j n) -> p j n", j=8),
                                  ecend[:, :, p:p + 1].broadcast_to([128, 8, 128]), ALU.mult),
                     reads=[stn, "ecend" + sfx], writes=[stn])
                for hf_ in range(2):
                    P.op("dve", f_tt(st[:, hf_ * 512:(hf_ + 1) * 512], st[:, hf_ * 512:(hf_ + 1) * 512], psf(bs[hf_]), ALU.add),
                         reads=[stn, psn(bs[hf_])], writes=[stn])
                yield

            def back_b(c):
                tk = slice(c * 128, (c + 1) * 128)
                p = c % 2
                YT, YTn = (YT0, YT1)[p], "YT%d" % p
                b7 = ps_get()
                for j in range(8):
                    P.op("pe", f_tr(psb(b7)[:, j * 128:(j + 1) * 128], YT[:, j * 128:(j + 1) * 128], identB), reads=[YTn, "cstb"], writes=[psn(b7)])
                P.op("pool", f_tt(FC[:].rearrange("p (j t) -> p j t", j=8), xbc[:, 0:8, tk],
                                  pc(PC_D, 8).unsqueeze(2).broadcast_to([128, 8, 128]), ALU.mult),
                     reads=["xbc%d" % j for j in range(8)] + ["cst"], writes=["FC"])
                P.op("dve", f_tt(FC[:], FC[:], psb(b7), ALU.add), reads=["FC", psn(b7)], writes=["FC"])
                zyr = ["zy%d" % j for j in range(8)]
                P.op("dve", f_tt(FDb.rearrange("p (j t) -> p j t", j=8), FC[:].rearrange("p (j t) -> p j t", j=8), zy[:, :, tk], ALU.mult),
                     reads=["FC"] + zyr, writes=["FDb"])
                yield
                P.op("act", f_act(YT, FDb, AF.Square), reads=["FDb"], writes=[YTn])
                b8 = ps_get()
                for g in range(2):
                    for jj in range(4):
                        P.op("pe", f_mm(psf(b8)[:, g * 128:(g + 1) * 128], onesB, YT[:, (4 * g + jj) * 128:(4 * g + jj + 1) * 128],
                                        start=jj == 0, stop=jj == 3), reads=[YTn, "cstb"], writes=[psn(b8)])
                P.op("act", f_act(rstdg[:], psf(b8)[:, 0:256], AF.Ln, bias=epsc, scale=1.0 / 512), reads=[psn(b8), "cst"], writes=["rstdg"])
                P.op("act", f_act(rstdg[:], rstdg[:], AF.Exp, scale=-0.5), reads=["rstdg"], writes=["rstdg"])
                for j in range(8):
                    P.op("dve", f_stt(zy[:, j, tk], FDb[:, j * 128:(j + 1) * 128], pc(PC_SNW + j),
                                      rstdg[:, (j // 4) * 128:(j // 4 + 1) * 128], ALU.mult, ALU.mult),
                         reads=["FDb", "rstdg", "cst"], writes=["zy%d" % j])
                yield

            def g_f1():
                for c in range(4):
                    while c >= 2 and not (("f2_%d" % (c - 2)) in ev and ("ba%d" % (c - 2)) in ev):
                        yield
                    yield from front1(c)
                    ev.add("f1_%d" % c)

            def g_f2():
                for c in range(4):
                    while ("f1_%d" % c) not in ev or (c >= 2 and ("bb%d" % (c - 2)) not in ev):
                        yield
                    yield from front2(c)
                    ev.add("f2_%d" % c)

            def g_ba():
                for c in range(4):
                    while ("f2_%d" % c) not in ev:
                        yield
                    yield from back_a(c)
                    ev.add("ba%d" % c)

            def g_bb():
                for c in range(4):
                    while ("ba%d" % c) not in ev:
                        yield
                    yield from back_b(c)
                    ev.add("bb%d" % c)

            g1_, g2_, ga_, gb_, gl_ = g_f1(), g_f2(), g_ba(), g_bb(), gen_fill()
            live = [g1_, g2_, ga_, gb_, gl_]
            order = [ga_, g2_, g1_, gb_, ga_, g2_, g1_, gl_]
            while live:
                for g_ in order:
                    if g_ in live:
                        try:
                            next(g_)
                        except StopIteration:
                            live.remove(g_)

            def rows_out(src_fn, nchunk, nrows, dst, stg, stgn):
                for i in range(nchunk):
                    b = ps_get()
                    sap, srn = src_fn(i)
                    P.op("pe", f_tr(psf(b)[0:nrows, 0:128], sap, identF), reads=[srn, "cst"], writes=[psn(b)])
                    P.op("act", f_act(stg[0:nrows, i * 128:(i + 1) * 128], psf(b)[0:nrows, 0:128], AF.Copy), reads=[psn(b)], writes=[stgn])
                P.dma("sp", stgn, f_dma(dst, stg[0:nrows, 0:nchunk * 128]), reads=[stgn])

            for i in range(4):
                s2 = i % 2
                P.op("act", f_act(csq[:, s2, :], cbf[:, i, :], AF.Silu, bias=pc(PC_LNB + i), scale=pc(PC_LNW + i)),
                     reads=["cbf%d" % i, "cst"], writes=["csq%d" % s2])
                P.op("dve", f_tt(gbs[:, i, :], csq[:, s2, :], gbs[:, i, :], ALU.mult), reads=["csq%d" % s2, "gbs%d" % i], writes=["gbs%d" % i])

            zyr = ["zy%d" % j for j in range(8)]
            gbr = ["gbs%d" % j for j in range(4)]
            gcr = ["gcs%d" % j for j in range(4)]
            merged = DMb[:].rearrange("p (j t) -> p j t", j=8)
            for j in range(8):
                k = wload(l, 8 + j)
                wk = "w%d" % k
                mres = "DMlo" if j < 4 else "DMhi"
                for br, (koff, goff, nk, src, srcr) in enumerate(((0, 2048, 8, zy, zyr), (1024, 3072, 4, gbs, gbr), (1536, 4096, 4, gcs, gcr))):
                    bo_ = ps_get()
                    for kc in range(nk):
                        P.op("pe", f_mm(psf(bo_), wring[:, k, koff + kc * 128:koff + (kc + 1) * 128], src[:, kc, :],
                                        start=kc == 0, stop=kc == nk - 1), reads=[wk] + srcr, writes=[psn(bo_)])
                    bg_ = inproj(k, goff)
                    sgt, sgn = ((FA, "FA"), (FB, "FB"), (FC, "FC"))[br]
                    P.op("act", f_act(sgt[:, 0:512], psf(bg_), AF.Sigmoid), reads=[psn(bg_)], writes=[sgn])
                    P.op("dve", f_tt(sgt[:, 0:512], sgt[:, 0:512], psf(bo_), ALU.mult), reads=[sgn, psn(bo_)], writes=[sgn])
                P.op("dve", f_tt(FA[:, 0:512], FA[:, 0:512], FB[:, 0:512], ALU.add), reads=["FA", "FB"], writes=["FA"])
                P.op("dve", f_tt(merged[:, j, :], FA[:, 0:512], FC[:, 0:512], ALU.add), reads=["FA", "FC"], writes=[mres])
            for hf in range(2):
                k = wload(l, 16 + hf, 4096)
                for i in range(NT):
                    b = ps_get()
                    for dc in range(8):
                        P.op("pe", f_mm(psf(b), merged[:, dc, i * 128:(i + 1) * 128], wring[:, k, dc * 512:(dc + 1) * 512],
                                        start=dc == 0, stop=dc == 7), reads=["w%d" % k, "DMlo", "DMhi"], writes=[psn(b)])
                    xv = x_tok[:, i, hf * 512:(hf + 1) * 512]
                    P.op("dve", f_tt(xv, xv, psf(b), ALU.add), reads=["xt%d" % i, psn(b)], writes=["xt%d" % i])

            if last_block:
                rows_out(lambda j: (halo_sc[:, l, j * 3:(j + 1) * 3], "halo_sc%d" % l), 12, 3, D["sc_p"][l], XC, "XC")
                rows_out(lambda i: (halo_cf[:, l, i * 30:(i + 1) * 30], "halo_cf%d" % l), 4, 30, D["cf_p"][l], XC, "XC")
                rows_out(lambda i: (halo_pl[:, l, i * 15:(i + 1) * 15], "halo_pl%d" % l), 4, 15, D["pl_p"][l], XC, "XC")
                P.dma("sp", "stP%d" % l, f_dma(D["ssm_p"][l].rearrange("(j p) n -> p j n", p=128),
                                               stP[:, l, :].rearrange("p (j n) -> p j n", j=8)), reads=["stP%d" % l])


        blocks = [("P", 0), ("S", 0), ("P", 1), ("P", 2), ("P", 3)]
        modes = {"P": Mode("P"), "S": Mode("S")}
        ALIAS_NAMES = (UNI_GUARD + ["vebf%d" % i for i in range(4)] + ["cbf%d" % i for i in range(4)]
                       + ["diagh0", "diagh1", "csq0", "csq1", "HA1", "HB1", "LT", "ecend", "ecend0", "ecend1", "HC", "HC0", "HC1",
                          "tmp16", "FA", "FB", "FC", "FD", "HA", "HB", "HD", "HCx", "XC", "DMlo", "DMhi", "hT0", "hT1", "rstdg",
                          "Q0", "Q1", "Q2", "Q3", "h0b0", "h0b1", "cbx0", "cbx1", "cbx2", "cbx3"]
                       + [n + sfx for n in ("cum", "ncum", "ecum", "dd", "dte", "cfc", "r2") for sfx in ("", "0", "1")])
        for (mn, k) in blocks:
            md = modes[mn]
            P.op("dve", f_ms(ss[:, 15:16], 0.0), writes=ALIAS_NAMES)
            for i in range(md.NT):
                src = D["xs"] if mn == "S" else D["xp"][k * 512 + i * 128:k * 512 + (i + 1) * 128, :]
                P.dma("act", "xt%d" % i, f_dma(x_tok[:, i, :], src), writes=["xt%d" % i])
            for l in range(DEPTH):
                if mn == "P":
                    layer_P(l, k, first_block=(k == 0), last_block=(k == 3))
                else:
                    layer(md, l, k, first_block=(k == 0), last_block=(k == 3))
            wstate["first"] = False
            for i in range(md.NT):
                sl = i % 2
                P.op("dve", f_ms(ss[:, i:i + 1], 0.0), writes=["ss%d" % i])
                P.op("act", f_act((HA, HB)[sl][:], x_tok[:, i, :], AF.Square, accum=ss[:, i:i + 1]),
                     reads=["xt%d" % i], writes=[("HA", "HB")[sl], "ss%d" % i])
            for i in range(md.NT):
                P.op("act", f_act(ss[:, 4 + i:5 + i], ss[:, i:i + 1], AF.Ln, bias=epsc, scale=1.0 / DM),
                     reads=["ss%d" % i, "cst"], writes=["ssb%d" % i])
            for i in range(md.NT):
                P.op("act", f_act(ss[:, 8 + i:9 + i], ss[:, 4 + i:5 + i], AF.Exp, scale=-0.5), reads=["ssb%d" % i], writes=["rstd%d" % i])
            for i in range(md.NT):
                P.op("dve", f_stt(x_tok[:, i, :], x_tok[:, i, :], ss[:, 8 + i:9 + i], prow[:, 64:64 + 1024], ALU.mult, ALU.mult),
                     reads=["xt%d" % i, "rstd%d" % i, "cst"], writes=["xt%d" % i])
            for i in range(md.NT):
                dst = D["ys"] if mn == "S" else D["yp"][k * 512 + i * 128:k * 512 + (i + 1) * 128, :]
                P.dma("act", "xt%d" % i, f_dma(dst, x_tok[:, i, :]), reads=["xt%d" % i])
        P.finalize(es)
    return nc


_NC_CACHE = {}


def kernel(**inputs):
    inp = {k: np.asarray(v) for k, v in inputs.items()}
    shared = host_shared(inp)
    in_maps = []
    for c in range(NCORES):
        m = host_layout(inp, c)
        m.update(shared)
        in_maps.append(m)
    if "nc" not in _NC_CACHE:
        _NC_CACHE["nc"] = build_nc()
    nc = _NC_CACHE["nc"]
    res = run_bass_kernel_spmd(nc, in_maps, core_ids=list(range(NCORES)))
    R = res.results
    f = np.float32
    y_prompt = np.stack([R[c]["yp"] for c in range(NCORES)]).astype(f)
    y_sample = np.concatenate([R[c]["ys"].reshape(NSEQ_S, LS, DM) for c in range(NCORES)]).astype(f)
    ssm_p = np.stack([R[c]["ssm_p"].reshape(DEPTH, 16, 64, 128) for c in range(NCORES)], axis=1).astype(f)
    ssm_s = np.concatenate([R[c]["ssm_s"].reshape(DEPTH, NSEQ_S, 16, 64, 128) for c in range(NCORES)], axis=1).astype(f)
    sc_p = np.stack([R[c]["sc_p"] for c in range(NCORES)], axis=1).astype(f)
    sc_s = np.concatenate([R[c]["sc_s"].reshape(DEPTH, NSEQ_S, 3, 1536) for c in range(NCORES)], axis=1).astype(f)
    cf_p = np.stack([R[c]["cf_p"] for c in range(NCORES)], axis=1).astype(f)
    cf_s = np.concatenate([R[c]["cf_s"].reshape(DEPTH, NSEQ_S, 30, 512) for c in range(NCORES)], axis=1).astype(f)
    pl_p = np.stack([R[c]["pl_p"] for c in range(NCORES)], axis=1).astype(f)
    pl_s = np.concatenate([R[c]["pl_s"].reshape(DEPTH, NSEQ_S, 15, 512) for c in range(NCORES)], axis=1).astype(f)
    return (y_prompt, y_sample, ssm_p, ssm_s, sc_p, sc_s, cf_p, cf_s, pl_p, pl_s)
```

```python
import contextlib
import numpy as np
import concourse.bass as bass
import concourse.mybir as mybir
from concourse.bass_utils import run_bass_kernel_spmd

F32 = mybir.dt.float32
BF16 = mybir.dt.bfloat16
AF = mybir.ActivationFunctionType
ALU = mybir.AluOpType

NCORES = 8
DEPTH = 2
DM = 1024
SEQ = 2048
NSEQ_S = 16
LS = 8
D_IN = 8208
NEG = -2000.0


class Res:
    __slots__ = ("name", "last_w", "readers")

    def __init__(self, name):
        self.name = name
        self.last_w = None
        self.readers = []


class Op:
    __slots__ = ("eng", "fn", "deps", "dma", "stream", "count", "needed")

    def __init__(self, eng, fn, dma=False, stream=None):
        self.eng = eng
        self.fn = fn
        self.deps = []
        self.dma = dma
        self.stream = stream
        self.count = 0
        self.needed = False


class Prog:
    ENGS = ("pe", "act", "dve", "pool", "sp")

    def __init__(self, nc):
        self.nc = nc
        self.ops = {e: [] for e in self.ENGS}
        self.all_ops = []
        self.res = {}

    def R(self, name):
        r = self.res.get(name)
        if r is None:
            r = self.res[name] = Res(name)
        return r

    def _track(self, op, reads, writes):
        deps = set()
        writes = list(writes) + [r for r in reads if r.startswith("ps") and r not in writes]
        reads = [r for r in reads if not r.startswith("ps")]
        reads = [self.R(r) for r in reads]
        writes = [self.R(w) for w in writes]
        for r in reads:
            if r.last_w is not None:
                deps.add(r.last_w)
        for w in writes:
            if w.last_w is not None:
                deps.add(w.last_w)
            deps.update(w.readers)
        for r in reads:
            r.readers.append(op)
        for w in writes:
            w.last_w = op
            w.readers = []
        deps.discard(op)
        op.deps = list(deps)

    def op(self, eng, fn, reads=(), writes=()):
        o = Op(eng, fn)
        self.all_ops.append(o)
        self.ops[eng].append(o)
        self._track(o, reads, writes)
        return o

    def dma(self, eng, stream, fn, reads=(), writes=()):
        o = Op(eng, fn, dma=True, stream=stream)
        self.all_ops.append(o)
        self.ops[eng].append(o)
        self._track(o, reads, writes)
        return o

    def finalize(self, es, final_wait_eng="sp"):
        nc = self.nc
        for o in self.all_ops:
            for d in o.deps:
                if d.eng == "pe" and o.eng == "pe" and not d.dma and not o.dma:
                    continue
                d.needed = True
        cnt = {e: 0 for e in self.ENGS}
        scnt = {}
        for o in self.all_ops:
            if o.dma:
                scnt[o.stream] = scnt.get(o.stream, 0) + 16
                o.count = scnt[o.stream]
            elif o.needed:
                cnt[o.eng] += 1
                o.count = cnt[o.eng]
        esem = {e: es.enter_context(nc.semaphore("s_" + e)) for e in ("pe", "act", "dve", "pool")}
        ssem = {s: es.enter_context(nc.semaphore("d_%d" % i)) for i, s in enumerate(scnt)}
        block = es.enter_context(nc.Block())
        engobj = {"pe": "tensor", "act": "scalar", "dve": "vector", "pool": "gpsimd", "sp": "sync"}

        def make(ename):
            ops = self.ops[ename]

            def body(eng):
                waited = {}
                for o in ops:
                    for d in o.deps:
                        if d.dma:
                            key, sem, val = ("s", d.stream), ssem[d.stream], d.count
                        else:
                            if d.eng == "pe" and ename == "pe" and not o.dma:
                                continue
                            key, sem, val = ("e", d.eng), esem[d.eng], d.count
                        if waited.get(key, 0) >= val:
                            continue
                        waited[key] = val
                        eng.wait_ge(sem, val)
                    ins = o.fn(eng)
                    if o.dma:
                        ins.then_inc(ssem[o.stream], 16)
                    elif o.needed:
                        ins.then_inc(esem[o.eng], 1)
                if ename == final_wait_eng:
                    for s, v in scnt.items():
                        eng.wait_ge(ssem[s], v)

            return body

        for ename in self.ENGS:
            getattr(block, engobj[ename])(make(ename))


def f_mm(out, lhsT, rhs, start=True, stop=True):
    return lambda e: e.matmul(out, lhsT, rhs, start=start, stop=stop)


def f_tr(out, in_, ident):
    return lambda e: e.transpose(out, in_, ident)


def f_act(out, in_, func, bias=None, scale=None, accum=None):
    kw = {}
    if bias is not None:
        kw["bias"] = bias
    if scale is not None:
        kw["scale"] = scale
    if accum is not None:
        kw["accum_out"] = accum
    return lambda e: e.activation(out=out, in_=in_, func=func, **kw)


def f_tt(out, in0, in1, op):
    return lambda e: e.tensor_tensor(out=out, in0=in0, in1=in1, op=op)


def f_ts(out, in0, s1, op0, s2=None, op1=None):
    if op1 is None:
        return lambda e: e.tensor_scalar(out=out, in0=in0, scalar1=s1, scalar2=None, op0=op0)
    return lambda e: e.tensor_scalar(out=out, in0=in0, scalar1=s1, scalar2=s2, op0=op0, op1=op1)


def f_stt(out, in0, scalar, in1, op0, op1):
    return lambda e: e.scalar_tensor_tensor(out=out, in0=in0, scalar=scalar, in1=in1, op0=op0, op1=op1)


def f_cp(out, in_):
    return lambda e: e.tensor_copy(out=out, in_=in_)


def f_ms(out, val):
    return lambda e: e.memset(out, val)


def f_dma(out, in_):
    return lambda e: e.dma_start(out=out, in_=in_)


PC_SCW, PC_SCB, PC_D, PC_SNW, PC_CFW, PC_CFB, PC_LNW, PC_LNB, PC_PSC, PC_NW, PC_DTB, PC_ALOG, PC_N = (
    0, 48, 60, 68, 76, 200, 204, 208, 212, 216, 224, 225, 226)
CF_ID, CF_TRIP, CF_TRIS, CF_SAMES, CF_ONES, CF_SEL, CF_RC, CF_EPS, CF_ONE, CF_L2, CF_PAIR, CF_N = (
    0, 128, 256, 384, 512, 640, 656, 720, 721, 722, 850, 858)
CB_ID, CB_ONES, CB_MNP, CB_MNS, CB_OH, CB_N = 0, 128, 256, 768, 1280, 1280 + 2048
NGRP = 8 + 8 + 2
GW = 5120

_Z0, _X0, _DT0, _GV0, _GG0, _GB0, _UP0, _GC0, _MA0, _MB0, _MC0 = (
    0, 1024, 2560, 2576, 3088, 3600, 4112, 4624, 5136, 6160, 7184)
BCH = ([("z", j, _Z0 + 128 * j) for j in range(8)] + [("x", j, _X0 + 128 * j) for j in range(12)]
       + sum([[("gv", i, _GV0 + 128 * i), ("gg", i, _GG0 + 128 * i)] for i in range(4)], [])
       + [("gb", i, _GB0 + 128 * i) for i in range(4)] + [("up", i, _UP0 + 128 * i) for i in range(4)]
       + [("gc", i, _GC0 + 128 * i) for i in range(4)])


def _kxm(w, c0, n):
    k = w.shape[0] // 128
    return w[:, c0:c0 + n].reshape(k, 128, n).transpose(1, 0, 2)


def host_layout(inp, core):
    f = np.float32
    m = {}
    m["xp"] = np.ascontiguousarray(inp["x_prompt"][core])
    sl = slice(core * NSEQ_S, (core + 1) * NSEQ_S)
    m["xs"] = np.ascontiguousarray(inp["x_sample"][sl].reshape(NSEQ_S * LS, DM))
    m["st_ssm"] = np.ascontiguousarray(inp["state_ssm"][:, sl].reshape(DEPTH, NSEQ_S, 1024, 128))
    m["st_sc"] = np.ascontiguousarray(inp["state_ssm_conv"][:, sl].reshape(DEPTH, NSEQ_S * 3, 1536))
    m["st_cf"] = np.ascontiguousarray(inp["state_cf_conv"][:, sl].reshape(DEPTH, NSEQ_S * 30, 512))
    m["st_pl"] = np.ascontiguousarray(inp["state_pool"][:, sl].reshape(DEPTH, NSEQ_S * 15, 512))
    return m


def host_shared(inp):
    f = np.float32
    wg = np.zeros((DEPTH, NGRP, 128, GW), f)
    wdt = np.zeros((DEPTH, 128, 128), f)
    wmix = np.zeros((DEPTH, 128, 512), f)
    pcol = np.zeros((DEPTH, 128, PC_N), f)
    for l in range(DEPTH):
        w_in = inp["w_in"][l]
        for g in range(8):
            for k in range(5):
                _, _, c0 = BCH[g * 5 + k]
                blk = _kxm(w_in, c0, 128)
                wg[l, g].reshape(128, 5, 8, 128)[:, k] = blk
        pa, pb, pc = inp["w_proj_a"][l], inp["w_proj_b"][l], inp["w_proj_c"][l]
        for j in range(8):
            t = wg[l, 8 + j]
            t[:, 0:1024] = _kxm(pa, 128 * j, 128).reshape(128, 1024)
            t[:, 1024:1536] = _kxm(pb, 128 * j, 128).reshape(128, 512)
            t[:, 1536:2048] = _kxm(pc, 128 * j, 128).reshape(128, 512)
            t[:, 2048:3072] = _kxm(w_in, _MA0 + 128 * j, 128).reshape(128, 1024)
            t[:, 3072:4096] = _kxm(w_in, _MB0 + 128 * j, 128).reshape(128, 1024)
            t[:, 4096:5120] = _kxm(w_in, _MC0 + 128 * j, 128).reshape(128, 1024)
        for hf in range(2):
            wg[l, 16 + hf][:, 0:4096] = _kxm(inp["w_out"][l], 512 * hf, 512).reshape(128, 4096)
        wdt[l] = _kxm(w_in, _DT0, 16).reshape(128, 128)
        wmix[l] = inp["pool_mix_w"][l].transpose(1, 0, 2).reshape(128, 512)
        p = pcol[l]
        p[:, PC_SCW:PC_SCW + 48] = inp["ssm_conv_w"][l].T.reshape(12, 128, 4).transpose(1, 0, 2).reshape(128, 48)
        p[:, PC_SCB:PC_SCB + 12] = inp["ssm_conv_b"][l].reshape(12, 128).T
        p[:, PC_D:PC_D + 8] = np.repeat(inp["ssm_d"][l], 64).reshape(8, 128).T
        p[:, PC_SNW:PC_SNW + 8] = inp["ssm_norm_w"][l].reshape(8, 128).T
        p[:, PC_CFW:PC_CFW + 124] = inp["cf_conv_w"][l].T.reshape(4, 128, 31).transpose(1, 0, 2).reshape(128, 124)
        p[:, PC_CFB:PC_CFB + 4] = inp["cf_conv_b"][l].reshape(4, 128).T
        p[:, PC_LNW:PC_LNW + 4] = inp["cf_ln_w"][l].reshape(4, 128).T
        p[:, PC_LNB:PC_LNB + 4] = inp["cf_ln_b"][l].reshape(4, 128).T
        p[:, PC_PSC:PC_PSC + 4] = inp["pool_scale"][l].reshape(4, 128).T
        p[:, PC_NW:PC_NW + 8] = inp["norm_w"][l].reshape(8, 128).T
    prow = np.zeros((128, 64 + 1024), f)
    for l in range(DEPTH):
        prow[:, 32 * l:32 * l + 16] = inp["ssm_dt_bias"][l][None, :]
        prow[:, 32 * l + 16:32 * l + 32] = inp["ssm_a_log"][l][None, :]
    prow[:, 64:] = inp["final_norm_w"][None, :]
    t = np.arange(128)
    seq = t // LS
    cf = np.zeros((128, CF_N), f)
    cf[:, CF_ID:CF_ID + 128] = np.eye(128)
    cf[:, CF_TRIP:CF_TRIP + 128] = (t[:, None] <= t[None, :])
    cf[:, CF_TRIS:CF_TRIS + 128] = (t[:, None] <= t[None, :]) & (seq[:, None] == seq[None, :])
    cf[:, CF_SAMES:CF_SAMES + 128] = (seq[:, None] == seq[None, :])
    cf[:, CF_ONES:CF_ONES + 128] = 1.0
    cf[:, CF_SEL:CF_SEL + 16] = (seq[:, None] == np.arange(16)[None, :])
    for g, w in enumerate((2, 4, 8, 16)):
        cf[:, CF_RC + 16 * g:CF_RC + 16 * g + 16] = (1.0 / np.minimum(np.arange(16) + 1, w))[None, :]
    cf[:, CF_EPS] = 1e-6
    cf[:, CF_ONE] = 1.0
    kk = np.arange(16)
    cf[0:16, CF_L2:CF_L2 + 128] = (kk[:, None] % 2 == (np.arange(128)[None, :] // 64))
    cf[0:16, CF_PAIR:CF_PAIR + 8] = (kk[:, None] // 2 == np.arange(8)[None, :])
    cb = np.zeros((128, CB_N), f)
    cb[:, CB_ID:CB_ID + 128] = np.eye(128)
    cb[:, CB_ONES:CB_ONES + 128] = 1.0
    mp = np.where(t[:, None] <= t[None, :], 0.0, NEG)
    ms = np.where((t[:, None] <= t[None, :]) & (seq[:, None] == seq[None, :]), 0.0, NEG)
    cb[:, CB_MNP:CB_MNP + 512] = np.tile(mp, (1, 4))
    cb[:, CB_MNS:CB_MNS + 512] = np.tile(ms, (1, 4))
    oh = (np.arange(16)[:, None] == seq[None, :]).astype(f)
    cb[:, CB_OH:CB_OH + 2048] = oh.reshape(1, 2048)
    return {"wg": wg, "wdt": wdt, "wmix": wmix, "pcol": pcol, "prow": prow, "cstf": cf, "cstb": cb}


class Mode:
    def __init__(self, name):
        self.name = name
        if name == "P":
            self.TB, self.NT, self.nseq, self.L = 512, 4, 1, 512
        else:
            self.TB, self.NT, self.nseq, self.L = 128, 1, NSEQ_S, LS


def build_nc():
    nc = bass.Bass("TRN2", target_bir_lowering=False)
    D = {}

    def din(name, shape):
        D[name] = nc.dram_tensor(name, list(shape), F32, kind="ExternalInput").ap()

    def dout(name, shape):
        D[name] = nc.dram_tensor(name, list(shape), F32, kind="ExternalOutput").ap()

    din("xp", [SEQ, DM]); din("xs", [128, DM])
    din("st_ssm", [DEPTH, NSEQ_S, 1024, 128]); din("st_sc", [DEPTH, 48, 1536])
    din("st_cf", [DEPTH, 480, 512]); din("st_pl", [DEPTH, 240, 512])
    din("wg", [DEPTH, NGRP, 128, GW]); din("wdt", [DEPTH, 128, 128]); din("wmix", [DEPTH, 128, 512])
    din("pcol", [DEPTH, 128, PC_N]); din("prow", [128, 64 + 1024])
    din("cstf", [128, CF_N]); din("cstb", [128, CB_N])
    dout("yp", [SEQ, DM]); dout("ys", [128, DM])
    dout("ssm_p", [DEPTH, 1024, 128]); dout("ssm_s", [DEPTH, NSEQ_S, 1024, 128])
    dout("sc_p", [DEPTH, 3, 1536]); dout("sc_s", [DEPTH, 48, 1536])
    dout("cf_p", [DEPTH, 30, 512]); dout("cf_s", [DEPTH, 480, 512])
    dout("pl_p", [DEPTH, 15, 512]); dout("pl_s", [DEPTH, 240, 512])

    P = Prog(nc)
    es = contextlib.ExitStack()
    with es:
        def sb(name, shape, dt=F32):
            return es.enter_context(nc.sbuf_tensor("sb_" + name, list(shape), dt))

        psum = es.enter_context(nc.psum_tensor("psum", [128, 8, 512], F32))
        ps_ctr = [0]

        pinned = set()

        def ps_get(pin=False):
            assert len(pinned) < 8
            while True:
                b = ps_ctr[0] % 8
                ps_ctr[0] += 1
                if b not in pinned:
                    break
            if pin:
                pinned.add(b)
            return b

        def psf(b):
            return psum[:, b, :]

        def psb(b):
            return psum[:, b, :].bitcast(BF16)

        def psn(b):
            return "ps%d" % b

        x_tok = sb("x_tok", [128, 4, 1024])
        xn_fm = sb("xn_fm", [128, 8, 512], BF16)
        zy = sb("zy", [128, 8, 512], BF16)
        xe = sb("xe", [128, 2, 528])
        xbc = sb("xbc", [128, 12, 512], BF16)
        ve = sb("ve", [128, 4, 608])
        ue = sb("ue", [128, 4, 528])
        gbs = sb("gbs", [128, 4, 512], BF16)
        gcs = sb("gcs", [128, 4, 512], BF16)
        wring = sb("wring", [128, 3, GW], BF16)
        wdt_sb = sb("wdt_sb", [128, 2, 128], BF16)
        wmix_sb = sb("wmix_sb", [128, 2, 512], BF16)
        halo_sc = sb("halo_sc", [128, 2, 36])
        halo_cf = sb("halo_cf", [128, 2, 120])
        halo_pl = sb("halo_pl", [128, 2, 60])
        stP = sb("stP", [128, 2, 1024])
        cstf = sb("cstf", [128, CF_N])
        cstb = sb("cstb", [128, CB_N], BF16)
        pcol = sb("pcol", [128, 2, PC_N])
        prow = sb("prow", [128, 64 + 1024])
        A_bc = sb("A_bc", [128, 2, 16])
        sm = sb("sm", [128, 16, 16])
        ss = sb("ss", [128, 16])
        dt_tok = sb("dt_tok", [128, 4, 16])
        a_tok = sb("a_tok", [128, 4, 16])
        FA = sb("FA", [128, 1024]); FB = sb("FB", [128, 1024]); FC = sb("FC", [128, 1024])
        FD = sb("FD", [128, 1024])
        HA = sb("HA", [128, 1024], BF16); HB = sb("HB", [128, 1024], BF16); HC = sb("HC", [128, 1024], BF16)
        HD = sb("HD", [128, 1024], BF16)
        XC = sb("XC", [128, 2048])
        DMb = sb("DMb", [128, 4096], BF16)
        hT = sb("hT", [128, 2, 1024], BF16)
        UNI = sb("UNI", [128, 4672])
        stgA = UNI[:, 0:2048]
        h0b = UNI[:, 2048:3072].bitcast(BF16)
        CBx = UNI[:, 3072:4096].bitcast(BF16).rearrange("p (a b) -> p a b", a=2)
        halo_scS = UNI[:, 4096:4672].rearrange("p (j e) -> p j e", j=12)
        ve_bf = UNI[:, 0:1088].bitcast(BF16)[:, 0:4 * 542].rearrange("p (i e) -> p i e", i=4)
        diag = UNI[:, 1088:1600].bitcast(BF16).rearrange("p (s m) -> p s m", s=8)
        cbf = UNI[:, 1600:2624].bitcast(BF16).rearrange("p (i t) -> p i t", i=4)
        csq = UNI[:, 2624:3136].bitcast(BF16).rearrange("p (s t) -> p s t", s=2)
        HA1 = UNI[:, 3136:3648].bitcast(BF16)
        HB1 = UNI[:, 3648:4160].bitcast(BF16)
        LT = UNI[:, 4160:4672]
        UNI_GUARD = ["stgA", "h0b", "Cx", "Bx"] + ["hscS%d" % j for j in range(12)] + ["ve%d" % i for i in range(4)] + ["YT0", "YT1", "FDb"]
        ve_flat = ve[:].rearrange("p a b -> p (a b)")
        YT0 = ve_flat[:, 0:512].bitcast(BF16)
        YT1 = ve_flat[:, 512:1024].bitcast(BF16)
        FDb = ve_flat[:, 1024:2048]
        ecend = sb("ecend", [128, 8, 16])
        rstdg = sb("rstdg", [128, 256])

        identF = cstf[:, CF_ID:CF_ID + 128]
        identB = cstb[:, CB_ID:CB_ID + 128]
        onesB = cstb[:, CB_ONES:CB_ONES + 128]
        onesF = cstf[:, CF_ONES:CF_ONES + 128]
        epsc = cstf[:, CF_EPS:CF_EPS + 1]
        onec = cstf[:, CF_ONE:CF_ONE + 1]

        P.dma("sp", "cst", f_dma(cstf[:], D["cstf"]), writes=["cst"])
        P.dma("sp", "cst", f_dma(prow[:], D["prow"]), writes=["cst"])
        for l in range(DEPTH):
            P.dma("sp", "cst", f_dma(pcol[:, l, :], D["pcol"][l]), writes=["cst"])
        P.dma("pool", "cstb", f_dma(cstb[:], D["cstb"]), writes=["cstb"])
        for l in range(DEPTH):
            P.dma("pool", "cstb", f_dma(wdt_sb[:, l, :], D["wdt"][l]), writes=["cstb"])
            P.dma("pool", "cstb", f_dma(wmix_sb[:, l, :], D["wmix"][l]), writes=["cstb"])
        for l in range(DEPTH):
            P.op("act", f_act(A_bc[:, l, :], prow[:, 32 * l + 16:32 * l + 32], AF.Exp), reads=["cst"], writes=["A_bc"])
            P.op("dve", f_ts(A_bc[:, l, :], A_bc[:, l, :], -1.0, ALU.mult), reads=["A_bc"], writes=["A_bc"])
        P.op("dve", f_ms(halo_sc[:], 0.0), writes=["halo_sc0", "halo_sc1"])
        P.op("dve", f_ms(halo_cf[:], 0.0), writes=["halo_cf0", "halo_cf1"])
        P.op("dve", f_ms(halo_pl[:], 0.0), writes=["halo_pl0", "halo_pl1"])
        P.op("dve", f_ms(stP[:], 0.0), writes=["stP0", "stP1"])

        wstate = {"n": 0}

        wsc = nc.dram_tensor("wsc", [DEPTH, NGRP, 128, GW], BF16, kind="Internal").ap()
        wstate["first"] = True

        def wload(l, g, ncols=GW):
            k = wstate["n"] % 3
            wstate["n"] += 1
            scn = "wsc_%d_%d" % (l, g)
            if wstate["first"]:
                P.dma("pool", "wp%d" % k, f_dma(wring[:, k, 0:ncols], D["wg"][l, g, :, 0:ncols]), writes=["w%d" % k])
                P.dma("sp", "wo%d" % k, f_dma(wsc[l, g, :, 0:ncols], wring[:, k, 0:ncols]), reads=["w%d" % k], writes=[scn])
            else:
                P.dma("sp", "w%d" % k, f_dma(wring[:, k, 0:ncols], wsc[l, g, :, 0:ncols]), reads=[scn], writes=["w%d" % k])
            return k

        def layer(md, l, blk_idx, first_block, last_block):
            TB, NT, nseq, L = md.TB, md.NT, md.nseq, md.L
            S = md.name == "S"
            pc = lambda c0, n=1: pcol[:, l, c0:c0 + n]
            TRI = cstf[:, CF_TRIS:CF_TRIS + 128] if S else cstf[:, CF_TRIP:CF_TRIP + 128]
            SAME = cstf[:, CF_SAMES:CF_SAMES + 128] if S else onesF
            MNEG = cstb[:, CB_MNS:CB_MNS + 512] if S else cstb[:, CB_MNP:CB_MNP + 512]
            SEL = cstf[:, CF_SEL:CF_SEL + 16] if S else cstf[:, CF_ONES:CF_ONES + 1]

            def tokv(ap):
                return ap.rearrange("p (b l) -> p b l", b=nseq)

            def extT(ap, E):
                return ap[:, 0:16 * E].rearrange("p (e b) -> p e b", b=16)

            def tok_ib(ap):
                return ap.rearrange("p (b i) -> p i b", b=16)

            def ib(ap):
                return ap.rearrange("p (i b) -> p i b", b=16)

            if S:
                P.dma("sp", "stgA", f_dma(stgA[0:48, 0:1536], D["st_sc"][l]), writes=["stgA"])
                for j in range(12):
                    b = ps_get()
                    P.op("pe", f_tr(psf(b)[:, 0:48], stgA[0:48, j * 128:(j + 1) * 128], identF[0:48, 0:48]),
                         reads=["stgA", "cst"], writes=[psn(b)])
                    P.op("act", f_act(halo_scS[:, j, :], psf(b)[:, 0:48], AF.Copy), reads=[psn(b)], writes=["hscS%d" % j])
                for q in range(4):
                    P.dma("sp", "stgA", f_dma(stgA[0:120, 0:512], D["st_cf"][l, q * 120:(q + 1) * 120, :]), writes=["stgA"])
                    for i in range(4):
                        b = ps_get()
                        P.op("pe", f_tr(psf(b)[:, 0:120], stgA[0:120, i * 128:(i + 1) * 128], identF[0:120, 0:120]),
                             reads=["stgA", "cst"], writes=[psn(b)])
                        dst = extT(ve[:, i, :], 38)[:, 0:30, 4 * q:4 * q + 4]
                        P.op("act", f_act(dst, psf(b)[:, 0:120].rearrange("p (b e) -> p e b", b=4), AF.Copy),
                             reads=[psn(b)], writes=["ve%d" % i])
                for q in range(2):
                    P.dma("sp", "stgA", f_dma(stgA[0:120, 0:512], D["st_pl"][l, q * 120:(q + 1) * 120, :]), writes=["stgA"])
                    for i in range(4):
                        b = ps_get()
                        P.op("pe", f_tr(psf(b)[:, 0:120], stgA[0:120, i * 128:(i + 1) * 128], identF[0:120, 0:120]),
                             reads=["stgA", "cst"], writes=[psn(b)])
                        dst = extT(ue[:, i, :], 23)[:, 0:15, 8 * q:8 * q + 8]
                        P.op("act", f_act(dst, psf(b)[:, 0:120].rearrange("p (b e) -> p e b", b=8), AF.Copy),
                             reads=[psn(b)], writes=["ue%d" % i])

            for i in range(NT):
                sl = i % 2
                P.op("dve", f_ms(ss[:, i:i + 1], 0.0), writes=["ss%d" % i])
                P.op("act", f_act((HA, HB)[sl][:], x_tok[:, i, :], AF.Square, accum=ss[:, i:i + 1]),
                     reads=["xt%d" % i], writes=[("HA", "HB")[sl], "ss%d" % i])
                P.op("act", f_act(ss[:, 4 + i:5 + i], ss[:, i:i + 1], AF.Ln, bias=epsc, scale=1.0 / DM),
                     reads=["ss%d" % i, "cst"], writes=["ssb%d" % i])
                P.op("act", f_act(ss[:, 8 + i:9 + i], ss[:, 4 + i:5 + i], AF.Exp, scale=-0.5),
                     reads=["ssb%d" % i], writes=["rstd%d" % i])
                P.op("dve", f_ts((HA, HB)[sl][:], x_tok[:, i, :], ss[:, 8 + i:9 + i], ALU.mult),
                     reads=["xt%d" % i, "rstd%d" % i], writes=[("HA", "HB")[sl]])
                b = ps_get()
                for dc in range(8):
                    P.op("pe", f_tr(psb(b)[:, dc * 128:(dc + 1) * 128], (HA, HB)[sl][:, dc * 128:(dc + 1) * 128], identB),
                         reads=[("HA", "HB")[sl], "cstb"], writes=[psn(b)])
                P.op("dve", f_tt(xn_fm[:, :, i * 128:(i + 1) * 128], psb(b).rearrange("p (d t) -> p d t", d=8),
                                 pc(PC_NW, 8).unsqueeze(2).broadcast_to([128, 8, 128]), ALU.mult),
                     reads=[psn(b), "cst"], writes=["xn_fm%d" % i])
            xnr = ["xn_fm%d" % i for i in range(NT)]

            def inproj(k, off, M=128):
                b = ps_get()
                for dc in range(8):
                    P.op("pe", f_mm(psf(b)[0:M, 0:TB], wring[:, k, off + dc * 128: off + dc * 128 + M], xn_fm[:, dc, 0:TB],
                                    start=dc == 0, stop=dc == 7), reads=["w%d" % k] + xnr, writes=[psn(b)])
                return b

            pend_val = {}
            for g in range(8):
                k = wload(l, g)
                for kk in range(5):
                    kind, j, _ = BCH[g * 5 + kk]
                    b = inproj(k, kk * 1024)
                    src = psf(b)[:, 0:TB]
                    if kind == "z":
                        P.op("act", f_act(zy[:, j, 0:TB], src, AF.Silu), reads=[psn(b)], writes=["zy%d" % j])
                    elif kind == "x":
                        s2 = j % 2
                        E = 3 + L
                        xevT = extT(xe[:, s2, :], 11)
                        xfl = xe[:, s2, :]
                        hsrc = halo_scS[:, j, :].rearrange("p (b r) -> p r b", b=16)
                        hres = "hscS%d" % j
                        P.op("dve", f_cp(xevT[:, 0:3, :], hsrc), reads=[hres], writes=["xe%d" % s2])
                        P.op("act", f_act(xevT[:, 3:11, :], tok_ib(src), AF.Copy), reads=[psn(b)], writes=["xe%d" % s2])
                        accb = (FC, FD)[s2]
                        accn = ("FC", "FD")[s2]
                        av = accb[:, 0:128]
                        P.op("dve", f_ts(av, xfl[:, 0:128], pc(PC_SCW + 4 * j), ALU.mult),
                             reads=["xe%d" % s2, "cst"], writes=[accn])
                        for t in range(1, 4):
                            P.op("dve", f_stt(av, xfl[:, t * 16:t * 16 + 128], pc(PC_SCW + 4 * j + t), av, ALU.mult, ALU.add),
                                 reads=["xe%d" % s2, accn, "cst"], writes=[accn])
                        P.op("act", f_act(tok_ib(xbc[:, j, 0:128]), ib(av), AF.Silu, bias=pc(PC_SCB + j)),
                             reads=[accn, "cst"], writes=["xbc%d" % j])
                        P.op("dve", f_cp(hsrc, xevT[:, 8:11, :]), reads=["xe%d" % s2], writes=[hres])
                    elif kind == "gv":
                        pend_val[j] = b
                    elif kind == "gg":
                        bv = pend_val.pop(j)
                        s2 = j % 2
                        sgt = (FA if s2 == 0 else FB)
                        sgn = "FA" if s2 == 0 else "FB"
                        P.op("act", f_act(sgt[:, 0:TB], src, AF.Sigmoid), reads=[psn(b)], writes=[sgn])
                        E = 30 + L
                        P.op("dve", f_tt(extT(ve[:, j, :], 38)[:, 30:38, :], tok_ib(psf(bv)[:, 0:128]), tok_ib(sgt[:, 0:128]), ALU.mult),
                             reads=[psn(bv), sgn], writes=["ve%d" % j])
                    elif kind == "gb":
                        P.op("act", f_act(gbs[:, j, 0:TB], src, AF.Silu), reads=[psn(b)], writes=["gbs%d" % j])
                    elif kind == "up":
                        E = 15 + L
                        P.op("act", f_act(extT(ue[:, j, :], 23)[:, 15:23, :], tok_ib(src), AF.Copy), reads=[psn(b)], writes=["ue%d" % j])
                    elif kind == "gc":
                        P.op("act", f_act(gcs[:, j, 0:TB], src, AF.Silu), reads=[psn(b)], writes=["gcs%d" % j])
                if g == 3:
                    b = ps_get()
                    for i in range(NT):
                        for dc in range(8):
                            P.op("pe", f_mm(psf(b)[:, i * 16:(i + 1) * 16], xn_fm[:, dc, i * 128:(i + 1) * 128],
                                            wdt_sb[:, l, dc * 16:(dc + 1) * 16], start=dc == 0, stop=dc == 7),
                                 reads=xnr + ["cstb"], writes=[psn(b)])
                    dtv = dt_tok[:, 0:NT, :]
                    P.op("dve", f_tt(dtv, psf(b)[:, 0:NT * 16].rearrange("p (i h) -> p i h", i=NT),
                                     prow[:, 32 * l:32 * l + 16].unsqueeze(1).broadcast_to([128, NT, 16]), ALU.add),
                         reads=[psn(b), "cst"], writes=["dt_tok"])
                    P.op("act", f_act(dtv, dtv, AF.Exp), reads=["dt_tok"], writes=["dt_tok"])
                    P.op("act", f_act(dtv, dtv, AF.Ln, bias=onec, scale=1.0), reads=["dt_tok", "cst"], writes=["dt_tok"])
                    P.op("dve", f_tt(a_tok[:, 0:NT, :], dtv, A_bc[:, l, :].unsqueeze(1).broadcast_to([128, NT, 16]), ALU.mult),
                         reads=["dt_tok", "A_bc"], writes=["a_tok"])

            decay = DMb[:, 0:2048].rearrange("p (h t) -> p h t", h=16)
            Mt = DMb[:, 2048:4096].rearrange("p (h t) -> p h t", h=16)
            for c in range(NT):
                tk = slice(c * 128, (c + 1) * 128)
                xsr = ["xbc%d" % j for j in range(8)]
                b1 = ps_get()
                for j in range(8):
                    P.op("pe", f_tr(psb(b1)[:, j * 128:(j + 1) * 128], xbc[:, j, tk], identB),
                         reads=["xbc%d" % j, "cstb"], writes=[psn(b1)])
                b2 = ps_get()
                for g in range(2):
                    P.op("pe", f_tr(psb(b2)[:, g * 128:(g + 1) * 128], xbc[:, 8 + g, tk], identB),
                         reads=["xbc%d" % (8 + g), "cstb"], writes=[psn(b2)])
                xdt = HA
                P.op("dve", f_tt(xdt[:].rearrange("p (h q) -> p h q", h=16), psb(b1).rearrange("p (h q) -> p h q", h=16),
                                 dt_tok[:, c, :].unsqueeze(2).broadcast_to([128, 16, 64]), ALU.mult),
                     reads=[psn(b1), "dt_tok"], writes=["HA"])
                Btok = HC
                P.op("act", f_act(Btok[:, 0:256], psb(b2)[:, 0:256], AF.Copy), reads=[psn(b2)], writes=["HC"])
                b3 = ps_get()
                P.op("pe", f_mm(psf(b3)[:, 0:16], TRI, a_tok[:, c, :]), reads=["a_tok", "cst"], writes=[psn(b3)])
                P.op("pe", f_mm(psf(b3)[:, 16:32], SAME, a_tok[:, c, :]), reads=["a_tok", "cst"], writes=[psn(b3)])
                P.op("pe", f_mm(psf(b3)[0:16, 128:256], a_tok[:, c, :], TRI), reads=["a_tok", "cst"], writes=[psn(b3)])
                cum, ncum, ecum, dd, dte = (sm[:, k, :] for k in range(5))
                P.op("dve", f_cp(cum, psf(b3)[:, 0:16]), reads=[psn(b3)], writes=["cum"])
                P.op("dve", f_ts(ncum, psf(b3)[:, 0:16], -1.0, ALU.mult), reads=[psn(b3)], writes=["ncum"])
                P.op("act", f_act(ecum, cum, AF.Exp), reads=["cum"], writes=["ecum"])
                P.op("dve", f_tt(dd, psf(b3)[:, 16:32], cum, ALU.subtract), reads=[psn(b3), "cum"], writes=["dd"])
                P.op("act", f_act(dte, dd, AF.Exp), reads=["dd"], writes=["dte"])
                xdtd = HB
                P.op("dve", f_tt(xdtd[:].rearrange("p (h q) -> p h q", h=16), xdt[:].rearrange("p (h q) -> p h q", h=16),
                                 dte.unsqueeze(2).broadcast_to([128, 16, 64]), ALU.mult),
                     reads=["HA", "dte"], writes=["HB"])
                P.op("dve", f_tt(XC[0:16, :].rearrange("p (h t) -> p h t", h=16),
                                 psf(b3)[0:16, 128:256].unsqueeze(1).broadcast_to([16, 16, 128]),
                                 identF[0:16, 0:16].unsqueeze(2).broadcast_to([16, 16, 128]), ALU.mult),
                     reads=[psn(b3), "cst"], writes=["XC"])
                b4 = ps_get()
                for g in range(2):
                    P.op("pe", f_mm(psf(b4)[:, g * 128:(g + 1) * 128], xbc[:, 8 + g, tk], xbc[:, 10 + g, tk]),
                         reads=["xbc%d" % (8 + g), "xbc%d" % (10 + g)], writes=[psn(b4)])
                for q in range(4):
                    bq = ps_get()
                    P.op("pe", f_mm(psf(bq), onesF[0:16, :], XC[0:16, q * 512:(q + 1) * 512], start=True, stop=False),
                         reads=["XC", "cst"], writes=[psn(bq)])
                    P.op("pe", f_mm(psf(bq), identB, MNEG, start=False, stop=True), reads=["cstb"], writes=[psn(bq)])
                    for hh in range(4):
                        h = 4 * q + hh
                        P.op("act", f_act(decay[:, h, :], psf(bq)[:, hh * 128:(hh + 1) * 128], AF.Exp, bias=ncum[:, h:h + 1]),
                             reads=[psn(bq), "ncum"], writes=["DMlo"])
                for g in range(2):
                    P.op("dve", f_tt(Mt[:, 8 * g:8 * g + 8, :], decay[:, 8 * g:8 * g + 8, :],
                                     psf(b4)[:, g * 128:(g + 1) * 128].unsqueeze(1).broadcast_to([128, 8, 128]), ALU.mult),
                         reads=["DMlo", psn(b4)], writes=["DMhi"])
                by = [ps_get(pin=True), ps_get(pin=True)]
                for h in range(16):
                    P.op("pe", f_mm(psf(by[h // 8])[:, (h % 8) * 64:(h % 8 + 1) * 64], Mt[:, h, :], xdt[:, h * 64:(h + 1) * 64]),
                         reads=["DMhi", "HA"], writes=[psn(by[h // 8])])
                a_x = FC
                P.op("dve", f_cp(a_x[:].rearrange("p (h q) -> p h q", h=16), a_tok[:, c, :].unsqueeze(2).broadcast_to([128, 16, 64])),
                     reads=["a_tok"], writes=["FC"])
                b5 = ps_get()
                for j in range(8):
                    P.op("pe", f_mm(psf(b5)[:, j * nseq:(j + 1) * nseq], a_x[:, j * 128:(j + 1) * 128], SEL),
                         reads=["FC", "cst"], writes=[psn(b5)])
                P.op("act", f_act(ecend[:, :, 0:nseq], psf(b5)[:, 0:8 * nseq].rearrange("p (j b) -> p j b", j=8), AF.Exp),
                     reads=[psn(b5)], writes=["ecend"])
                bo = [ps_get(pin=True), ps_get(pin=True)]
                if not S:
                    st = stP[:, l, :]
                    stn = "stP%d" % l
                    stb = HD
                    P.op("act", f_act(stb[:], st, AF.Copy), reads=[stn], writes=["HD"])
                    b6 = ps_get()
                    for j in range(8):
                        P.op("pe", f_tr(psb(b6)[:, j * 128:(j + 1) * 128], stb[:, j * 128:(j + 1) * 128], identB),
                             reads=["HD", "cstb"], writes=[psn(b6)])
                    P.op("act", f_act(hT[:, 0, :], psb(b6), AF.Copy), reads=[psn(b6)], writes=["hT0"])
                    for g in range(2):
                        P.op("pe", f_mm(psf(bo[g]), xbc[:, 10 + g, tk], hT[:, 0, g * 512:(g + 1) * 512]),
                             reads=["xbc%d" % (10 + g), "hT0"], writes=[psn(bo[g])])
                    bs = [ps_get(), ps_get()]
                    for j in range(8):
                        P.op("pe", f_mm(psf(bs[j // 4])[:, (j % 4) * 128:(j % 4 + 1) * 128], xdtd[:, j * 128:(j + 1) * 128],
                                        Btok[:, (j // 4) * 128:(j // 4 + 1) * 128]),
                             reads=["HB", "HC"], writes=[psn(bs[j // 4])])
                    for j in range(8):
                        P.op("dve", f_stt(st[:, j * 128:(j + 1) * 128], st[:, j * 128:(j + 1) * 128], ecend[:, j, 0:1],
                                          psf(bs[j // 4])[:, (j % 4) * 128:(j % 4 + 1) * 128], ALU.mult, ALU.add),
                             reads=[stn, "ecend", psn(bs[j // 4])], writes=[stn])
                else:
                    OH = cstb[:, CB_OH:CB_OH + 2048].rearrange("p (b t) -> p b t", b=16)
                    SELb = cstf[:, CF_SEL:CF_SEL + 16]
                    P.op("dve", f_ms(ss[:, 14:15], 0.0), writes=["stgA", "XC", "h0b", "Cx", "Bx", "Q0", "Q1", "Q2", "Q3", "h0b0", "h0b1"] + ["cbx%d" % k_ for k_ in range(4)])
                    Qs = [(stgA[:, 0:1024], "Q0"), (stgA[:, 1024:2048], "Q1"), (XC[:, 0:1024], "Q2"), (XC[:, 1024:2048], "Q3")]
                    CBf = CBx.rearrange("p a b -> p (a b)")
                    def q_load(sq_):
                        Q_, Qn_ = Qs[sq_ % 4]
                        P.dma("sp", Qn_, f_dma(Q_.rearrange("p (j n) -> p j n", j=8),
                                               D["st_ssm"][l, sq_].rearrange("(j p) n -> p j n", p=128)), writes=[Qn_])

                    for sq in range(4):
                        q_load(sq)
                    for sq in range(NSEQ_S):
                        Q, Qn = Qs[sq % 4]
                        s2 = sq % 2
                        hb_, hbn = h0b[:, s2 * 1024:(s2 + 1) * 1024], "h0b%d" % s2
                        cb_, cbn = CBf[:, (sq % 4) * 512:(sq % 4 + 1) * 512], "cbx%d" % (sq % 4)
                        Q3 = Q.rearrange("p (j n) -> p j n", j=8)
                        P.op("act", f_act(hb_, Q, AF.Copy), reads=[Qn], writes=[hbn])
                        for g in range(2):
                            P.op("pool", f_tt(cb_[:, g * 128:(g + 1) * 128], xbc[:, 10 + g, 0:128], OH[:, sq, :], ALU.mult),
                                 reads=["xbc%d" % (10 + g), "cstb"], writes=[cbn])
                            P.op("pool", f_ts(cb_[:, 256 + g * 128:256 + (g + 1) * 128], Btok[:, g * 128:(g + 1) * 128], SELb[:, sq:sq + 1], ALU.mult),
                                 reads=["HC", "cst"], writes=[cbn])
                        b6 = ps_get()
                        for j in range(8):
                            P.op("pe", f_tr(psb(b6)[:, j * 128:(j + 1) * 128], hb_[:, j * 128:(j + 1) * 128], identB),
                                 reads=[hbn, "cstb"], writes=[psn(b6)])
                        P.op("act", f_act(hT[:, s2, :], psb(b6), AF.Copy), reads=[psn(b6)], writes=["hT%d" % s2])
                        for g in range(2):
                            P.op("pe", f_mm(psf(bo[g]), cb_[:, g * 128:(g + 1) * 128], hT[:, s2, g * 512:(g + 1) * 512],
                                            start=(sq == 0), stop=(sq == NSEQ_S - 1)),
                                 reads=[cbn, "hT%d" % s2], writes=[psn(bo[g])])
                        P.op("dve", f_tt(Q3, Q3, ecend[:, :, sq:sq + 1].broadcast_to([128, 8, 128]), ALU.mult),
                             reads=[Qn, "ecend", hbn], writes=[Qn])
                        bs_ = [ps_get(), ps_get()]
                        for j in range(8):
                            g = j // 4
                            P.op("pe", f_mm(psf(bs_[g])[:, (j % 4) * 128:(j % 4 + 1) * 128], xdtd[:, j * 128:(j + 1) * 128],
                                            cb_[:, 256 + g * 128:256 + (g + 1) * 128]),
                                 reads=["HB", cbn], writes=[psn(bs_[g])])
                        for g in range(2):
                            P.op("dve", f_tt(Q[:, g * 512:(g + 1) * 512], Q[:, g * 512:(g + 1) * 512], psf(bs_[g]), ALU.add),
                                 reads=[Qn, psn(bs_[g])], writes=[Qn])
                        P.dma("sp", Qn, f_dma(D["ssm_s"][l, sq].rearrange("(j p) n -> p j n", p=128), Q3), reads=[Qn])
                        if sq + 4 < NSEQ_S:
                            q_load(sq + 4)
                    P.op("dve", f_ms(ss[:, 14:15], 0.0), writes=["stgA", "XC", "h0b", "Cx", "Bx", "Q0", "Q1", "Q2", "Q3", "h0b0", "h0b1"] + ["cbx%d" % k_ for k_ in range(4)])
                t1 = FD
                ytok = HD
                for g in range(2):
                    P.op("dve", f_tt(t1[:, g * 512:(g + 1) * 512].rearrange("p (h q) -> p h q", h=8),
                                     psf(bo[g]).rearrange("p (h q) -> p h q", h=8),
                                     ecum[:, 8 * g:8 * g + 8].unsqueeze(2).broadcast_to([128, 8, 64]), ALU.mult),
                         reads=[psn(bo[g]), "ecum"], writes=["FD"])
                    P.op("dve", f_tt(ytok[:, g * 512:(g + 1) * 512], t1[:, g * 512:(g + 1) * 512], psf(by[g]), ALU.add),
                         reads=["FD", psn(by[g])], writes=["HD"])
                for b_ in by + bo:
                    pinned.discard(b_)
                b7 = ps_get()
                for j in range(8):
                    P.op("pe", f_tr(psb(b7)[:, j * 128:(j + 1) * 128], ytok[:, j * 128:(j + 1) * 128], identB),
                         reads=["HD", "cstb"], writes=[psn(b7)])
                dxs = FC
                P.op("dve", f_tt(dxs[:].rearrange("p (j t) -> p j t", j=8), xbc[:, 0:8, tk],
                                  pc(PC_D, 8).unsqueeze(2).broadcast_to([128, 8, 128]), ALU.mult),
                     reads=xsr + ["cst"], writes=["FC"])
                P.op("dve", f_tt(dxs[:], dxs[:], psb(b7), ALU.add), reads=["FC", psn(b7)], writes=["FC"])
                zyr = ["zy%d" % j for j in range(8)]
                yg = FD
                P.op("dve", f_tt(yg[:].rearrange("p (j t) -> p j t", j=8), dxs[:].rearrange("p (j t) -> p j t", j=8),
                                 zy[:, :, tk], ALU.mult), reads=["FC"] + zyr, writes=["FD"])
                ygsq = HA
                P.op("act", f_act(ygsq[:], yg[:], AF.Square), reads=["FD"], writes=["HA"])
                b8 = ps_get()
                for g in range(2):
                    for jj in range(4):
                        P.op("pe", f_mm(psf(b8)[:, g * 128:(g + 1) * 128], onesB, ygsq[:, (4 * g + jj) * 128:(4 * g + jj + 1) * 128],
                                        start=jj == 0, stop=jj == 3), reads=["HA", "cstb"], writes=[psn(b8)])
                P.op("act", f_act(rstdg[:], psf(b8)[:, 0:256], AF.Ln, bias=epsc, scale=1.0 / 512), reads=[psn(b8), "cst"], writes=["rstdg"])
                P.op("act", f_act(rstdg[:], rstdg[:], AF.Exp, scale=-0.5), reads=["rstdg"], writes=["rstdg"])
                for j in range(8):
                    P.op("dve", f_stt(zy[:, j, tk], yg[:, j * 128:(j + 1) * 128], pc(PC_SNW + j),
                                      rstdg[:, (j // 4) * 128:(j // 4 + 1) * 128], ALU.mult, ALU.mult),
                         reads=["FD", "rstdg", "cst"], writes=["zy%d" % j])

            cacc = XC
            for i in range(4):
                E = 30 + L
                cv = cacc[:, i * 512:i * 512 + 128]
                rn = "XC"
                vbfS = UNI[:, 0:1216].bitcast(BF16).rearrange("p (i e) -> p i e", i=4)
                dgS = UNI[:, 2048:2560].bitcast(BF16).rearrange("p (s m) -> p s m", s=8)
                P.op("act", f_act(vbfS[:, i, :], ve[:, i, :], AF.Copy), reads=["ve%d" % i], writes=["stgA"])
                bcv = ps_get()
                for t0 in range(0, 31, 4):
                    nt_ = min(4, 31 - t0)
                    hs = (t0 // 4) % 2
                    P.op("dve", f_tt(dgS[:, 4 * hs:4 * hs + nt_, :], identB.unsqueeze(1).broadcast_to([128, nt_, 128]),
                                     pc(PC_CFW + 31 * i + t0, nt_).unsqueeze(2).broadcast_to([128, nt_, 128]), ALU.mult),
                         reads=["cstb", "cst"], writes=["h0b%d" % hs])
                    for t in range(t0, t0 + nt_):
                        P.op("pe", f_mm(psf(bcv)[:, 0:128], dgS[:, 4 * hs + t - t0, :], vbfS[:, i, t * 16:t * 16 + 128], start=t == 0, stop=t == 30),
                             reads=["h0b%d" % hs, "stgA"], writes=[psn(bcv)])
                P.op("act", f_act(cv, psf(bcv)[:, 0:128], AF.Copy), reads=[psn(bcv)], writes=[rn])
            bm, bq_ = ps_get(), ps_get()
            for i in range(4):
                cfl = cacc[:, i * 512:i * 512 + TB]
                s2 = i % 2
                hb = HA if s2 == 0 else HB
                hq = HC if s2 == 0 else HD
                P.op("act", f_act(hb[:, 0:TB], cfl, AF.Identity, bias=pc(PC_CFB + i)), reads=["XC", "cst"], writes=["HA" if s2 == 0 else "HB"])
                P.op("act", f_act(hq[:, 0:TB], cfl, AF.Square, bias=pc(PC_CFB + i)), reads=["XC", "cst"], writes=["HC" if s2 == 0 else "HD"])
                P.op("pe", f_mm(psf(bm)[:, 0:TB], onesB, hb[:, 0:TB], start=i == 0, stop=i == 3),
                     reads=["HA" if s2 == 0 else "HB", "cstb"], writes=[psn(bm)])
                P.op("pe", f_mm(psf(bq_)[:, 0:TB], onesB, hq[:, 0:TB], start=i == 0, stop=i == 3),
                     reads=["HC" if s2 == 0 else "HD", "cstb"], writes=[psn(bq_)])
            mean = FA
            m2 = FB
            P.op("dve", f_ts(mean[:, 0:TB], psf(bm)[:, 0:TB], 1.0 / 512, ALU.mult), reads=[psn(bm)], writes=["FA"])
            P.op("dve", f_tt(m2[:, 0:TB], mean[:, 0:TB], mean[:, 0:TB], ALU.mult), reads=["FA"], writes=["FB"])
            P.op("dve", f_stt(m2[:, 0:TB], psf(bq_)[:, 0:TB], 1.0 / 512, m2[:, 0:TB], ALU.mult, ALU.subtract),
                 reads=[psn(bq_), "FB"], writes=["FB"])
            P.op("act", f_act(m2[:, 0:TB], m2[:, 0:TB], AF.Ln, bias=epsc, scale=1.0), reads=["FB", "cst"], writes=["FB"])
            P.op("act", f_act(m2[:, 0:TB], m2[:, 0:TB], AF.Exp, scale=-0.5), reads=["FB"], writes=["FB"])
            for i in range(4):
                cfl = cacc[:, i * 512:i * 512 + TB]
                P.op("dve", f_stt(cfl, cfl, pc(PC_CFB + i), mean[:, 0:TB], ALU.add, ALU.subtract), reads=["XC", "FA", "cst"], writes=["XC"])
                P.op("dve", f_tt(cfl, cfl, m2[:, 0:TB], ALU.mult), reads=["XC", "FB"], writes=["XC"])
                P.op("act", f_act(FC[:, 0:TB], cfl, AF.Silu, bias=pc(PC_LNB + i), scale=pc(PC_LNW + i)), reads=["XC", "cst"], writes=["FC"])
                P.op("dve", f_tt(tokv(gbs[:, i, 0:TB]), FC[:, 0:TB].rearrange("p (i b) -> p b i", b=16), tokv(gbs[:, i, 0:TB]), ALU.mult),
                     reads=["FC", "gbs%d" % i], writes=["gbs%d" % i])

            for g in range(4):
                w = 2 ** (g + 1)
                E = 15 + L
                ufl = ue[:, g, :]
                cur, curn = ufl, "ue%d" % g
                off = 0
                bufs = [(FA, "FA"), (FB, "FB")]
                for si, d in enumerate([1, 2, 4, 8][:g + 1]):
                    nb, nbn = bufs[si % 2]
                    P.op("dve", f_tt(nb[:, (off + d) * 16:E * 16], cur[:, (off + d) * 16:E * 16], cur[:, off * 16:(E - d) * 16], ALU.add),
                         reads=[curn], writes=[nbn])
                    cur, curn = nb, nbn
                    off += d
                pl = HA if g % 2 == 0 else HB
                pln = "HA" if g % 2 == 0 else "HB"
                P.op("dve", f_stt(pl[:, 0:128], cur[:, 15 * 16:23 * 16], 1.0 / w, ufl[:, 15 * 16:23 * 16], ALU.mult, ALU.subtract),
                     reads=[curn, "ue%d" % g], writes=[pln])
                if (not S) and first_block:
                    tmp = sm[:, 8, :]
                    P.op("dve", f_tt(tmp, cur[:, 0, 15:31], cstf[:, CF_RC + 16 * g:CF_RC + 16 * g + 16], ALU.mult),
                         reads=[curn, "cst"], writes=["tmp16"])
                    P.op("dve", f_tt(pl[:, 0:16], tmp, uev[:, 0, 15:31], ALU.subtract), reads=["tmp16", "ue%d" % g, pln], writes=[pln])
                if not S:
                    P.op("dve", f_cp(halo_pl[:, l, g * 15:(g + 1) * 15].rearrange("p (b e) -> p b e", b=1), uev[:, :, L:L + 15]),
                         reads=["ue%d" % g], writes=["halo_pl%d" % l])
                b = ps_get()
                P.op("pe", f_mm(psf(b)[:, 0:TB], wmix_sb[:, l, g * 128:(g + 1) * 128], pl[:, 0:TB]), reads=[pln, "cstb"], writes=[psn(b)])
                P.op("dve", f_stt(tokv(gcs[:, g, 0:TB]), psf(b)[:, 0:TB].rearrange("p (i b) -> p b i", b=16), pc(PC_PSC + g),
                                  tokv(gcs[:, g, 0:TB]), ALU.mult, ALU.mult),
                     reads=[psn(b), "gcs%d" % g, "cst"], writes=["gcs%d" % g])

            def rows_out(src_fn, nchunk, nrows, dst, stg, stgn):
                for i in range(nchunk):
                    b = ps_get()
                    sap, srn = src_fn(i)
                    if len(sap.shape) > 2:
                        fb, fbn = ((FA, "FA"), (FB, "FB"))[i % 2]
                        P.op("dve", f_cp(fb[:, 0:nrows].rearrange("p (b e) -> p b e", b=sap.shape[1]), sap), reads=[srn], writes=[fbn])
                        sap, srn = fb[:, 0:nrows], fbn
                    P.op("pe", f_tr(psf(b)[0:nrows, 0:128], sap, identF), reads=[srn, "cst"], writes=[psn(b)])
                    P.op("act", f_act(stg[0:nrows, i * 128:(i + 1) * 128], psf(b)[0:nrows, 0:128], AF.Copy), reads=[psn(b)], writes=[stgn])
                P.dma("sp", stgn, f_dma(dst, stg[0:nrows, 0:nchunk * 128]), reads=[stgn])

            zyr = ["zy%d" % j for j in range(8)]
            gbr = ["gbs%d" % j for j in range(4)]
            gcr = ["gcs%d" % j for j in range(4)]
            merged = DMb[:].rearrange("p (j t) -> p j t", j=8)
            for j in range(8):
                k = wload(l, 8 + j)
                wk = "w%d" % k
                mres = "DMlo" if j < 4 else "DMhi"
                prods = []
                for br, (koff, goff, nk, src, srcr) in enumerate(((0, 2048, 8, zy, zyr), (1024, 3072, 4, gbs, gbr), (1536, 4096, 4, gcs, gcr))):
                    bo_ = ps_get()
                    for kc in range(nk):
                        P.op("pe", f_mm(psf(bo_)[:, 0:TB], wring[:, k, koff + kc * 128:koff + (kc + 1) * 128], src[:, kc, 0:TB],
                                        start=kc == 0, stop=kc == nk - 1), reads=[wk] + srcr, writes=[psn(bo_)])
                    bg_ = inproj(k, goff)
                    sgt, sgn = ((FA, "FA"), (FB, "FB"), (FC, "FC"))[br]
                    P.op("act", f_act(sgt[:, 0:TB], psf(bg_)[:, 0:TB], AF.Sigmoid), reads=[psn(bg_)], writes=[sgn])
                    P.op("dve", f_tt(sgt[:, 0:TB], sgt[:, 0:TB], psf(bo_)[:, 0:TB], ALU.mult), reads=[sgn, psn(bo_)], writes=[sgn])
                P.op("dve", f_tt(FA[:, 0:TB], FA[:, 0:TB], FB[:, 0:TB], ALU.add), reads=["FA", "FB"], writes=["FA"])
                P.op("dve", f_tt(merged[:, j, 0:TB], FA[:, 0:TB], FC[:, 0:TB], ALU.add), reads=["FA", "FC"], writes=[mres])
            for hf in range(2):
                k = wload(l, 16 + hf, 4096)
                for i in range(NT):
                    b = ps_get()
                    for dc in range(8):
                        P.op("pe", f_mm(psf(b), merged[:, dc, i * 128:(i + 1) * 128], wring[:, k, dc * 512:(dc + 1) * 512],
                                        start=dc == 0, stop=dc == 7), reads=["w%d" % k, "DMlo", "DMhi"], writes=[psn(b)])
                    xv = x_tok[:, i, hf * 512:(hf + 1) * 512]
                    P.op("dve", f_tt(xv, xv, psf(b), ALU.add), reads=["xt%d" % i, psn(b)], writes=["xt%d" % i])

            if S:
                rows_out(lambda j: (halo_scS[:, j, :], "hscS%d" % j), 12, 48, D["sc_s"][l], XC, "XC")
                for q in range(4):
                    rows_out(lambda i, q=q: (extT(ve[:, i, :], 38)[:, 8:38, 4 * q:4 * q + 4].rearrange("p e b -> p b e"), "ve%d" % i),
                             4, 120, D["cf_s"][l, q * 120:(q + 1) * 120, :], XC, "XC")
                for q in range(2):
                    rows_out(lambda i, q=q: (extT(ue[:, i, :], 23)[:, 8:23, 8 * q:8 * q + 8].rearrange("p e b -> p b e"), "ue%d" % i),
                             4, 120, D["pl_s"][l, q * 120:(q + 1) * 120, :], XC, "XC")
            elif last_block:
                rows_out(lambda j: (halo_sc[:, l, j * 3:(j + 1) * 3], "halo_sc%d" % l), 12, 3, D["sc_p"][l], XC, "XC")
                rows_out(lambda i: (halo_cf[:, l, i * 30:(i + 1) * 30], "halo_cf%d" % l), 4, 30, D["cf_p"][l], XC, "XC")
                rows_out(lambda i: (halo_pl[:, l, i * 15:(i + 1) * 15], "halo_pl%d" % l), 4, 15, D["pl_p"][l], XC, "XC")
                P.dma("sp", "stP%d" % l, f_dma(D["ssm_p"][l].rearrange("(j p) n -> p j n", p=128),
                                               stP[:, l, :].rearrange("p (j n) -> p j n", j=8)), reads=["stP%d" % l])


        def interleave(gens):
            gens = list(gens)
            while gens:
                for g_ in list(gens):
                    try:
                        next(g_)
                    except StopIteration:
                        gens.remove(g_)
                yield

        def layer_P(l, k_blk, first_block, last_block):
            TB, NT, L = 512, 4, 512
            guard = []
            pc = lambda c0, n=1: pcol[:, l, c0:c0 + n]
            TRI = cstf[:, CF_TRIP:CF_TRIP + 128]
            MNEG = cstb[:, CB_MNP:CB_MNP + 512]
            L2c = cstf[0:16, CF_L2:CF_L2 + 128]
            PAIR = cstf[0:16, CF_PAIR:CF_PAIR + 8]

            xts = [(HA, "HA"), (HB, "HB"), (HC, "HCx"), (HD, "HD")]
            for i in range(NT):
                xt_, xtn = xts[i]
                P.op("dve", f_ms(ss[:, i:i + 1], 0.0), writes=["ss%d" % i])
                P.op("act", f_act(xt_[:], x_tok[:, i, :], AF.Square, accum=ss[:, i:i + 1]),
                     reads=["xt%d" % i], writes=[xtn, "ss%d" % i] + (["HC0", "HC1"] if i == 2 else []))
            for i in range(NT):
                P.op("act", f_act(ss[:, 4 + i:5 + i], ss[:, i:i + 1], AF.Ln, bias=epsc, scale=1.0 / DM),
                     reads=["ss%d" % i, "cst"], writes=["ssb%d" % i])
            for i in range(NT):
                P.op("act", f_act(ss[:, 8 + i:9 + i], ss[:, 4 + i:5 + i], AF.Exp, scale=-0.5),
                     reads=["ssb%d" % i], writes=["rstd%d" % i])
            for i in range(NT):
                xt_, xtn = xts[i]
                P.op("dve", f_ts(xt_[:], x_tok[:, i, :], ss[:, 8 + i:9 + i], ALU.mult),
                     reads=["xt%d" % i, "rstd%d" % i], writes=[xtn])
            pbs = []
            for i in range(NT):
                xt_, xtn = xts[i]
                b = ps_get()
                pbs.append(b)
                for dc in range(8):
                    P.op("pe", f_tr(psb(b)[:, dc * 128:(dc + 1) * 128], xt_[:, dc * 128:(dc + 1) * 128], identB),
                         reads=[xtn, "cstb"], writes=[psn(b)])
            for i in range(NT):
                b = pbs[i]
                P.op("dve", f_tt(xn_fm[:, :, i * 128:(i + 1) * 128], psb(b).rearrange("p (d t) -> p d t", d=8),
                                 pc(PC_NW, 8).unsqueeze(2).broadcast_to([128, 8, 128]), ALU.mult),
                     reads=[psn(b), "cst"], writes=["xn_fm%d" % i])
            xnr = ["xn_fm%d" % i for i in range(NT)]

            def inproj(k, off, pin=False):
                b = ps_get(pin=pin)
                for dc in range(8):
                    P.op("pe", f_mm(psf(b), wring[:, k, off + dc * 128: off + dc * 128 + 128], xn_fm[:, dc, :],
                                    start=dc == 0, stop=dc == 7), reads=["w%d" % k] + xnr, writes=[psn(b)])
                return b

            pend_val = {}
            pend_x = []
            xctr = [0]

            def bchunk(k, g, kk):
                kind, j, _ = BCH[g * 5 + kk]
                b = inproj(k, kk * 1024, pin=(kind == "gv"))
                src = psf(b)
                while pend_x:
                    pend_x.pop(0)()
                if kind == "z":
                    P.op("act", f_act(zy[:, j, :], src, AF.Silu), reads=[psn(b)], writes=["zy%d" % j])
                elif kind == "x":
                    s2 = j % 2
                    xeb = xe[:, s2, :].bitcast(BF16)[:, 0:515]
                    hsrc = halo_sc[:, l, j * 3:(j + 1) * 3]
                    hres = "halo_sc%d" % l
                    P.op("dve", f_cp(xeb[:, 0:3], hsrc), reads=[hres], writes=["xe%d" % s2])
                    P.op("act", f_act(xeb[:, 3:515], src, AF.Copy), reads=[psn(b)], writes=["xe%d" % s2])
                    P.op("act", f_act(hsrc, src[:, 509:512], AF.Copy), reads=[psn(b)], writes=[hres])
                    hs = xctr[0] % 2
                    xctr[0] += 1
                    P.op("dve", f_tt(diag[:, 4 * hs:4 * hs + 4, :], identB.unsqueeze(1).broadcast_to([128, 4, 128]),
                                     pc(PC_SCW + 4 * j, 4).unsqueeze(2).broadcast_to([128, 4, 128]), ALU.mult),
                         reads=["cstb", "cst"], writes=["diagh%d" % hs])

                    def part2(j=j, hs=hs, s2=s2, xeb=xeb):
                        b2_ = ps_get()
                        for t in range(4):
                            P.op("pe", f_mm(psf(b2_), diag[:, 4 * hs + t, :], xeb[:, t:t + 512], start=t == 0, stop=t == 3),
                                 reads=["diagh%d" % hs, "xe%d" % s2], writes=[psn(b2_)])
                        P.op("act", f_act(xbc[:, j, :], psf(b2_), AF.Silu, bias=pc(PC_SCB + j)), reads=[psn(b2_), "cst"], writes=["xbc%d" % j])

                    pend_x.append(part2)
                elif kind == "gv":
                    pend_val[j] = b
                elif kind == "gg":
                    bv = pend_val.pop(j)
                    s2 = j % 2
                    sgt, sgn = (FA, FB)[s2], ("FA", "FB")[s2]
                    P.op("act", f_act(sgt[:, 0:512], src, AF.Sigmoid), reads=[psn(b)], writes=[sgn])
                    hcf = halo_cf[:, l, j * 30:(j + 1) * 30]
                    P.op("dve", f_cp(ve_bf[:, j, 0:30], hcf), reads=["halo_cf%d" % l], writes=["vebf%d" % j])
                    P.op("dve", f_tt(ve_bf[:, j, 30:542], psf(bv), sgt[:, 0:512], ALU.mult), reads=[psn(bv), sgn], writes=["vebf%d" % j])
                    P.op("dve", f_tt(hcf, psf(bv)[:, 482:512], sgt[:, 482:512], ALU.mult), reads=[psn(bv), sgn], writes=["halo_cf%d" % l])
                    pinned.discard(bv)
                elif kind == "gb":
                    P.op("act", f_act(gbs[:, j, :], src, AF.Silu), reads=[psn(b)], writes=["gbs%d" % j])
                elif kind == "up":
                    uev = ue[:, j, 0:527]
                    P.op("dve", f_cp(uev[:, 0:15], halo_pl[:, l, j * 15:(j + 1) * 15]), reads=["halo_pl%d" % l], writes=["ue%d" % j])
                    P.op("act", f_act(uev[:, 15:527], src, AF.Copy), reads=[psn(b)], writes=["ue%d" % j])
                    P.op("dve", f_cp(halo_pl[:, l, j * 15:(j + 1) * 15], uev[:, 512:527]), reads=["ue%d" % j], writes=["halo_pl%d" % l])
                elif kind == "gc":
                    P.op("act", f_act(gcs[:, j, :], src, AF.Silu), reads=[psn(b)], writes=["gcs%d" % j])

            deferred_up = []
            for g in range(8):
                k = wload(l, g)
                for kk in range(5):
                    if BCH[g * 5 + kk][0] == "up":
                        deferred_up.append((k, g, kk))
                    else:
                        bchunk(k, g, kk)
            while pend_x:
                pend_x.pop(0)()
            b = ps_get()
            for i in range(NT):
                for dc in range(8):
                    P.op("pe", f_mm(psf(b)[:, i * 16:(i + 1) * 16], xn_fm[:, dc, i * 128:(i + 1) * 128],
                                    wdt_sb[:, l, dc * 16:(dc + 1) * 16], start=dc == 0, stop=dc == 7),
                         reads=xnr + ["cstb"], writes=[psn(b)])
            P.op("dve", f_tt(dt_tok[:], psf(b)[:, 0:64].rearrange("p (i h) -> p i h", i=4),
                             prow[:, 32 * l:32 * l + 16].unsqueeze(1).broadcast_to([128, 4, 16]), ALU.add),
                 reads=[psn(b), "cst"], writes=["dt_tok"])
            P.op("act", f_act(dt_tok[:], dt_tok[:], AF.Exp), reads=["dt_tok"], writes=["dt_tok"])
            P.op("act", f_act(dt_tok[:], dt_tok[:], AF.Ln, bias=onec, scale=1.0), reads=["dt_tok", "cst"], writes=["dt_tok"])
            P.op("dve", f_tt(a_tok[:], dt_tok[:], A_bc[:, l, :].unsqueeze(1).broadcast_to([128, 4, 16]), ALU.mult),
                 reads=["dt_tok", "A_bc"], writes=["a_tok"])

            def gen_conv():
                dctr = 0
                for i in range(4):
                    bc_ = ps_get(pin=True)
                    for t0 in range(0, 31, 4):
                        nt_ = min(4, 31 - t0)
                        hs = dctr % 2
                        dctr += 1
                        P.op("dve", f_tt(diag[:, 4 * hs:4 * hs + nt_, :], identB.unsqueeze(1).broadcast_to([128, nt_, 128]),
                                         pc(PC_CFW + 31 * i + t0, nt_).unsqueeze(2).broadcast_to([128, nt_, 128]), ALU.mult),
                             reads=["cstb", "cst"], writes=["diagh%d" % hs] + guard)
                        for t in range(t0, t0 + nt_):
                            P.op("pe", f_mm(psf(bc_), diag[:, 4 * hs + t - t0, :], ve_bf[:, i, t:t + 512], start=t == 0, stop=t == 30),
                                 reads=["diagh%d" % hs, "vebf%d" % i], writes=[psn(bc_)])
                        yield
                    P.op("act", f_act(cbf[:, i, :], psf(bc_), AF.Identity, bias=pc(PC_CFB + i)), reads=[psn(bc_), "cst"], writes=["cbf%d" % i] + guard)
                    pinned.discard(bc_)
                    yield

            for _ in gen_conv():
                pass
            pre_k = []

            def gen_fill():
                for (k_, g_, kk_) in deferred_up:
                    bchunk(k_, g_, kk_)
                    yield
                for j_ in range(3):
                    pre_k.append(wload(l, 8 + j_))
                bm, bq_ = ps_get(), ps_get()
                for i in range(4):
                    s2 = i % 2
                    P.op("act", f_act(csq[:, s2, :], cbf[:, i, :], AF.Square), reads=["cbf%d" % i], writes=["csq%d" % s2] + guard)
                    P.op("pe", f_mm(psf(bm), onesB, cbf[:, i, :], start=i == 0, stop=i == 3), reads=["cbf%d" % i, "cstb"], writes=[psn(bm)])
                    P.op("pe", f_mm(psf(bq_), onesB, csq[:, s2, :], start=i == 0, stop=i == 3), reads=["csq%d" % s2, "cstb"], writes=[psn(bq_)])
                mean, m2 = FA, FB
                P.op("dve", f_ts(mean[:, 0:512], psf(bm), 1.0 / 512, ALU.mult), reads=[psn(bm)], writes=["FA"])
                P.op("dve", f_tt(m2[:, 0:512], mean[:, 0:512], mean[:, 0:512], ALU.mult), reads=["FA"], writes=["FB"])
                P.op("dve", f_stt(m2[:, 0:512], psf(bq_), 1.0 / 512, m2[:, 0:512], ALU.mult, ALU.subtract), reads=[psn(bq_), "FB"], writes=["FB"])
                P.op("act", f_act(m2[:, 0:512], m2[:, 0:512], AF.Ln, bias=epsc, scale=1.0), reads=["FB", "cst"], writes=["FB"])
                P.op("act", f_act(m2[:, 0:512], m2[:, 0:512], AF.Exp, scale=-0.5), reads=["FB"], writes=["FB"])
                yield
                for i in range(4):
                    s2 = i % 2
                    P.op("pool", f_tt(LT, cbf[:, i, :], mean[:, 0:512], ALU.subtract), reads=["cbf%d" % i, "FA"], writes=["LT"] + guard)
                    P.op("pool", f_tt(cbf[:, i, :], LT, m2[:, 0:512], ALU.mult), reads=["LT", "FB"], writes=["cbf%d" % i])
                    yield
                for g in range(4):
                    w = 2 ** (g + 1)
                    uev = ue[:, g, 0:527]
                    cur, curn, off = uev, "ue%d" % g, 0
                    bufs = [(FA, "FA"), (FB, "FB")]
                    for si, d in enumerate([1, 2, 4, 8][:g + 1]):
                        nb, nbn = bufs[si % 2]
                        P.op("pool", f_tt(nb[:, off + d:527], cur[:, off + d:527], cur[:, off:527 - d], ALU.add), reads=[curn], writes=[nbn])
                        cur, curn = nb[:, 0:527], nbn
                        off += d
                    s2 = g % 2
                    pl, pln = csq[:, s2, :], "csq%d" % s2
                    P.op("dve", f_stt(pl, cur[:, 15:527], 1.0 / w, uev[:, 15:527], ALU.mult, ALU.subtract), reads=[curn, "ue%d" % g], writes=[pln])
                    if first_block:
                        tmp = sm[:, 10, :]
                        P.op("dve", f_tt(tmp, cur[:, 15:31], cstf[:, CF_RC + 16 * g:CF_RC + 16 * g + 16], ALU.mult), reads=[curn, "cst"], writes=["tmp16"])
                        P.op("dve", f_tt(pl[:, 0:16], tmp, uev[:, 15:31], ALU.subtract), reads=["tmp16", "ue%d" % g, pln], writes=[pln])
                    b = ps_get()
                    P.op("pe", f_mm(psf(b), wmix_sb[:, l, g * 128:(g + 1) * 128], pl), reads=[pln, "cstb"], writes=[psn(b)])
                    P.op("dve", f_stt(gcs[:, g, :], psf(b), pc(PC_PSC + g), gcs[:, g, :], ALU.mult, ALU.mult),
                         reads=[psn(b), "gcs%d" % g, "cst"], writes=["gcs%d" % g])
                    yield

            decay = DMb[:, 0:2048].rearrange("p (h t) -> p h t", h=16)
            Mt = DMb[:, 2048:4096].rearrange("p (h t) -> p h t", h=16)
            byb = {}

            def bufs_p(c):
                p = c % 2
                return ((HA, HA1)[p], ("HA", "HA1")[p], (HB, HB1)[p], ("HB", "HB1")[p],
                        HC[:, p * 256:(p + 1) * 256], "HC%d" % p, p)

            fctx = {}
            ev = set()

            def front1(c):
                tk = slice(c * 128, (c + 1) * 128)
                xdt, xdtn, xdtd, xdtdn, Btok, Btn, p = bufs_p(c)
                gd = guard if p == 1 else []
                b1 = ps_get()
                for j in range(8):
                    P.op("pe", f_tr(psb(b1)[:, j * 128:(j + 1) * 128], xbc[:, j, tk], identB), reads=["xbc%d" % j, "cstb"], writes=[psn(b1)])
                b2 = ps_get()
                for g in range(2):
                    P.op("pe", f_tr(psb(b2)[:, g * 128:(g + 1) * 128], xbc[:, 8 + g, tk], identB), reads=["xbc%d" % (8 + g), "cstb"], writes=[psn(b2)])
                b3 = ps_get(pin=True)
                P.op("pe", f_mm(psf(b3)[:, 0:16], TRI, a_tok[:, c, :]), reads=["a_tok", "cst"], writes=[psn(b3)])
                P.op("pe", f_mm(psf(b3)[:, 16:32], onesF, a_tok[:, c, :]), reads=["a_tok", "cst"], writes=[psn(b3)])
                P.op("pe", f_mm(psf(b3)[0:16, 128:256], a_tok[:, c, :], TRI), reads=["a_tok", "cst"], writes=[psn(b3)])
                P.op("dve", f_tt(xdt[:].rearrange("p (h q) -> p h q", h=16), psb(b1).rearrange("p (h q) -> p h q", h=16),
                                 dt_tok[:, c, :].unsqueeze(2).broadcast_to([128, 16, 64]), ALU.mult),
                     reads=[psn(b1), "dt_tok"], writes=[xdtn] + gd)
                P.op("act", f_act(Btok, psb(b2)[:, 0:256], AF.Copy), reads=[psn(b2)], writes=[Btn, "HCx"])
                yield
                cum, ncum, ecum, dd, dte = (sm[:, 5 * p + q_, :] for q_ in range(5))
                sfx = str(p)
                P.op("dve", f_cp(cum, psf(b3)[:, 0:16]), reads=[psn(b3)], writes=["cum" + sfx])
                P.op("dve", f_ts(ncum, psf(b3)[:, 0:16], -1.0, ALU.mult), reads=[psn(b3)], writes=["ncum" + sfx])
                P.op("act", f_act(ecum, cum, AF.Exp), reads=["cum" + sfx], writes=["ecum" + sfx])
                P.op("dve", f_tt(dd, psf(b3)[:, 16:32], cum, ALU.subtract), reads=[psn(b3), "cum" + sfx], writes=["dd" + sfx])
                P.op("act", f_act(dte, dd, AF.Exp), reads=["dd" + sfx], writes=["dte" + sfx])
                yield
                while c > 0 and ("seg%d" % (c - 1)) not in ev:
                    yield
                P.op("pool", f_tt(xdtd[:].rearrange("p (h q) -> p h q", h=16), xdt[:].rearrange("p (h q) -> p h q", h=16),
                                  dte.unsqueeze(2).broadcast_to([128, 16, 64]), ALU.mult),
                     reads=[xdtn, "dte" + sfx], writes=[xdtdn] + gd)
                P.op("dve", f_tt(XC[0:16, :].rearrange("p (h t) -> p h t", h=16),
                                 psf(b3)[0:16, 128:256].unsqueeze(1).broadcast_to([16, 16, 128]),
                                 identF[0:16, 0:16].unsqueeze(2).broadcast_to([16, 16, 128]), ALU.mult),
                     reads=[psn(b3), "cst"], writes=["XC"])
                cfc = sm[0:16, 11 + p, 0:1]
                r2 = sm[0:16, 13 + p, 0:8]
                P.op("dve", f_cp(cfc, psf(b3)[0:16, 255:256]), reads=[psn(b3)], writes=["cfc" + sfx])
                P.op("dve", f_ts(r2, PAIR, cfc, ALU.mult), reads=["cfc" + sfx, "cst"], writes=["r2" + sfx])
                pinned.discard(b3)
                b4 = ps_get(pin=True)
                for g in range(2):
                    P.op("pe", f_mm(psf(b4)[:, g * 128:(g + 1) * 128], xbc[:, 8 + g, tk], xbc[:, 10 + g, tk]),
                         reads=["xbc%d" % (8 + g), "xbc%d" % (10 + g)], writes=[psn(b4)])
                b5 = ps_get()
                P.op("pe", f_mm(psf(b5)[:, 0:8], L2c, r2), reads=["r2" + sfx, "cst"], writes=[psn(b5)])
                P.op("act", f_act(ecend[:, :, p], psf(b5)[:, 0:8], AF.Exp), reads=[psn(b5)], writes=["ecend" + sfx])
                fctx[c] = b4
                yield

            def front2(c):
                xdt, xdtn, xdtd, xdtdn, Btok, Btn, p = bufs_p(c)
                sfx = str(p)
                ncum = sm[:, 5 * p + 1, :]
                b4 = fctx.pop(c)
                for q in range(4):
                    bq = ps_get()
                    P.op("pe", f_mm(psf(bq), onesF[0:16, :], XC[0:16, q * 512:(q + 1) * 512], start=True, stop=False), reads=["XC", "cst"], writes=[psn(bq)])
                    P.op("pe", f_mm(psf(bq), identB, MNEG, start=False, stop=True), reads=["cstb"], writes=[psn(bq)])
                    for hh in range(4):
                        h = 4 * q + hh
                        P.op("act", f_act(decay[:, h, :], psf(bq)[:, hh * 128:(hh + 1) * 128], AF.Exp, bias=ncum[:, h:h + 1]),
                             reads=[psn(bq), "ncum" + sfx], writes=["DMlo"])
                    yield
                ev.add("seg%d" % c)
                for g in range(2):
                    P.op("dve", f_tt(Mt[:, 8 * g:8 * g + 8, :], decay[:, 8 * g:8 * g + 8, :],
                                     psf(b4)[:, g * 128:(g + 1) * 128].unsqueeze(1).broadcast_to([128, 8, 128]), ALU.mult),
                         reads=["DMlo", psn(b4)], writes=["DMhi"])
                pinned.discard(b4)
                yield
                by = [ps_get(), ps_get()]
                for h in range(16):
                    P.op("pe", f_mm(psf(by[h // 8])[:, (h % 8) * 64:(h % 8 + 1) * 64], Mt[:, h, :], xdt[:, h * 64:(h + 1) * 64]),
                         reads=["DMhi", xdtn], writes=[psn(by[h // 8])])
                YT, YTn = (YT0, YT1)[p], "YT%d" % p
                for g in range(2):
                    P.op("act", f_act(YT[:, g * 512:(g + 1) * 512], psf(by[g]), AF.Copy), reads=[psn(by[g])], writes=[YTn])
                yield

            def back_a(c):
                tk = slice(c * 128, (c + 1) * 128)
                xdt, xdtn, xdtd, xdtdn, Btok, Btn, p = bufs_p(c)
                sfx = str(p)
                ecum = sm[:, 5 * p + 2, :]
                YT, YTn = (YT0, YT1)[p], "YT%d" % p
                st, stn = stP[:, l, :], "stP%d" % l
                P.op("act", f_act(HD[:], st, AF.Copy), reads=[stn], writes=["HD"])
                b6 = ps_get()
                for j in range(8):
                    P.op("pe", f_tr(psb(b6)[:, j * 128:(j + 1) * 128], HD[:, j * 128:(j + 1) * 128], identB), reads=["HD", "cstb"], writes=[psn(b6)])
                P.op("act", f_act(hT[:, 0, :], psb(b6), AF.Copy), reads=[psn(b6)], writes=["hT0"])
                yield
                bo = [ps_get(), ps_get()]
                for g in range(2):
                    P.op("pe", f_mm(psf(bo[g]), xbc[:, 10 + g, tk], hT[:, 0, g * 512:(g + 1) * 512]),
                         reads=["xbc%d" % (10 + g), "hT0"], writes=[psn(bo[g])])
                for g in range(2):
                    P.op("dve", f_tt(FD[:, g * 512:(g + 1) * 512].rearrange("p (h q) -> p h q", h=8),
                                     psf(bo[g]).rearrange("p (h q) -> p h q", h=8),
                                     ecum[:, 8 * g:8 * g + 8].unsqueeze(2).broadcast_to([128, 8, 64]), ALU.mult),
                         reads=[psn(bo[g]), "ecum" + sfx], writes=["FD"])
                    P.op("dve", f_tt(YT[:, g * 512:(g + 1) * 512], FD[:, g * 512:(g + 1) * 512], YT[:, g * 512:(g + 1) * 512], ALU.add),
                         reads=["FD", YTn], writes=[YTn])
                yield
                bs = [ps_get(), ps_get()]
                for j in range(8):
                    P.op("pe", f_mm(psf(bs[j // 4])[:, (j % 4) * 128:(j % 4 + 1) * 128], xdtd[:, j * 128:(j + 1) * 128],
                                    Btok[:, (j // 4) * 128:(j // 4 + 1) * 128]), reads=[xdtdn, Btn], writes=[psn(bs[j // 4])])
                P.op("pool", f_tt(st.rearrange("p (j n) -> p j n", j=8), st.rearrange("p (j n) -> p j n", j=8),
                                  ecend[:, :, p:p + 1].broadcast_to([128, 8, 128]), ALU.mult),
                     reads=[stn, "ecend" + sfx], writes=[stn])
                for hf_ in range(2):
                    P.op("dve", f_tt(st[:, hf_ * 512:(hf_ + 1) * 512], st[:, hf_ * 512:(hf_ + 1) * 512], psf(bs[hf_]), ALU.add),
                         reads=[stn, psn(bs[hf_])], writes=[stn])
                yield

            def back_b(c):
                tk = slice(c * 128, (c + 1) * 128)
                p = c % 2
                YT, YTn = (YT0, YT1)[p], "YT%d" % p
                b7 = ps_get()
                for j in range(8):
                    P.op("pe", f_tr(psb(b7)[:, j * 128:(j + 1) * 128], YT[:, j * 128:(j + 1) * 128], identB), reads=[YTn, "cstb"], writes=[psn(b7)])
                P.op("pool", f_tt(FC[:].rearrange("p (j t) -> p j t", j=8), xbc[:, 0:8, tk],
                                  pc(PC_D, 8).unsqueeze(2).broadcast_to([128, 8, 128]), ALU.mult),
                     reads=["xbc%d" % j for j in range(8)] + ["cst"], writes=["FC"])
                P.op("dve", f_tt(FC[:], FC[:], psb(b7), ALU.add), reads=["FC", psn(b7)], writes=["FC"])
                zyr = ["zy%d" % j for j in range(8)]
                P.op("dve", f_tt(FDb.rearrange("p (j t) -> p j t", j=8), FC[:].rearrange("p (j t) -> p j t", j=8), zy[:, :, tk], ALU.mult),
                     reads=["FC"] + zyr, writes=["FDb"])
                yield
                P.op("act", f_act(YT, FDb, AF.Square), reads=["FDb"], writes=[YTn])
                b8 = ps_get()
                for g in range(2):
                    for jj in range(4):
                        P.op("pe", f_mm(psf(b8)[:, g * 128:(g + 1) * 128], onesB, YT[:, (4 * g + jj) * 128:(4 * g + jj + 1) * 128],
                                        start=jj == 0, stop=jj == 3), reads=[YTn, "cstb"], writes=[psn(b8)])
                P.op("act", f_act(rstdg[:], psf(b8)[:, 0:256], AF.Ln, bias=epsc, scale=1.0 / 512), reads=[psn(b8), "cst"], writes=["rstdg"])
                P.op("act", f_act(rstdg[:], rstdg[:], AF.Exp, scale=-0.5), reads=["rstdg"], writes=["rstdg"])
                for j in range(8):
                    P.op("dve", f_stt(zy[:, j, tk], FDb[:, j * 128:(j + 1) * 128], pc(PC_SNW + j),
                                      rstdg[:, (j // 4) * 128:(j // 4 + 1) * 128], ALU.mult, ALU.mult),
                         reads=["FDb", "rstdg", "cst"], writes=["zy%d" % j])
                yield

            def g_f1():
                for c in range(4):
                    while c >= 2 and not (("f2_%d" % (c - 2)) in ev and ("ba%d" % (c - 2)) in ev):
                        yield
                    yield from front1(c)
                    ev.add("f1_%d" % c)

            def g_f2():
                for c in range(4):
                    while ("f1_%d" % c) not in ev or (c >= 2 and ("bb%d" % (c - 2)) not in ev):
                        yield
                    yield from front2(c)
                    ev.add("f2_%d" % c)

            def g_ba():
                for c in range(4):
                    while ("f2_%d" % c) not in ev:
                        yield
                    yield from back_a(c)
                    ev.add("ba%d" % c)

            def g_bb():
                for c in range(4):
                    while ("ba%d" % c) not in ev:
                        yield
                    yield from back_b(c)
                    ev.add("bb%d" % c)

            g1_, g2_, ga_, gb_, gl_ = g_f1(), g_f2(), g_ba(), g_bb(), gen_fill()
            live = [g1_, g2_, ga_, gb_, gl_]
            order = [ga_, g2_, g1_, gb_, ga_, g2_, g1_, gl_]
            while live:
                for g_ in order:
                    if g_ in live:
                        try:
                            next(g_)
                        except StopIteration:
                            live.remove(g_)

            def rows_out(src_fn, nchunk, nrows, dst, stg, stgn):
                for i in range(nchunk):
                    b = ps_get()
                    sap, srn = src_fn(i)
                    P.op("pe", f_tr(psf(b)[0:nrows, 0:128], sap, identF), reads=[srn, "cst"], writes=[psn(b)])
                    P.op("act", f_act(stg[0:nrows, i * 128:(i + 1) * 128], psf(b)[0:nrows, 0:128], AF.Copy), reads=[psn(b)], writes=[stgn])
                P.dma("sp", stgn, f_dma(dst, stg[0:nrows, 0:nchunk * 128]), reads=[stgn])

            for i in range(4):
                s2 = i % 2
                P.op("act", f_act(csq[:, s2, :], cbf[:, i, :], AF.Silu, bias=pc(PC_LNB + i), scale=pc(PC_LNW + i)),
                     reads=["cbf%d" % i, "cst"], writes=["csq%d" % s2])
                P.op("dve", f_tt(gbs[:, i, :], csq[:, s2, :], gbs[:, i, :], ALU.mult), reads=["csq%d" % s2, "gbs%d" % i], writes=["gbs%d" % i])

            zyr = ["zy%d" % j for j in range(8)]
            gbr = ["gbs%d" % j for j in range(4)]
            gcr = ["gcs%d" % j for j in range(4)]
            merged = DMb[:].rearrange("p (j t) -> p j t", j=8)
            for j in range(8):
                k = pre_k[j] if j < 3 else wload(l, 8 + j)
                wk = "w%d" % k
                mres = "DMlo" if j < 4 else "DMhi"
                for br, (koff, goff, nk, src, srcr) in enumerate(((0, 2048, 8, zy, zyr), (1024, 3072, 4, gbs, gbr), (1536, 4096, 4, gcs, gcr))):
                    bo_ = ps_get()
                    for kc in range(nk):
                        P.op("pe", f_mm(psf(bo_), wring[:, k, koff + kc * 128:koff + (kc + 1) * 128], src[:, kc, :],
                                        start=kc == 0, stop=kc == nk - 1), reads=[wk] + srcr, writes=[psn(bo_)])
                    bg_ = inproj(k, goff)
                    sgt, sgn = ((FA, "FA"), (FB, "FB"), (FC, "FC"))[br]
                    P.op("act", f_act(sgt[:, 0:512], psf(bg_), AF.Sigmoid), reads=[psn(bg_)], writes=[sgn])
                    P.op("dve", f_tt(sgt[:, 0:512], sgt[:, 0:512], psf(bo_), ALU.mult), reads=[sgn, psn(bo_)], writes=[sgn])
                P.op("dve", f_tt(FA[:, 0:512], FA[:, 0:512], FB[:, 0:512], ALU.add), reads=["FA", "FB"], writes=["FA"])
                P.op("dve", f_tt(merged[:, j, :], FA[:, 0:512], FC[:, 0:512], ALU.add), reads=["FA", "FC"], writes=[mres])
            for hf in range(2):
                k = wload(l, 16 + hf, 4096)
                for i in range(NT):
                    b = ps_get()
                    for dc in range(8):
                        P.op("pe", f_mm(psf(b), merged[:, dc, i * 128:(i + 1) * 128], wring[:, k, dc * 512:(dc + 1) * 512],
                                        start=dc == 0, stop=dc == 7), reads=["w%d" % k, "DMlo", "DMhi"], writes=[psn(b)])
                    xv = x_tok[:, i, hf * 512:(hf + 1) * 512]
                    P.op("dve", f_tt(xv, xv, psf(b), ALU.add), reads=["xt%d" % i, psn(b)], writes=["xt%d" % i])

            if last_block:
                rows_out(lambda j: (halo_sc[:, l, j * 3:(j + 1) * 3], "halo_sc%d" % l), 12, 3, D["sc_p"][l], XC, "XC")
                rows_out(lambda i: (halo_cf[:, l, i * 30:(i + 1) * 30], "halo_cf%d" % l), 4, 30, D["cf_p"][l], XC, "XC")
                rows_out(lambda i: (halo_pl[:, l, i * 15:(i + 1) * 15], "halo_pl%d" % l), 4, 15, D["pl_p"][l], XC, "XC")
                P.dma("sp", "stP%d" % l, f_dma(D["ssm_p"][l].rearrange("(j p) n -> p j n", p=128),
                                               stP[:, l, :].rearrange("p (j n) -> p j n", j=8)), reads=["stP%d" % l])


        blocks = [("P", 0), ("S", 0), ("P", 1), ("P", 2), ("P", 3)]
        modes = {"P": Mode("P"), "S": Mode("S")}
        ALIAS_NAMES = (UNI_GUARD + ["vebf%d" % i for i in range(4)] + ["cbf%d" % i for i in range(4)]
                       + ["diagh0", "diagh1", "csq0", "csq1", "HA1", "HB1", "LT", "ecend", "ecend0", "ecend1", "HC", "HC0", "HC1",
                          "tmp16", "FA", "FB", "FC", "FD", "HA", "HB", "HD", "HCx", "XC", "DMlo", "DMhi", "hT0", "hT1", "rstdg",
                          "Q0", "Q1", "Q2", "Q3", "h0b0", "h0b1", "cbx0", "cbx1", "cbx2", "cbx3"]
                       + [n + sfx for n in ("cum", "ncum", "ecum", "dd", "dte", "cfc", "r2") for sfx in ("", "0", "1")])
        for (mn, k) in blocks:
            md = modes[mn]
            P.op("dve", f_ms(ss[:, 15:16], 0.0), writes=ALIAS_NAMES)
            for i in range(md.NT):
                src = D["xs"] if mn == "S" else D["xp"][k * 512 + i * 128:k * 512 + (i + 1) * 128, :]
                P.dma("act", "xt%d" % i, f_dma(x_tok[:, i, :], src), writes=["xt%d" % i])
            for l in range(DEPTH):
                if mn == "P":
                    layer_P(l, k, first_block=(k == 0), last_block=(k == 3))
                else:
                    layer(md, l, k, first_block=(k == 0), last_block=(k == 3))
            wstate["first"] = False
            for i in range(md.NT):
                sl = i % 2
                P.op("dve", f_ms(ss[:, i:i + 1], 0.0), writes=["ss%d" % i])
                P.op("act", f_act((HA, HB)[sl][:], x_tok[:, i, :], AF.Square, accum=ss[:, i:i + 1]),
                     reads=["xt%d" % i], writes=[("HA", "HB")[sl], "ss%d" % i])
            for i in range(md.NT):
                P.op("act", f_act(ss[:, 4 + i:5 + i], ss[:, i:i + 1], AF.Ln, bias=epsc, scale=1.0 / DM),
                     reads=["ss%d" % i, "cst"], writes=["ssb%d" % i])
            for i in range(md.NT):
                P.op("act", f_act(ss[:, 8 + i:9 + i], ss[:, 4 + i:5 + i], AF.Exp, scale=-0.5), reads=["ssb%d" % i], writes=["rstd%d" % i])
            for i in range(md.NT):
                P.op("dve", f_stt(x_tok[:, i, :], x_tok[:, i, :], ss[:, 8 + i:9 + i], prow[:, 64:64 + 1024], ALU.mult, ALU.mult),
                     reads=["xt%d" % i, "rstd%d" % i, "cst"], writes=["xt%d" % i])
            for i in range(md.NT):
                dst = D["ys"] if mn == "S" else D["yp"][k * 512 + i * 128:k * 512 + (i + 1) * 128, :]
                P.dma("act", "xt%d" % i, f_dma(dst, x_tok[:, i, :]), reads=["xt%d" % i])
        P.finalize(es)
    return nc


_NC_CACHE = {}


def kernel(**inputs):
    inp = {k: np.asarray(v) for k, v in inputs.items()}
    shared = host_shared(inp)
    in_maps = []
    for c in range(NCORES):
        m = host_layout(inp, c)
        m.update(shared)
        in_maps.append(m)
    if "nc" not in _NC_CACHE:
        _NC_CACHE["nc"] = build_nc()
    nc = _NC_CACHE["nc"]
    res = run_bass_kernel_spmd(nc, in_maps, core_ids=list(range(NCORES)))
    R = res.results
    f = np.float32
    y_prompt = np.stack([R[c]["yp"] for c in range(NCORES)]).astype(f)
    y_sample = np.concatenate([R[c]["ys"].reshape(NSEQ_S, LS, DM) for c in range(NCORES)]).astype(f)
    ssm_p = np.stack([R[c]["ssm_p"].reshape(DEPTH, 16, 64, 128) for c in range(NCORES)], axis=1).astype(f)
    ssm_s = np.concatenate([R[c]["ssm_s"].reshape(DEPTH, NSEQ_S, 16, 64, 128) for c in range(NCORES)], axis=1).astype(f)
    sc_p = np.stack([R[c]["sc_p"] for c in range(NCORES)], axis=1).astype(f)
    sc_s = np.concatenate([R[c]["sc_s"].reshape(DEPTH, NSEQ_S, 3, 1536) for c in range(NCORES)], axis=1).astype(f)
    cf_p = np.stack([R[c]["cf_p"] for c in range(NCORES)], axis=1).astype(f)
    cf_s = np.concatenate([R[c]["cf_s"].reshape(DEPTH, NSEQ_S, 30, 512) for c in range(NCORES)], axis=1).astype(f)
    pl_p = np.stack([R[c]["pl_p"] for c in range(NCORES)], axis=1).astype(f)
    pl_s = np.concatenate([R[c]["pl_s"].reshape(DEPTH, NSEQ_S, 15, 512) for c in range(NCORES)], axis=1).astype(f)
    return (y_prompt, y_sample, ssm_p, ssm_s, sc_p, sc_s, cf_p, cf_s, pl_p, pl_s)
```
